# Optimizing a Trainium2 kernel written in Bass

```python
import jax, jax.numpy as jnp
from jax import lax
import numpy as np

D_MODEL = 1024
BATCH = 8
SEQ = 2048
DEPTH = 1
DEC_BATCH = 128
DEC_SEQ = 8
PAST_LEN = 16384
PAGE_SIZE = 128

N_META = 16
D_MIX = D_MODEL
D_A = D_MIX // 2
H_A = 4
DH_A = D_A // H_A
D_B = D_MIX - D_A
H_B = 4
DH_B = D_B // H_B
D_IN = 4 * D_A + 2 * H_A + 4 * D_B
D_FF = 2816
CONV_W = 3
CHUNK = 64
ALPHA = (2.0 * DEPTH) ** 0.25
BETA = (8.0 * DEPTH) ** -0.25
LN_EPS = 1e-5
RMS_EPS = 1e-6

kernel_name = 'hymba_mlstm_hgrn2_convffn_step'


def layer_norm(x, g, b):
    xf = x.astype(jnp.float32)
    mu = xf.mean(-1, keepdims=True)
    var = jnp.square(xf - mu).mean(-1, keepdims=True)
    return ((xf - mu) * lax.rsqrt(var + LN_EPS) * g + b).astype(x.dtype)


def head_rms(h, g):
    return h * lax.rsqrt(jnp.mean(h * h, -1, keepdims=True) + RMS_EPS) * g


def mlstm_chunk(carry, xs):
    C, n, m0 = carry
    q, k, v, ig, lf = xs
    T = q.shape[1]
    b = jnp.cumsum(lf, axis=1).transpose(0, 2, 1)
    igh = ig.transpose(0, 2, 1)
    causal = jnp.tril(jnp.ones((T, T), dtype=bool))
    D = jnp.where(causal, b[..., :, None] - b[..., None, :] + igh[..., None, :], -jnp.inf)
    m_t = jnp.maximum(b + m0[..., None], D.max(-1))
    dec = jnp.exp(b + m0[..., None] - m_t)
    W = jnp.exp(D - m_t[..., None])
    Sw = W * jnp.einsum('bthd,bshd->bhts', q, k)
    num = dec[..., None] * jnp.einsum('bthd,bhde->bhte', q, C) + jnp.einsum('bhts,bshe->bhte', Sw, v)
    den = dec * jnp.einsum('bthd,bhd->bht', q, n) + Sw.sum(-1)
    h = num / jnp.maximum(jnp.abs(den), jnp.exp(-m_t))[..., None]
    wT = W[:, :, -1, :]
    C_new = dec[..., -1, None, None] * C + jnp.einsum('bhs,bshd,bshe->bhde', wT, k, v)
    n_new = dec[..., -1, None] * n + jnp.einsum('bhs,bshd->bhd', wT, k)
    return (C_new, n_new, m_t[..., -1]), h.transpose(0, 2, 1, 3)


def hgrn_chunk(S, xs):
    q, lf, k, iv = xs
    T = q.shape[1]
    a = jnp.cumsum(lf, axis=1)
    causal = jnp.tril(jnp.ones((T, T), dtype=bool))[None, :, :, None, None]
    decay = jnp.exp(jnp.where(causal, a[:, :, None] - a[:, None, :], -jnp.inf))
    scores = jnp.einsum('bthc,btshc,bshc->bhts', q, decay, k)
    o = jnp.einsum('bthc,bhce->bthe', q * jnp.exp(a), S) + jnp.einsum('bhts,bshe->bthe', scores, iv)
    aT = a[:, -1]
    wk = jnp.exp(aT[:, None] - a) * k
    S_new = jnp.exp(aT)[..., None] * S + jnp.einsum('bshc,bshe->bhce', wk, iv)
    return S_new, o


def run_causal(fn, state, xs, lead):
    if lead is None:
        return fn(state, xs)
    state, out_head = fn(state, tuple(a[:, :lead] for a in xs))
    rest = tuple(a[:, lead:] for a in xs)
    Bn, L = rest[0].shape[0], rest[0].shape[1]
    n_c = L // CHUNK
    xs_c = tuple(jnp.moveaxis(a.reshape(Bn, n_c, CHUNK, *a.shape[2:]), 1, 0) for a in rest)
    state, outs = lax.scan(fn, state, xs_c)
    outs = jnp.moveaxis(outs, 0, 1).reshape(Bn, L, *outs.shape[3:])
    return state, jnp.concatenate([out_head, outs], axis=1)


def decoder_layer(x, lb, lead, mstate, hstate, conv_state, w_in, b_in, b_fgate_a, g_norm_a, g_norm_b,
                  w_out, b_out, ln1_g, ln1_b, w_up, b_up, w_conv, b_conv, w_down, b_down, ln2_g, ln2_b):
    Bn, T, _ = x.shape
    proj = (x @ w_in + b_in).astype(jnp.float32)
    splits = [int(s) for s in np.cumsum([D_A, D_A, D_A, D_A, H_A, H_A, D_B, D_B, D_B])]
    qa, ka, va, oa, ia, fa, qb, fb, ib, gb = jnp.split(proj, splits, axis=-1)
    hd = lambda t, H: t.reshape(Bn, T, H, -1)
    qa = hd(qa, H_A)
    ka = hd(ka, H_A) * (DH_A ** -0.5)
    va = hd(va, H_A)
    lfa = jax.nn.log_sigmoid(fa + b_fgate_a.astype(jnp.float32))
    mstate, h_a = run_causal(mlstm_chunk, mstate, (qa, ka, va, ia, lfa), lead)
    h_a = jax.nn.sigmoid(hd(oa, H_A)) * head_rms(h_a, g_norm_a.astype(jnp.float32))
    lbh = lb.reshape(H_B, DH_B)
    fb = hd(fb, H_B)
    lfb = jnp.log(lbh + (1.0 - lbh) * jax.nn.sigmoid(fb))
    kb = (1.0 - lbh) * jax.nn.sigmoid(-fb)
    qb = jax.nn.silu(hd(qb, H_B))
    ib = hd(ib, H_B)
    hstate, o_b = run_causal(hgrn_chunk, hstate, (qb, lfb, kb, ib), lead)
    h_b = jax.nn.sigmoid(hd(gb, H_B)) * head_rms(o_b, g_norm_b.astype(jnp.float32))
    mix = jnp.concatenate([h_a.reshape(Bn, T, D_A), h_b.reshape(Bn, T, D_B)], -1).astype(x.dtype)
    x = layer_norm(ALPHA * x + (mix @ w_out + b_out), ln1_g, ln1_b)
    up = x @ w_up + b_up
    u, gate = jnp.split(up, 2, axis=-1)
    full = jnp.concatenate([conv_state.astype(u.dtype), u], axis=1)
    conv = b_conv + full[:, 0:T] * w_conv[0]
    for j in range(1, CONV_W):
        conv = conv + full[:, j:j + T] * w_conv[j]
    new_conv = full[:, T:]
    ffn = (jax.nn.silu(conv) * gate) @ w_down + b_down
    x = layer_norm(ALPHA * x + ffn, ln2_g, ln2_b)
    return x, mstate, hstate, new_conv


def setup_inputs(seed: int = 0) -> dict:
    key = jax.random.key(seed)
    ks = jax.random.split(key, 32)
    nrm = lambda k, s, sc: jax.random.normal(k, s, jnp.float32) * sc
    f32 = jnp.float32
    return {
        'x_prompt': nrm(ks[0], (BATCH, SEQ, D_MODEL), 1.0),
        'x_sample': nrm(ks[1], (DEC_BATCH, DEC_SEQ, D_MODEL), 1.0),
        'state_mlstm_C': nrm(ks[2], (DEPTH, DEC_BATCH, H_A, DH_A, DH_A), 0.1),
        'state_mlstm_n': nrm(ks[3], (DEPTH, DEC_BATCH, H_A, DH_A), 0.5),
        'state_mlstm_m': nrm(ks[4], (DEPTH, DEC_BATCH, H_A), 1.0),
        'state_hgrn_S': nrm(ks[5], (DEPTH, DEC_BATCH, H_B, DH_B, DH_B), 0.1),
        'state_ffn_conv': nrm(ks[6], (DEPTH, DEC_BATCH, CONV_W - 1, D_FF), 1.0),
        'meta_tokens': nrm(ks[7], (N_META, D_MODEL), 1.0),
        'ln_emb_g': 1.0 + nrm(ks[8], (D_MODEL,), 0.02),
        'ln_emb_b': nrm(ks[9], (D_MODEL,), 0.02),
        'w_in': nrm(ks[10], (DEPTH, D_MODEL, D_IN), D_MODEL ** -0.5),
        'b_in': nrm(ks[11], (DEPTH, D_IN), 0.02),
        'b_fgate_a': jnp.broadcast_to(jnp.linspace(3.0, 6.0, H_A, dtype=f32), (DEPTH, H_A)) + nrm(ks[12], (DEPTH, H_A), 0.1),
        'g_norm_a': 1.0 + nrm(ks[13], (DEPTH, H_A, DH_A), 0.02),
        'g_norm_b': 1.0 + nrm(ks[14], (DEPTH, H_B, DH_B), 0.02),
        'hgrn_lb_logits': nrm(ks[15], (DEPTH + 1, D_B), 0.1),
        'w_out': nrm(ks[16], (DEPTH, D_MIX, D_MODEL), BETA * D_MIX ** -0.5),
        'b_out': nrm(ks[17], (DEPTH, D_MODEL), 0.02),
        'ln1_g': 1.0 + nrm(ks[18], (DEPTH, D_MODEL), 0.02),
        'ln1_b': nrm(ks[19], (DEPTH, D_MODEL), 0.02),
        'w_up': nrm(ks[20], (DEPTH, D_MODEL, 2 * D_FF), D_MODEL ** -0.5),
        'b_up': nrm(ks[21], (DEPTH, 2 * D_FF), 0.02),
        'w_conv': nrm(ks[22], (DEPTH, CONV_W, D_FF), CONV_W ** -0.5),
        'b_conv': nrm(ks[23], (DEPTH, D_FF), 0.02),
        'w_down': nrm(ks[24], (DEPTH, D_FF, D_MODEL), BETA * D_FF ** -0.5),
        'b_down': nrm(ks[25], (DEPTH, D_MODEL), 0.02),
        'ln2_g': 1.0 + nrm(ks[26], (DEPTH, D_MODEL), 0.02),
        'ln2_b': nrm(ks[27], (DEPTH, D_MODEL), 0.02),
    }


def reference(x_prompt, x_sample, state_mlstm_C, state_mlstm_n, state_mlstm_m, state_hgrn_S, state_ffn_conv,
              meta_tokens, ln_emb_g, ln_emb_b, w_in, b_in, b_fgate_a, g_norm_a, g_norm_b, hgrn_lb_logits,
              w_out, b_out, ln1_g, ln1_b, w_up, b_up, w_conv, b_conv, w_down, b_down, ln2_g, ln2_b):
    f32 = jnp.float32
    Bp = x_prompt.shape[0]
    lbs = jnp.cumsum(jax.nn.softmax(hgrn_lb_logits.astype(f32), axis=0), axis=0)
    meta = jnp.broadcast_to(meta_tokens[None].astype(x_prompt.dtype), (Bp, N_META, D_MODEL))
    xp = layer_norm(jnp.concatenate([meta, x_prompt], axis=1), ln_emb_g, ln_emb_b)
    xs = layer_norm(x_sample, ln_emb_g, ln_emb_b)
    Cp, np_, mp, Sp, cvp = [], [], [], [], []
    Cs, ns, ms, Ss, cvs = [], [], [], [], []
    for l in range(DEPTH):
        prm = (w_in[l], b_in[l], b_fgate_a[l], g_norm_a[l], g_norm_b[l], w_out[l], b_out[l], ln1_g[l], ln1_b[l],
               w_up[l], b_up[l], w_conv[l], b_conv[l], w_down[l], b_down[l], ln2_g[l], ln2_b[l])
        m0 = (jnp.zeros((Bp, H_A, DH_A, DH_A), f32), jnp.zeros((Bp, H_A, DH_A), f32), jnp.zeros((Bp, H_A), f32))
        h0 = jnp.zeros((Bp, H_B, DH_B, DH_B), f32)
        c0 = jnp.zeros((Bp, CONV_W - 1, D_FF), xp.dtype)
        xp, (c_p, n_p, m_p), s_p, cv_p = decoder_layer(xp, lbs[l], N_META, m0, h0, c0, *prm)
        ms0 = (state_mlstm_C[l].astype(f32), state_mlstm_n[l].astype(f32), state_mlstm_m[l].astype(f32))
        xs, (c_s, n_s, m_s), s_s, cv_s = decoder_layer(xs, lbs[l], None, ms0, state_hgrn_S[l].astype(f32),
                                                       state_ffn_conv[l], *prm)
        Cp.append(c_p); np_.append(n_p); mp.append(m_p); Sp.append(s_p); cvp.append(cv_p)
        Cs.append(c_s); ns.append(n_s); ms.append(m_s); Ss.append(s_s); cvs.append(cv_s)
    y_prompt = xp[:, N_META:]
    y_sample = xs
    return (y_prompt, y_sample,
            jnp.stack(Cp), jnp.stack(np_), jnp.stack(mp), jnp.stack(Sp), jnp.stack(cvp),
            jnp.stack(Cs), jnp.stack(ns), jnp.stack(ms), jnp.stack(Ss), jnp.stack(cvs))
```

```python
import os
import threading
from contextlib import ExitStack

import numpy as np
import concourse.bass as bass
import concourse.mybir as mybir
from concourse.bass_utils import run_bass_kernel_spmd

F32 = mybir.dt.float32
BF16 = mybir.dt.bfloat16
AF = mybir.ActivationFunctionType
ALU = mybir.AluOpType
AX = mybir.AxisListType

D = 1024
DIN = 4104
DFF = 2816
NFC = DFF // 128
NCOL = 2192
ALPHA = 2.0 ** 0.25
LN_EPS = 1e-5
RMS_EPS = 1e-6
NEG = -1.0e30
NT = 18
STAGE = int(os.environ.get("KSTAGE", "99"))
LEAD = (0, 0)
SQ = (1, 2)
QUANTA = (6, 4)


class _St:
    __slots__ = ("w", "r", "dsem", "dcnt")

    def __init__(self):
        self.w = None
        self.r = {}
        self.dsem = {}
        self.dcnt = {}


class Tk:
    def __init__(self, h, name, st=None):
        self.h = h
        self.name = name
        self._s = st or _St()

    def view(self, fn):
        return Tk(fn(self.h), self.name + "_v", self._s)

    def __getitem__(self, k):
        return self.h[k]

    w = property(lambda self: self._s.w, lambda self, v: setattr(self._s, "w", v))
    r = property(lambda self: self._s.r, lambda self, v: setattr(self._s, "r", v))


class Sched:
    def __init__(self, nc, st):
        self.nc = nc
        self.st = st
        self.eng = {"pe": nc.tensor, "act": nc.scalar, "dve": nc.vector, "pool": nc.gpsimd, "sp": nc.sync}
        self.sem = {k: st.enter_context(nc.semaphore("s_" + k)) for k in self.eng}
        self.cnt = {k: 0 for k in self.eng}
        self.seen = {k: {} for k in self.eng}
        self.out_events = {}
        self.all_dma = {}
        self.nd = 0
        self.cur = None
        self.atomic = 0

    def sb(self, name, shape, dt=F32, st=None):
        h = (st or self.st).enter_context(self.nc.sbuf_tensor(name, list(shape), dt))
        return Tk(h, name)

    def ps(self, name, shape, dt=F32, st=None):
        h = (st or self.st).enter_context(self.nc.psum_tensor(name, list(shape), dt))
        return Tk(h, name)

    def dram(self, name, shape, dt=F32):
        h = self.nc.dram_tensor(name, list(shape), dt, kind="Internal")
        return Tk(h.ap(), name)

    def _wait(self, e, deps):
        for key, sem, val in deps:
            if self.seen[e].get(key, 0) >= val:
                continue
            self.eng[e].wait_ge(sem, val)
            self.seen[e][key] = val

    def _deps(self, e, reads, writes):
        deps = []
        for t in reads:
            if t.w is not None:
                deps.append(t.w)
        for t in writes:
            if t.w is not None and not (e == "pe" and t.w[0] == "pe"):
                deps.append(t.w)
            for k, (sem, val) in t.r.items():
                deps.append((k, sem, val))
        return deps

    def op(self, e, fn, reads=(), writes=()):
        self._wait(e, self._deps(e, reads, writes))
        ins = fn(self.eng[e])
        self.cnt[e] += 1
        ins.then_inc(self.sem[e], 1)
        ev = (e, self.sem[e], self.cnt[e])
        for t in writes:
            t.w = ev
            t.r = {}
        for t in reads:
            t.r[e] = (self.sem[e], self.cnt[e])
        self._sw()

    def dma(self, q, out, in_, reads=(), writes=(), sem_tile=None, final=False, **kw):
        self._wait(q, self._deps(q, reads, writes))
        kind = "sw" if q == "pool" else "hw"
        stt = sem_tile._s
        if kind not in stt.dsem:
            self.nd += 1
            stt.dsem[kind] = self.st.enter_context(self.nc.semaphore("d%d" % self.nd))
            stt.dcnt[kind] = 0
        ins = self.eng[q].dma_start(out=out, in_=in_, **kw)
        stt.dcnt[kind] += 1
        ins.then_inc(stt.dsem[kind], 16)
        key = ("d", id(stt), kind)
        ev = (key, stt.dsem[kind], 16 * stt.dcnt[kind])
        for t in writes:
            t.w = ev
            t.r = {}
        for t in reads:
            t.r[key] = (ev[1], ev[2])
        self.all_dma[key] = ev
        if final:
            self.out_events[key] = ev
        self._sw()

    def _sw(self):
        st = self.cur
        if st is not None:
            st.n += 1
            if self.atomic == 0 and st.n >= st.quantum:
                st.n = 0
                st.back.release()
                st.go.acquire()

    def run_streams(self, fns, quanta=None):
        class _Stream:
            pass
        streams = []
        for i, fn in enumerate(fns):
            st = _Stream()
            st.n = -(LEAD[i] if (quanta and i < len(LEAD)) else 0)
            st.quantum = quanta[i] if quanta else 1
            st.go = threading.Semaphore(0); st.back = threading.Semaphore(0); st.done = False; st.exc = None

            def body(st=st, fn=fn):
                st.go.acquire()
                try:
                    fn()
                except BaseException as e:
                    st.exc = e
                st.done = True
                st.back.release()
            st.th = threading.Thread(target=body)
            st.th.start()
            streams.append(st)
        live = list(streams)
        while live:
            for st in list(live):
                self.cur = st
                st.go.release()
                st.back.acquire()
                self.cur = None
                if st.done:
                    st.th.join()
                    live.remove(st)
                    if st.exc is not None:
                        for o in live:
                            o.done = True
                        raise st.exc
        self.cur = None

    def barrier(self):
        deps = list(self.all_dma.values())
        for e in ("pe", "act", "dve", "pool"):
            if self.cnt[e] > 0:
                deps.append((e, self.sem[e], self.cnt[e]))
        for e in self.eng:
            self._wait(e, [d for d in deps if d[0] != e])

    def finish(self):
        deps = list(self.out_events.values())
        for e in ("pe", "act", "dve", "pool"):
            if self.cnt[e] > 0:
                deps.append((e, self.sem[e], self.cnt[e]))
        self._wait("sp", deps)


def bc(ap, shape):
    return ap.to_broadcast(list(shape))


def build_program():
    nc = bass.Bass("TRN2", target_bir_lowering=False)

    def di(name, shape, dt=F32):
        return nc.dram_tensor(name, list(shape), dt, kind="ExternalInput").ap()

    def do(name, shape):
        return nc.dram_tensor(name, list(shape), F32, kind="ExternalOutput").ap()

    x_p = di("x_p", [2048, D]); x_s = di("x_s", [128, D]); meta = di("meta", [16, D])
    C_s = di("C_s", [16, 128, 4, 128]); n_s = di("n_s", [16, 128, 4]); m_s = di("m_s", [16, 4])
    S_s = di("S_s", [16, 128, 4, 128]); cv_s = di("cv_s", [32, DFF])
    ln_emb_g = di("ln_emb_g", [D]); ln_emb_b = di("ln_emb_b", [D])
    w_in = di("w_in", [D, DIN]); b_in = di("b_in", [1, DIN]); b_fg = di("b_fg", [4])
    g_a = di("g_a", [512]); g_b = di("g_b", [512]); lb_log = di("lb_log", [2, 512])
    w_out = di("w_out", [D, D]); b_out = di("b_out", [1, D])
    ln1_g = di("ln1_g", [D]); ln1_b = di("ln1_b", [D])
    w_up = di("w_up", [D, 2 * DFF]); b_up = di("b_up", [2 * DFF])
    w_conv = di("w_conv", [3, DFF]); b_conv = di("b_conv", [DFF])
    w_down = di("w_down", [DFF, D]); b_down = di("b_down", [1, D])
    ln2_g = di("ln2_g", [D]); ln2_b = di("ln2_b", [D])
    c_ident = di("c_ident", [128, 128]); c_triP = di("c_triP", [128, 128]); c_triS = di("c_triS", [128, 128])
    c_negBlk = di("c_negBlk", [128, 128]); c_blkones = di("c_blkones", [128, 128])
    c_blkind = di("c_blkind", [128, 16]); c_blkindT = di("c_blkindT", [16, 128]); c_sel0 = di("c_sel0", [128, 16])

    y_p = do("y_p", [2048, D]); y_s = do("y_s", [128, D])
    C_po = do("C_po", [4, 128, 128]); n_po = do("n_po", [4, 128]); m_po = do("m_po", [1, 4])
    S_po = do("S_po", [4, 128, 128]); cv_po = do("cv_po", [2, DFF])
    C_so = do("C_so", [16, 128, 4, 128]); n_so = do("n_so", [16, 128, 4]); m_so = do("m_so", [16, 4])
    S_so = do("S_so", [16, 128, 4, 128]); cv_so = do("cv_so", [32, DFF])

    with ExitStack() as st:
        S = Sched(nc, st)
        x1h_d = S.dram("x1h_d", [NT * 128, D], F32)
        x1T_d = S.dram("x1T_d", [128, 8, NCOL], BF16)

        ident = S.sb("ident", [128, 128]); identB = S.sb("identB", [128, 128], BF16)
        ones = S.sb("ones", [128, 128]); onesB = S.sb("onesB", [128, 128], BF16)
        S.dma("sp", ident[:, :], c_ident, writes=[ident], sem_tile=ident)
        S.op("dve", lambda e: e.tensor_copy(out=identB[:, :], in_=ident[:, :]), [ident], [identB])
        S.op("dve", lambda e: e.memset(ones[:, :], 1.0), [], [ones])
        S.op("dve", lambda e: e.memset(onesB[:, :], 1.0), [], [onesB])

        spa = ExitStack()
        psT = S.ps("psT", [128, 8, 128], BF16, st=spa)
        psP = [S.ps("psP0", [128, 512], st=spa), S.ps("psP1", [128, 512], st=spa)]
        psS = S.ps("psS", [128, 4, 128], st=spa)
        psO = S.ps("psO", [128, 2, 512], st=spa)
        psC = S.ps("psC", [128, 2, 512], st=spa)
        pp_i = [0]

        def pp():
            pp_i[0] ^= 1
            return psP[pp_i[0]]

        def mm(out, lhsT, rhs, start, stop, reads, writes, skip=False):
            kw = {"skip_group_check": True} if skip else {}
            S.op("pe", lambda e: e.matmul(out, lhsT, rhs, start=start, stop=stop, **kw), reads, writes)

        def bias_rows(hl, src, n, f, h32, tb):
            for c0 in range(0, n, 1024):
                w = min(1024, n - c0)
                S.dma("sp", f[0:1, :w], src[:, c0:c0 + w], writes=[f], sem_tile=f)
                S.op("dve", lambda e: e.tensor_copy(out=hl[0:1, c0:c0 + w], in_=f[0:1, :w]), [f], [hl])
                S.op("dve", lambda e: e.tensor_copy(out=h32[0:1, :w], in_=hl[0:1, c0:c0 + w]), [hl], [h32])
                S.op("dve", lambda e: e.tensor_tensor(out=tb[0:1, :w], in0=f[0:1, :w], in1=h32[0:1, :w],
                                                      op=ALU.subtract), [f, h32], [tb])
                S.dma("sp", hl[1:2, c0:c0 + w], tb[0:1, :w], reads=[tb], writes=[hl], sem_tile=hl)

        def layer_norm_rows(x, T, xo, g_bc, b_bc, wk, xo_bf=None):
            stt, mv, rs = wk
            for cch in range(2):
                S.op("dve", lambda e, cch=cch: e.bn_stats(out=stt[:T, cch * 6:cch * 6 + 6], in_=x[:T, cch * 512:(cch + 1) * 512]),
                     [x], [stt])
            S.op("dve", lambda e: e.bn_aggr(out=mv[:T, :], in_=stt[:T, :]), [stt], [mv])
            S.op("act", lambda e: e.activation(out=rs[:T, :], in_=mv[:T, 1:2], func=AF.Ln, bias=eps_ln[:T, :], scale=1.0),
                 [mv, eps_ln], [rs])
            S.op("act", lambda e: e.activation(out=rs[:T, :], in_=rs[:T, :], func=AF.Exp, scale=-0.5), [rs], [rs])
            if g_bc is None:
                if xo_bf is not None:
                    S.op("dve", lambda e: e.tensor_scalar(out=xo_bf[:T, :], in0=x[:T, :], scalar1=mv[:T, 0:1], scalar2=rs[:T, 0:1],
                                                          op0=ALU.subtract, op1=ALU.mult), [x, mv, rs], [xo_bf])
                S.op("dve", lambda e: e.tensor_scalar(out=xo[:T, :], in0=x[:T, :], scalar1=mv[:T, 0:1], scalar2=rs[:T, 0:1],
                                                      op0=ALU.subtract, op1=ALU.mult), [x, mv, rs], [xo])
                return
            S.op("dve", lambda e: e.tensor_scalar(out=xo[:T, :], in0=x[:T, :], scalar1=mv[:T, 0:1], scalar2=rs[:T, 0:1],
                                                  op0=ALU.subtract, op1=ALU.mult), [x, mv, rs], [xo])
            S.op("dve", lambda e: e.tensor_tensor(out=xo[:T, :], in0=xo[:T, :], in1=g_bc[:T, :], op=ALU.mult),
                 [xo, g_bc], [xo])
            if xo_bf is not None:
                S.op("dve", lambda e: e.tensor_tensor(out=xo_bf[:T, :], in0=xo[:T, :], in1=b_bc[:T, :], op=ALU.add),
                     [xo, b_bc], [xo_bf])
                S.op("pool", lambda e: e.tensor_tensor(out=xo[:T, :], in0=xo[:T, :], in1=b_bc[:T, :], op=ALU.add),
                     [xo, b_bc], [xo])
            else:
                S.op("dve", lambda e: e.tensor_tensor(out=xo[:T, :], in0=xo[:T, :], in1=b_bc[:T, :], op=ALU.add),
                     [xo, b_bc], [xo])

        eps_ln = S.sb("eps_ln", [128, 1]); eps_rms = S.sb("eps_rms", [128, 1])
        S.op("dve", lambda e: e.memset(eps_ln[:, :], LN_EPS), [], [eps_ln])
        S.op("dve", lambda e: e.memset(eps_rms[:, :], RMS_EPS), [], [eps_rms])
        lnw = (S.sb("ln_st", [128, 12]), S.sb("ln_mv", [128, 2]), S.sb("ln_rs", [128, 1]))

        with ExitStack() as sa:
            def A(name, shape, dt=F32):
                return S.sb(name, shape, dt, st=sa)

            def sigmoid_act(dst, src_ap, T, reads):
                S.op("act", lambda e: e.activation(out=dst[:T, :], in_=src_ap, func=AF.Exp, scale=-1.0), reads, [dst])
                S.op("act", lambda e: e.activation(out=dst[:T, :], in_=dst[:T, :], func=AF.Ln, bias=ones[:T, 0:1], scale=1.0),
                     [dst, ones], [dst])
                S.op("act", lambda e: e.activation(out=dst[:T, :], in_=dst[:T, :], func=AF.Exp, scale=-1.0), [dst], [dst])


            triP = A("triP", [128, 128]); triS = A("triS", [128, 128]); negBlk = A("negBlk", [128, 128])
            blkones = A("blkones", [128, 128]); blkind = A("blkind", [128, 16]); blkindT = A("blkindT", [16, 128])
            sel0 = A("sel0", [128, 16])
            GRP = {"qa": (0, 512), "ka": (512, 1024), "va": (1024, 1536), "oa": (1536, 2048), "g": (2048, 2056),
                   "qb": (2056, 2568), "fb": (2568, 3080), "ib": (3080, 3592), "gb": (3592, 4104)}
            w_in_v = w_in.rearrange("(kc p) n -> p kc n", p=128)
            w_in_g = {}
            for nm, lo_, hi_, keys in (("B", 2048, 3080, ("g", "qb", "fb")), ("A", 0, 2048, ("qa", "ka", "va", "oa")),
                                       ("C", 3080, 4104, ("ib", "gb"))):
                slab = A("w_in_" + nm, [128, 8, hi_ - lo_], BF16)
                S.dma("pool", slab[:, :, :], w_in_v[:, :, lo_:hi_], writes=[slab], sem_tile=slab)
                for key in keys:
                    c0_, c1_ = GRP[key]
                    w_in_g[key] = slab.view(lambda h, a=c0_ - lo_, b=c1_ - lo_: h[:, :, a:b])
            w_out_t = A("w_out_t", [128, 8, D], BF16)
            S.dma("pool", w_out_t[:, :, :], w_out.rearrange("(kc p) n -> p kc n", p=128), writes=[w_out_t],
                  sem_tile=w_out_t)
            bin_hl = A("bin_hl", [2, DIN], BF16)
            bout_hl = A("bout_hl", [2, D], BF16)

            embg = A("embg", [128, D]); embb = A("embb", [128, D])
            ga_bc = A("ga_bc", [128, 512]); gb_bc = A("gb_bc", [128, 512])
            lb_bc = A("lb_bc", [128, 512]); oml_bc = A("oml_bc", [128, 512]); bfg_bc = A("bfg_bc", [128, 4])
            ln1g_col = A("ln1g_col", [128, 8]); ln1b_col = A("ln1b_col", [128, 8])

            xin = A("xin", [128, D]); xnb = A("xnb", [128, D], BF16); xnT = A("xnT", [128, 8, 128], BF16)
            gates = A("gates", [128, 8]); g1s = A("g1s", [128, 12]); diag = A("diag", [128, 4, 128])
            qa_bf = A("qa_bf", [128, 512], BF16); qt_bf = A("qt_bf", [128, 512], BF16)
            t1 = A("t1", [128, 512]); Fb = A("Fb", [128, 512]); Eb = A("Eb", [128, 512]); Qb = A("Qb", [128, 512])
            nBc = A("nBc", [128, 4]); zero4 = A("zero4", [128, 4])

            def mkset(i):
                n = lambda x: "%s_%d" % (x, i)
                return dict(
                    xn=A(n("xn"), [128, D]), gq=A(n("gq"), [128, 8]), rmax=A(n("rmax"), [128, 4]), tot=A(n("tot"), [128, 4]),
                    va=A(n("va"), [128, 512]), ka_bf=A(n("ka_bf"), [128, 512], BF16), qkT=A(n("qkT"), [128, 8, 128], BF16),
                    SmT=A(n("SmT"), [128, 4, 128], BF16), og=A(n("og"), [128, 512]), kt_bf=A(n("kt_bf"), [128, 512], BF16),
                    iv_bf=A(n("iv_bf"), [128, 512], BF16), qkT2=A(n("qkT2"), [128, 8, 128], BF16),
                    ScT=A(n("ScT"), [128, 4, 128], BF16), gg=A(n("gg"), [128, 512]), eaT=A(n("eaT"), [128, 4, 16]))
            sets = [mkset(0), mkset(1)]
            Mc = A("Mc", [128, 4]); rr = A("rr", [128, 4]); alpha = A("alpha", [128, 4]); g2s = A("g2s", [128, 28])
            Vaug = A("Vaug", [128, 4, 129], BF16)
            hn = A("hn", [128, 4, 128]); mix = A("mix", [128, D], BF16); mixT = A("mixT", [128, 8, 128], BF16)
            yb = A("yb", [128, D]); x1hb = A("x1hb", [128, D], BF16); x1Tt = A("x1Tt", [128, 8, 128], BF16)
            Cst = A("Cst", [128, 4, 129]); Cp32 = A("Cp32", [128, 4, 129]); Cb = A("Cb", [128, 4, 129], BF16)
            Sst = A("Sst", [128, 4, 128]); Sb = A("Sb", [128, 4, 128], BF16); Stm = A("Stm", [128, 4, 128])
            nT_sb = A("nT_sb", [128, 64]); nld = A("nld", [64, 128]); alphaD = A("alphaD", [128, 64])
            selA = A("selA", [128, 16, 4])
            qTm = A("qTm", [128, 4, 128], BF16); Vm = A("Vm", [128, 4, 129], BF16); ivm = A("ivm", [128, 512], BF16)
            Cp32b = A("Cp32b", [128, 4, 129]); Cb2 = A("Cb2", [128, 4, 129], BF16); Sb2 = A("Sb2", [128, 4, 128], BF16)
            Cp32s = [Cp32, Cp32b]; Cbs = [Cb, Cb2]; Sbs = [Sb, Sb2]
            vq = lambda h: h[:, 0:512].rearrange("p (a b) -> p a b", a=4)
            qTmA = [qTm, qa_bf.view(vq)]
            qTmB = [qt_bf.view(vq), xnb.view(vq)]
            v129 = lambda h: h[:, 0:516].rearrange("p (a b) -> p a b", a=4)
            v128 = lambda h: h[:, 0:512].rearrange("p (a b) -> p a b", a=4)
            Cout1 = A("Cout1", [128, 4, 129])
            Cld = [xin.view(v128), sets[0]["xn"].view(v128), sets[0]["va"].view(v128), sets[0]["og"].view(v128)]
            nld4 = [A("nld4_%d" % i, [128, 4]) for i in range(4)]
            NCS, NSS = 4, 3
            Cout = [yb.view(v128), Cout1.view(lambda h: h[:, :, 0:128])]
            nout = [A("nout0", [128, 4]), A("nout1", [128, 4])]
            Sld = [t1.view(v128), Fb.view(v128), sets[0]["gg"].view(v128)]
            Sout = [Eb.view(v128), Qb.view(v128)]
            tmpS = xin.view(v128)
            tmp3 = Stm
            msl = A("msl", [16, 4]); mnew = A("mnew", [128, 4])
            lnw2 = (S.sb("ln_st2", [128, 12], st=sa), S.sb("ln_mv2", [128, 2], st=sa), S.sb("ln_rs2", [128, 1], st=sa))

            bias_rows(bin_hl, b_in, DIN, sets[1]["xn"], yb, x1hb)
            for t_, s_ in ((embg, ln_emb_g), (embb, ln_emb_b), (bfg_bc, b_fg)):
                S.dma("sp", t_[:, :], s_.partition_broadcast(128), writes=[t_], sem_tile=t_)
            ln_pre = [True]
            for t_, s_ in ((triP, c_triP), (blkones, c_blkones), (blkind, c_blkind)):
                S.dma("sp", t_[:, :], s_, writes=[t_], sem_tile=t_)
            lbl = yb.view(lambda h: h[:, :].rearrange("p (r c) -> p r c", r=2))
            for r in range(2):
                S.dma("sp", lbl[:, r, :], lb_log[r, :].partition_broadcast(128), writes=[lbl], sem_tile=lbl)
            S.op("dve", lambda e: e.tensor_tensor(out=lb_bc[:, :], in0=lbl[:, 0, :], in1=lbl[:, 1, :], op=ALU.subtract),
                 [lbl], [lb_bc])
            sigmoid_act(lb_bc, lb_bc[:, :], 128, [lb_bc])
            S.op("dve", lambda e: e.tensor_scalar(out=oml_bc[:, :], in0=lb_bc[:, :], scalar1=-1.0, scalar2=1.0,
                                                  op0=ALU.mult, op1=ALU.add), [lb_bc], [oml_bc])
            for t_, s_ in ((ga_bc, g_a), (gb_bc, g_b)):
                S.dma("sp", t_[:, :], s_.partition_broadcast(128), writes=[t_], sem_tile=t_)
            bias_rows(bout_hl, b_out, D, sets[1]["xn"], yb, x1hb)
            for t_, s_ in ((triS, c_triS), (negBlk, c_negBlk), (blkindT, c_blkindT), (sel0, c_sel0)):
                S.dma("sp", t_[:, :], s_, writes=[t_], sem_tile=t_)
            S.dma("sp", ln1g_col[:, :], ln1_g.rearrange("(kc p) -> p kc", p=128), writes=[ln1g_col], sem_tile=ln1g_col,
                  allow_slow_non_contiguous=True)
            S.dma("sp", ln1b_col[:, :], ln1_b.rearrange("(kc p) -> p kc", p=128), writes=[ln1b_col], sem_tile=ln1b_col,
                  allow_slow_non_contiguous=True)
            S.op("dve", lambda e: e.memset(nBc[:, :], 0.0), [], [nBc])
            S.op("dve", lambda e: e.memset(zero4[:, :], 0.0), [], [zero4])
            S.op("dve", lambda e: e.memset(Mc[:, :], 0.0), [], [Mc])
            S.op("pool", lambda e: e.memset(qTm[:, :, :], 0.0), [], [qTm])

            def proj(T, key):
                c0, c1 = GRP[key]
                p = pp()
                n = c1 - c0
                wg = w_in_g[key]
                for kc in range(8):
                    mm(p[:T, :n], xnT[:, kc, :T], wg[:, kc, :], kc == 0, False, [xnT, wg], [p])
                mm(p[:T, :n], onesB[0:2, :T], bin_hl[0:2, c0:c1], False, True, [onesB, bin_hl], [p])
                return p

            def transposes(src, T, n, dst_off, pst):
                for i in range(n):
                    S.op("pe", lambda e, i=i: e.transpose(pst[:, dst_off + i, :T], src[:T, i * 128:(i + 1) * 128],
                                                          identB[:T, :T]), [src, identB], [pst])

            def tile_info(c):
                sm = (c == 17)
                T = 16 if c == 0 else 128
                if c == 0:
                    src, col0 = meta, 0
                elif sm:
                    src, col0 = x_s, 2064
                else:
                    src, col0 = x_p[(c - 1) * 128:c * 128, :], 16 + (c - 1) * 128
                return sm, T, src, col0

            def ln_stage(c):
                sm_, T_, src_, _ = tile_info(c)
                xn_ = sets[c & 1]["xn"]
                S.dma("sp", xin[:T_, :], src_, writes=[xin], sem_tile=xin)
                layer_norm_rows(xin, T_, xn_, embg, embb, lnw, xo_bf=xnb)

            def stage1(c):
                sm, T, src, col0 = tile_info(c)
                J = 16 if sm else 1
                B = sets[c & 1]
                xn, gq, rmax, tot = B["xn"], B["gq"], B["rmax"], B["tot"]
                kt_bf, eaT = B["kt_bf"], B["eaT"]
                ka_bf, va, iv_bf, og, gg = B["ka_bf"], B["va"], B["iv_bf"], B["og"], B["gg"]
                qkT, SmT, qkT2, ScT = B["qkT"], B["SmT"], B["qkT2"], B["ScT"]
                tri = triS if sm else triP
                fab, e1, l1 = (g1s[:, 4 * i:4 * i + 4] for i in range(3))
                nBt, G = gq[:, 0:4], gq[:, 4:8]
                S.atomic += 1
                transposes(xnb, T, 8, 0, psT)
                S.atomic -= 1
                S.op("act", lambda e: e.copy(out=xnT[:, :, :T], in_=psT[:, :, :T]), [psT], [xnT])
                yield
                p = proj(T, "g")
                S.op("dve", lambda e: e.tensor_copy(out=gates[:T, :], in_=p[:T, 0:8]), [p], [gates])
                S.op("dve", lambda e: e.tensor_tensor(out=fab[:T], in0=gates[:T, 4:8], in1=bfg_bc[:T, :], op=ALU.add),
                     [gates, bfg_bc], [g1s])
                S.op("act", lambda e: e.activation(out=e1[:T], in_=fab[:T], func=AF.Exp, scale=-1.0), [g1s], [g1s])
                S.op("act", lambda e: e.activation(out=l1[:T], in_=e1[:T], func=AF.Ln, bias=ones[:T, 0:1], scale=1.0),
                     [g1s, ones], [g1s])
                yield
                p = proj(T, "fb")
                sigmoid_act(t1, p[:T, :], T, [p])
                S.op("dve", lambda e: e.tensor_tensor(out=t1[:T, :], in0=t1[:T, :], in1=oml_bc[:T, :], op=ALU.mult),
                     [t1, oml_bc], [t1])
                S.op("dve", lambda e: e.tensor_tensor(out=Fb[:T, :], in0=t1[:T, :], in1=lb_bc[:T, :], op=ALU.add),
                     [t1, lb_bc], [Fb])
                S.op("act", lambda e: e.activation(out=Fb[:T, :], in_=Fb[:T, :], func=AF.Ln), [Fb], [Fb])
                S.op("dve", lambda e: e.tensor_tensor(out=t1[:T, :], in0=oml_bc[:T, :], in1=t1[:T, :], op=ALU.subtract),
                     [t1, oml_bc], [t1])
                yield
                p3 = pp()
                mm(p3[:T, 0:4], tri[:T, :T], l1[:T], True, True, [tri, g1s], [p3])
                tot_l = blkones if sm else ones
                mm(p3[:128, 4:8], tot_l[:T, :128], l1[:T], True, True, [tot_l, g1s], [p3], skip=True)
                nBsrc = zero4 if sm else nBc
                S.op("dve", lambda e: e.tensor_tensor(out=nBt[:T], in0=p3[:T, 0:4], in1=nBsrc[:T, :], op=ALU.add),
                     [p3, nBsrc], [gq])
                S.op("dve", lambda e: e.tensor_copy(out=tot[:, :], in_=p3[:, 4:8]), [p3], [tot])
                if not sm:
                    S.op("dve", lambda e: e.tensor_tensor(out=nBc[:, :], in0=nBc[:, :], in1=tot[:, :], op=ALU.add),
                         [nBc, tot], [nBc])
                S.op("dve", lambda e: e.tensor_tensor(out=G[:T], in0=gates[:T, 0:4], in1=nBt[:T], op=ALU.add),
                     [gates, gq], [gq])
                S.op("dve", lambda e: e.tensor_tensor(out=diag[:T, :, :T], in0=bc(ident[:T, :T].unsqueeze(1), [T, 4, T]),
                                                      in1=bc(G[:T].unsqueeze(2), [T, 4, T]), op=ALU.mult),
                     [ident, gq], [diag])
                yield
                pq = proj(T, "qb")
                sigmoid_act(Qb, pq[:T, :], T, [pq])
                S.op("dve", lambda e: e.tensor_tensor(out=Qb[:T, :], in0=pq[:T, :], in1=Qb[:T, :], op=ALU.mult),
                     [pq, Qb], [Qb])
                yield
                pa = pp()
                mm(pa[:T, :], tri[:T, :T], Fb[:T, :], True, True, [tri, Fb], [pa])
                for h in range(4):
                    mm(psS[:, h, :T], ones[:T, :128], diag[:T, h, :T], True, True, [ones, diag], [psS])
                if sm:
                    S.op("dve", lambda e: e.tensor_tensor(out=tmpS[:, :, :], in0=psS[:, :, :],
                                                          in1=bc(negBlk[:, :].unsqueeze(1), [128, 4, 128]), op=ALU.add),
                         [psS, negBlk], [tmpS])
                    S.op("dve", lambda e: e.tensor_reduce(out=rmax[:, :], in_=tmpS[:, :, :], axis=AX.X, op=ALU.max),
                         [tmpS], [rmax])
                else:
                    S.op("dve", lambda e: e.tensor_reduce(out=rmax[:, :], in_=psS[:, :, :T], axis=AX.X, op=ALU.max),
                         [psS], [rmax])
                S.op("act", lambda e: e.activation(out=Eb[:T, :], in_=pa[:T, :], func=AF.Exp), [pa], [Eb])
                S.op("dve", lambda e: e.tensor_tensor(out=qt_bf[:T, :], in0=Qb[:T, :], in1=Eb[:T, :], op=ALU.mult),
                     [Qb, Eb], [qt_bf])
                S.op("act", lambda e: e.activation(out=Eb[:T, :], in_=pa[:T, :], func=AF.Exp, scale=-1.0), [pa], [Eb])
                S.op("dve", lambda e: e.tensor_tensor(out=kt_bf[:T, :], in0=t1[:T, :], in1=Eb[:T, :], op=ALU.mult),
                     [t1, Eb], [kt_bf])
                yield
                p = proj(T, "ka")
                S.op("act", lambda e: e.activation(out=ka_bf[:T, :], in_=p[:T, :], func=AF.Identity, scale=float(128 ** -0.5)),
                     [p], [ka_bf])
                yield
                p = proj(T, "qa")
                S.op("dve", lambda e: e.tensor_copy(out=qa_bf[:T, :], in_=p[:T, :]), [p], [qa_bf])
                yield
                pe_ = pp()
                rsel = blkind if sm else ones
                for h in range(4):
                    mm(pe_[:, h * J:(h + 1) * J], Fb[:T, h * 128:(h + 1) * 128], rsel[:T, 0:J], True, True,
                       [Fb, rsel], [pe_], skip=True)
                S.op("act", lambda e: e.activation(out=eaT[:, :, 0:J], in_=pe_[:, 0:4 * J].rearrange("p (h j) -> p h j", h=4),
                                                   func=AF.Exp), [pe_], [eaT])
                yield
                p = proj(T, "va")
                S.op("act", lambda e: e.copy(out=va[:T, :], in_=p[:T, :]), [p], [va])
                yield
                S.atomic += 1
                transposes(qa_bf, T, 4, 0, psT)
                transposes(ka_bf, T, 4, 4, psT)
                S.atomic -= 1
                S.op("act", lambda e: e.copy(out=qkT[:, :, :T], in_=psT[:, :, :T]), [psT], [qkT])
                yield
                p = proj(T, "ib")
                S.op("act", lambda e: e.copy(out=iv_bf[:T, :], in_=p[:T, :]), [p], [iv_bf])
                for h in range(4):
                    mm(psS[:T, h, :T], qkT[:, 4 + h, :T], qkT[:, h, :T], True, True, [qkT], [psS])
                S.op("dve", lambda e: e.tensor_tensor(out=SmT[:T, :, :T], in0=psS[:T, :, :T],
                                                      in1=bc(tri[:T, :T].unsqueeze(1), [T, 4, T]), op=ALU.mult),
                     [psS, tri], [SmT])
                yield
                S.atomic += 1
                transposes(qt_bf, T, 4, 0, psT)
                transposes(kt_bf, T, 4, 4, psT)
                S.atomic -= 1
                S.op("act", lambda e: e.copy(out=qkT2[:, :, :T], in_=psT[:, :, :T]), [psT], [qkT2])
                yield
                p = proj(T, "oa")
                sigmoid_act(og, p[:T, :], T, [p])
                S.op("pool", lambda e: e.tensor_tensor(out=og[:T, :], in0=og[:T, :], in1=ga_bc[:T, :], op=ALU.mult),
                     [og, ga_bc], [og])
                for h in range(4):
                    mm(psS[:T, h, :T], qkT2[:, 4 + h, :T], qkT2[:, h, :T], True, True, [qkT2], [psS])
                S.op("dve", lambda e: e.tensor_tensor(out=ScT[:T, :, :T], in0=psS[:T, :, :T],
                                                      in1=bc(tri[:T, :T].unsqueeze(1), [T, 4, T]), op=ALU.mult),
                     [psS, tri], [ScT])
                yield
                p = proj(T, "gb")
                sigmoid_act(gg, p[:T, :], T, [p])
                S.op("pool", lambda e: e.tensor_tensor(out=gg[:T, :], in0=gg[:T, :], in1=gb_bc[:T, :], op=ALU.mult),
                     [gg, gb_bc], [gg])
                yield
                if c + 1 < ntiles:
                    ln_stage(c + 1)

            def stage2(c):
                sm, T, src, col0 = tile_info(c)
                J = 16 if sm else 1
                has_state = (c > 0)
                B = sets[c & 1]
                xn, gq, rmax, tot = B["xn"], B["gq"], B["rmax"], B["tot"]
                ka_bf, va, iv_bf, og, gg = B["ka_bf"], B["va"], B["iv_bf"], B["og"], B["gg"]
                qkT, SmT, qkT2, ScT, kt_bf, eaT = B["qkT"], B["SmT"], B["qkT2"], B["ScT"], B["kt_bf"], B["eaT"]
                nBt, G = gq[:, 0:4], gq[:, 4:8]
                d1, u, thr, den, rec, ssq, rstd = (g2s[:, 4 * i:4 * i + 4] for i in range(7))
                psC0 = psC.view(lambda h: h[:, 0, :])
                psO0 = psO.view(lambda h: h[:, 0, :])
                def prescale():
                    S.op("dve", lambda e: e.tensor_scalar(out=yb[:T, :], in0=xn[:T, :], scalar1=float(ALPHA), scalar2=None,
                                                          op0=ALU.mult), [xn], [yb])

                def x1_tail(cp):
                    _, Tp, _, colp = tile_info(cp)
                    S.atomic += 1
                    transposes(x1hb, Tp, 8, 0, psT)
                    for kc in range(8):
                        if kc == 7:
                            S.atomic -= 1
                        S.op("act", lambda e, kc=kc: e.activation(out=x1Tt[:, kc, :Tp], in_=psT[:, kc, :Tp], func=AF.Identity,
                                                                  bias=ln1b_col[:, kc:kc + 1], scale=ln1g_col[:, kc:kc + 1]),
                             [psT, ln1b_col, ln1g_col], [x1Tt])
                    S.dma("pool", x1T_d[:, :, colp:colp + Tp], x1Tt[:, :, :Tp], reads=[x1Tt], writes=[x1T_d], sem_tile=x1Tt)

                if not sm:
                    prescale()
                if c > 0:
                    x1_tail(c - 1)
                if sm:
                    S.dma("sp", msl[:, :], m_s, writes=[msl], sem_tile=msl)
                    p2 = psC0
                    mm(p2[:128, 0:4], blkindT[:16, :128], msl[:16, :4], True, True, [blkindT, msl], [p2])
                    S.op("dve", lambda e: e.tensor_copy(out=Mc[:, :], in_=p2[:, 0:4]), [p2], [Mc])
                S.op("dve", lambda e: e.tensor_tensor(out=rr[:, :], in0=rmax[:, :], in1=Mc[:, :], op=ALU.max), [rmax, Mc], [rr])
                S.op("dve", lambda e: e.tensor_tensor(out=d1[:T], in0=G[:T], in1=rr[:T, :], op=ALU.subtract), [gq, rr], [g2s])
                S.op("act", lambda e: e.activation(out=u[:T], in_=d1[:T], func=AF.Exp), [g2s], [g2s])
                S.op("dve", lambda e: e.tensor_tensor(out=d1[:T], in0=nBt[:T], in1=rr[:T, :], op=ALU.subtract), [gq, rr], [g2s])
                S.op("act", lambda e: e.activation(out=thr[:T], in_=d1[:T], func=AF.Exp), [g2s], [g2s])
                S.op("dve", lambda e: e.tensor_tensor(out=alpha[:, :], in0=Mc[:, :], in1=rr[:, :], op=ALU.subtract),
                     [Mc, rr], [alpha])
                S.op("act", lambda e: e.activation(out=alpha[:, :], in_=alpha[:, :], func=AF.Exp), [alpha], [alpha])
                if sm:
                    S.op("dve", lambda e: e.tensor_tensor(out=mnew[:, :], in0=rr[:, :], in1=tot[:, :], op=ALU.subtract),
                         [rr, tot], [mnew])
                    S.dma("pool", m_so, mnew[0:128:8, :], reads=[mnew], sem_tile=mnew, final=True)
                    S.op("dve", lambda e: e.tensor_tensor(out=selA[:, :, :], in0=bc(sel0[:, :].unsqueeze(2), [128, 16, 4]),
                                                          in1=bc(alpha[:, :].unsqueeze(1), [128, 16, 4]), op=ALU.mult),
                         [sel0, alpha], [selA])
                    p4 = psC0
                    mm(p4[:, 0:64], ones[:, :], selA[:, :, :], True, True, [ones, selA], [p4])
                    S.op("dve", lambda e: e.tensor_copy(out=alphaD[:, :], in_=p4[:, 0:64]), [p4], [alphaD])
                else:
                    S.op("dve", lambda e: e.tensor_copy(out=Mc[:, :], in_=rr[:, :]), [rr], [Mc])
                yield
                S.op("dve", lambda e: e.tensor_tensor(out=Vaug[:T, :, 0:128], in0=va[:T, :].rearrange("p (h e) -> p h e", h=4),
                                                      in1=bc(u[:T].unsqueeze(2), [T, 4, 128]), op=ALU.mult),
                     [va, g2s], [Vaug])
                S.op("dve", lambda e: e.tensor_copy(out=Vaug[:T, :, 128], in_=u[:T]), [g2s, Vaug], [Vaug])

                yield
                def oreg(h, n=129):
                    return psO[:T, h // 2, (h % 2) * 129:(h % 2) * 129 + n]

                def creg(h):
                    return psC[:, h // 2, (h % 2) * 129:(h % 2) * 129 + 129]

                started = [False, False]

                def omm(h, lhsT, rhs, reads, last):
                    b = h // 2
                    mm(oreg(h), lhsT, rhs, not started[b], last, reads, [psO], skip=True)
                    started[b] = True

                def rms_gate(src, gate, dst_lo):
                    S.op("dve", lambda e: e.tensor_tensor(out=tmp3[:T, :, :], in0=src[:T, :, :], in1=src[:T, :, :],
                                                           op=ALU.mult), [src], [tmp3])
                    S.op("dve", lambda e: e.tensor_reduce(out=ssq[:T], in_=tmp3[:T, :, :], axis=AX.X, op=ALU.add),
                         [tmp3], [g2s])
                    S.op("act", lambda e: e.activation(out=rstd[:T], in_=ssq[:T], func=AF.Ln, bias=eps_rms[:T, :],
                                                       scale=1.0 / 128.0), [g2s, eps_rms], [g2s])
                    S.op("act", lambda e: e.activation(out=rstd[:T], in_=rstd[:T], func=AF.Exp, scale=-0.5), [g2s], [g2s])
                    S.op("dve", lambda e: e.tensor_tensor(out=src[:T, :, :], in0=src[:T, :, :],
                                                          in1=bc(rstd[:T].unsqueeze(2), [T, 4, 128]), op=ALU.mult),
                         [src, g2s], [src])
                    S.op("dve", lambda e: e.tensor_tensor(out=mix[:T, dst_lo:dst_lo + 512].rearrange("p (h e) -> p h e", h=4),
                                                          in0=src[:T, :, :], in1=gate[:T, :].rearrange("p (h e) -> p h e", h=4),
                                                          op=ALU.mult), [src, gate], [mix])

                def load_C(j):
                    k = j % NCS
                    S.dma("sp", Cld[k][:, :, :], C_s[j], writes=[Cld[k]], sem_tile=Cld[k])
                    S.dma("sp", nld4[k][:, :], n_s[j], writes=[nld4[k]], sem_tile=nld4[k])

                def mloop():
                    def X(j):
                        Cp, Cbb = Cp32s[j & 1], Cbs[j & 1]
                        if sm:
                            Csrc, al, al_r = Cld[j % NCS], alphaD[:, 4 * j:4 * j + 4], [alphaD]
                            nsrc = nld4[j % NCS]
                            S.op("dve", lambda e: e.tensor_tensor(out=Cp[:, :, 0:128], in0=Csrc[:, :, :],
                                                                  in1=bc(al.unsqueeze(2), [128, 4, 128]), op=ALU.mult),
                                 [Csrc] + al_r, [Cp])
                            S.op("dve", lambda e: e.tensor_tensor(out=Cp[:, :, 128], in0=nsrc[:, :], in1=al, op=ALU.mult),
                                 [nsrc, Cp] + al_r, [Cp])
                        else:
                            Csrc, al, al_r = Cst, alpha[:, :], [alpha]
                            if not has_state:
                                return
                            S.op("dve", lambda e: e.tensor_tensor(out=Cp[:, :, :], in0=Csrc[:, :, :],
                                                                  in1=bc(al.unsqueeze(2), [128, 4, 129]), op=ALU.mult),
                                 [Csrc] + al_r, [Cp])
                        S.op("act", lambda e: e.copy(out=Cbb[:, :, :], in_=Cp[:, :, :]), [Cp], [Cbb])
                        if sm:
                            qm = qTmA[j & 1]
                            if j > 1:
                                S.op("act", lambda e: e.mul(out=qm[:, :, 8 * (j - 2):8 * (j - 1)],
                                                            in_=qkT[:, 0:4, 8 * (j - 2):8 * (j - 1)], mul=0.0), [qkT], [qm])
                            S.op("act", lambda e: e.copy(out=qm[:, :, 8 * j:8 * j + 8],
                                                         in_=qkT[:, 0:4, 8 * j:8 * j + 8]), [qkT], [qm])
                            qop, qr = qm, [qm]
                        else:
                            qop, qr = qkT, [qkT]
                        for h in range(4):
                            omm(h, qop[:, h, :T], Cbb[:, h, :], qr + [Cbb], False)

                    def Y(j):
                        Cp = Cp32s[j & 1]
                        Cdst = Cout[j & 1] if sm else Cst
                        if sm:
                            S.op("dve", lambda e: e.tensor_scalar(out=Vm[:, :, :], in0=Vaug[:, :, :], scalar1=blkind[:, j:j + 1],
                                                                  scalar2=None, op0=ALU.mult), [Vaug, blkind], [Vm])
                            Vop = Vm
                        else:
                            Vop = Vaug
                        for h in range(4):
                            mm(creg(h), ka_bf[:T, h * 128:(h + 1) * 128], Vop[:T, h, :], True, True, [ka_bf, Vop], [psC])
                        cview = psC[:, :, 0:258].rearrange("p g (i e) -> p g i e", i=2)
                        if sm:
                            nd_ = nout[j & 1]
                            S.op("dve", lambda e: e.tensor_tensor(out=Cdst[:, :, :].rearrange("p (g i) e -> p g i e", g=2),
                                                                  in0=Cp[:, :, 0:128].rearrange("p (g i) e -> p g i e", g=2),
                                                                  in1=cview[:, :, :, 0:128], op=ALU.add), [Cp, psC], [Cdst])
                            S.op("dve", lambda e: e.tensor_tensor(out=nd_[:, :].rearrange("p (g i) -> p g i", g=2),
                                                                  in0=Cp[:, :, 128].rearrange("p (g i) -> p g i", g=2),
                                                                  in1=cview[:, :, :, 128], op=ALU.add), [Cp, psC], [nd_])
                            S.dma("pool", C_so[j], Cdst[:, :, :], reads=[Cdst], sem_tile=Cdst, final=True)
                            S.dma("pool", n_so[j], nd_[:, :], reads=[nd_], sem_tile=nd_, final=True)
                        elif has_state:
                            S.op("dve", lambda e: e.tensor_tensor(out=Cdst[:, :, :].rearrange("p (g i) e -> p g i e", g=2),
                                                                  in0=Cp[:, :, :].rearrange("p (g i) e -> p g i e", g=2),
                                                                  in1=cview, op=ALU.add), [Cp, psC], [Cdst])
                        else:
                            S.op("dve", lambda e: e.tensor_copy(out=Cdst[:, :, :].rearrange("p (g i) e -> p g i e", g=2),
                                                                in_=cview), [psC], [Cdst])

                    if not sm:
                        yield
                        X(0)
                        Y(0)
                        return
                    for j0 in range(NCS - 1):
                        load_C(j0)
                    for qb_ in qTmA:
                        S.op("pool", lambda e: e.memset(qb_[:, :, :], 0.0), [], [qb_])
                    X(0)
                    for j in range(J):
                        yield
                        if j + NCS - 1 < J:
                            load_C(j + NCS - 1)
                        if j + 1 < J:
                            X(j + 1)
                        Y(j)

                if sm:
                    pOb_t, pCb_t, Stm_ = psP[0], psP[1], diag
                    psOb = psP[0][:, :].rearrange("p (h e) -> p h e", h=4)
                    psCb = psP[1][:, :].rearrange("p (h e) -> p h e", h=4)
                else:
                    pOb_t, pCb_t, Stm_ = psO, psC, Stm
                    psOb = psO[:, 0, :].rearrange("p (h e) -> p h e", h=4)
                    psCb = psC[:, 0, :].rearrange("p (h e) -> p h e", h=4)
                startedB = [False]

                def obmm(h, lhsT, rhs, reads, last):
                    mm(psOb[:T, h, :], lhsT, rhs, not startedB[0], last, reads, [pOb_t], skip=True)
                    startedB[0] = True

                def load_S(j):
                    k = j % NSS
                    S.dma("sp", Sld[k][:, :, :], S_s[j], writes=[Sld[k]], sem_tile=Sld[k])

                def hloop():
                    def X(j):
                        if sm:
                            Ssrc, Sbb = Sld[j % NSS], Sbs[j & 1]
                            S.op("act", lambda e: e.copy(out=Sbb[:, :, :], in_=Ssrc[:, :, :]), [Ssrc], [Sbb])
                            qm = qTmB[j & 1]
                            if j > 1:
                                S.op("act", lambda e: e.mul(out=qm[:, :, 8 * (j - 2):8 * (j - 1)],
                                                            in_=qkT2[:, 0:4, 8 * (j - 2):8 * (j - 1)], mul=0.0), [qkT2], [qm])
                            S.op("act", lambda e: e.copy(out=qm[:, :, 8 * j:8 * j + 8],
                                                         in_=qkT2[:, 0:4, 8 * j:8 * j + 8]), [qkT2], [qm])
                            qop, qr = qm, [qm]
                        else:
                            Sbb = Sb
                            qop, qr = qkT2, [qkT2]
                        if has_state:
                            for h in range(4):
                                obmm(h, qop[:, h, :T], Sbb[:, h, :], qr + [Sbb], False)

                    def Y(j):
                        if sm:
                            Ssrc, Sdst = Sld[j % NSS], Sout[j & 1]
                            S.op("dve", lambda e: e.tensor_scalar(out=ivm[:, :], in0=iv_bf[:, :], scalar1=blkind[:, j:j + 1],
                                                                  scalar2=None, op0=ALU.mult), [iv_bf, blkind], [ivm])
                            ivop = ivm
                        else:
                            Ssrc, Sdst = Sst, Sst
                            ivop = iv_bf
                        for h in range(4):
                            mm(psCb[:, h, :], kt_bf[:T, h * 128:(h + 1) * 128], ivop[:T, h * 128:(h + 1) * 128], True, True,
                               [kt_bf, ivop], [pCb_t])
                        ea_j = bc(eaT[:, :, j:j + 1], [128, 4, 128])
                        if has_state:
                            S.op("dve", lambda e: e.tensor_tensor(out=Stm_[:, :, :], in0=Ssrc[:, :, :], in1=psCb, op=ALU.add),
                                 [Ssrc, pCb_t], [Stm_])
                            S.op("dve", lambda e: e.tensor_tensor(out=Sdst[:, :, :], in0=Stm_[:, :, :], in1=ea_j, op=ALU.mult),
                                 [Stm_, eaT], [Sdst])
                        else:
                            S.op("dve", lambda e: e.tensor_tensor(out=Sdst[:, :, :], in0=psCb, in1=ea_j, op=ALU.mult),
                                 [pCb_t, eaT], [Sdst])
                        if sm:
                            S.dma("pool", S_so[j], Sdst[:, :, :], reads=[Sdst], sem_tile=Sdst, final=True)
                        else:
                            S.op("act", lambda e: e.copy(out=Sb[:, :, :], in_=Sst[:, :, :]), [Sst], [Sb])

                    if not sm:
                        yield
                        X(0)
                        Y(0)
                        return
                    for j0 in range(NSS):
                        load_S(j0)
                    for qb_ in qTmB:
                        S.op("pool", lambda e: e.memset(qb_[:, :, :], 0.0), [], [qb_])
                    X(0)
                    for j in range(J):
                        yield
                        if j + 1 < J:
                            X(j + 1)
                        Y(j)
                        if j + NSS < J:
                            load_S(j + NSS)

                def mpost():
                    for h in range(4):
                        omm(h, SmT[:T, h, :T], Vaug[:T, h, :], [SmT, Vaug], True)
                    yield
                    oden = psO[:T, :, 0:258].rearrange("p g (i e) -> p g i e", i=2)[:, :, :, 128]
                    onum = psO[:T, :, 0:258].rearrange("p g (i e) -> p g i e", i=2)[:, :, :, 0:128]
                    S.op("dve", lambda e: e.tensor_tensor(out=rec[:T].rearrange("p (g i) -> p g i", g=2), in0=oden,
                                                          in1=thr[:T].rearrange("p (g i) -> p g i", g=2), op=ALU.max),
                         [psO, g2s], [g2s])
                    S.op("dve", lambda e: e.scalar_tensor_tensor(out=den[:T].rearrange("p (g i) -> p g i", g=2), in0=oden,
                                                                 scalar=-1.0, in1=rec[:T].rearrange("p (g i) -> p g i", g=2),
                                                                 op0=ALU.mult, op1=ALU.max), [psO, g2s], [g2s])
                    S.op("dve", lambda e: e.reciprocal(out=rec[:T], in_=den[:T]), [g2s], [g2s])
                    S.op("dve", lambda e: e.tensor_tensor(out=hn[:T, :, :].rearrange("p (g i) e -> p g i e", g=2), in0=onum,
                                                          in1=bc(rec[:T].rearrange("p (g i) -> p g i", g=2).unsqueeze(3),
                                                                 [T, 2, 2, 128]), op=ALU.mult), [psO, g2s], [hn])
                    yield
                    rms_gate(hn, og, 0)

                def hpost():
                    for h in range(4):
                        obmm(h, ScT[:T, h, :T], iv_bf[:T, h * 128:(h + 1) * 128], [ScT, iv_bf], True)
                    S.op("act", lambda e: e.copy(out=hn[:T, :, :], in_=psOb[:T, :, :]), [pOb_t], [hn])
                    yield
                    rms_gate(hn, gg, 512)

                def il(*gens):
                    gens = list(gens)
                    while gens:
                        for g in list(gens):
                            try:
                                next(g)
                                yield
                            except StopIteration:
                                gens.remove(g)

                if sm:
                    S.run_streams([exhaust(mloop()), exhaust(hloop())], quanta=SQ)
                    yield from mpost()
                    yield
                    yield from hpost()
                else:
                    yield from mloop()
                    yield
                    yield from mpost()
                    yield
                    yield from hloop()
                    yield
                    yield from hpost()
                yield
                if sm:
                    prescale()
                S.atomic += 1
                transposes(mix, T, 8, 0, psT)
                S.atomic -= 1
                S.op("act", lambda e: e.copy(out=mixT[:, :, :T], in_=psT[:, :, :T]), [psT], [mixT])
                for half in range(2):
                    c0 = half * 512
                    p = (psC0, psO0)[half]
                    for kc in range(8):
                        mm(p[:T, :], mixT[:, kc, :T], w_out_t[:, kc, c0:c0 + 512], kc == 0, False, [mixT, w_out_t], [p])
                    mm(p[:T, :], onesB[0:2, :T], bout_hl[0:2, c0:c0 + 512], False, True, [onesB, bout_hl], [p])
                    S.op("dve", lambda e, p=p, c0=c0: e.tensor_tensor(out=yb[:T, c0:c0 + 512], in0=yb[:T, c0:c0 + 512],
                                                                     in1=p[:T, :], op=ALU.add), [yb, p], [yb])
                yield
                layer_norm_rows(yb, T, yb, None, None, lnw2, xo_bf=x1hb)
                S.dma("pool", x1h_d[c * 128:c * 128 + T, :], yb[:T, :], reads=[yb], writes=[x1h_d], sem_tile=yb)
                if c == ntiles - 1:
                    x1_tail(c)

                yield
                if c == 16:
                    S.dma("pool", C_po.rearrange("h d e -> d h e"), Cst[:, :, 0:128], reads=[Cst], sem_tile=Cst, final=True)
                    S.dma("pool", n_po.rearrange("h d -> d h"), Cst[:, :, 128], reads=[Cst], sem_tile=Cst, final=True,
                          allow_slow_non_contiguous=True)
                    S.dma("pool", S_po.rearrange("h d e -> d h e"), Sst[:, :, :], reads=[Sst], sem_tile=Sst, final=True)
                    S.op("dve", lambda e: e.tensor_tensor(out=mnew[:, :], in0=Mc[:, :], in1=nBc[:, :], op=ALU.subtract),
                         [Mc, nBc], [mnew])
                    S.dma("pool", m_po, mnew[0:1, :], reads=[mnew], sem_tile=mnew, final=True)

            ntiles = NT if STAGE >= 2 else 2
            def run_interleaved(*gens):
                gens = [g for g in gens if g is not None]
                while gens:
                    for g in list(gens):
                        try:
                            next(g)
                        except StopIteration:
                            gens.remove(g)

            def exhaust(gen):
                def f():
                    for _ in gen:
                        pass
                return f

            ln_stage(0)
            run_interleaved(stage1(0))
            for c in range(1, ntiles):
                S.run_streams([exhaust(stage1(c)), exhaust(stage2(c - 1))], quanta=QUANTA)
            run_interleaved(stage2(ntiles - 1))
            print("phase A sbuf bytes remaining:", nc.sbuf_bytes_remaining)

        spa.close()
        S.barrier()

        with ExitStack() as sb_:
            def Bf(name, shape, dt=F32):
                return S.sb(name, shape, dt, st=sb_)

            pu = [S.ps("pu0", [128, 512], st=sb_), S.ps("pu1", [128, 512], st=sb_)]
            pg = [S.ps("pg0", [128, 512], st=sb_), S.ps("pg1", [128, 512], st=sb_)]
            pc = [S.ps("pc0", [128, 512], st=sb_), S.ps("pc1", [128, 512], st=sb_)]
            pm = S.ps("pm", [128, 512], st=sb_)

            w_dn_t = Bf("w_dn_t", [128, NFC, D], BF16)
            ln1g_bc = Bf("ln1g_bc", [128, D]); ln1b_bc = Bf("ln1b_bc", [128, D])
            ln2g_bc = Bf("ln2g_bc", [128, D]); ln2b_bc = Bf("ln2b_bc", [128, D])
            x1Th = Bf("x1Th", [128, 8, 1152], BF16)
            S.dma("sp", x1Th[:, :, 0:1040], x1T_d[:, :, 0:1040], reads=[x1T_d], writes=[x1Th], sem_tile=x1Th)
            x1l = [Bf("x1l0", [128, D]), Bf("x1l1", [128, D])]
            zb = [Bf("zb0", [128, D]), Bf("zb1", [128, D])]
            bdn_hl = Bf("bdn_hl", [2, D], BF16)

            prm_ld = Bf("prm_ld", [NFC, 6, 128]); prm = Bf("prm", [128, 6, NFC])
            srcs = [b_up[0:DFF], b_up[DFF:2 * DFF], w_conv[0, :], w_conv[1, :], w_conv[2, :], b_conv]
            for k, s_ in enumerate(srcs):
                S.dma("sp", prm_ld[:, k, :], s_.rearrange("(fc p) -> fc p", p=128), writes=[prm_ld], sem_tile=prm_ld)
            for k in range(6):
                S.op("pe", lambda e, k=k: e.transpose(pm[:, k * NFC:(k + 1) * NFC], prm_ld[:NFC, k, :], ident[:NFC, :NFC]),
                     [prm_ld, ident], [pm])
            S.op("dve", lambda e: e.tensor_copy(out=prm[:, :, :], in_=pm[:, 0:6 * NFC].rearrange("p (k f) -> p k f", k=6)),
                 [pm], [prm])
            cvbuf = Bf("cvbuf", [34, DFF]); cst = Bf("cst", [128, NFC, 32])
            S.dma("sp", cvbuf[0:32, :], cv_s, writes=[cvbuf], sem_tile=cvbuf)
            for f0 in range(0, NFC, 11):
                for fc in range(f0, f0 + 11):
                    S.op("pe", lambda e, fc=fc, f0=f0: e.transpose(pm[:, (fc - f0) * 32:(fc - f0) * 32 + 32],
                                                                  cvbuf[0:32, fc * 128:(fc + 1) * 128], ident[:32, :32]),
                         [cvbuf, ident], [pm])
                S.op("dve", lambda e, f0=f0: e.tensor_copy(out=cst[:, f0:f0 + 11, :],
                                                           in_=pm[:, 0:352].rearrange("p (f r) -> p f r", f=11)),
                     [pm], [cst])
            ulast = Bf("ulast", [128, NFC, 34]); ucar = Bf("ucar", [128, NFC, 2])
            hbuf = Bf("hbuf", [128, NFC, 1152], BF16)
            bias_rows(bdn_hl, b_down, D, zb[0], zb[1], hbuf.view(lambda h: h[:, 0, :]))
            for t_, s_ in ((ln1g_bc, ln1_g), (ln1b_bc, ln1_b), (ln2g_bc, ln2_g), (ln2b_bc, ln2_b)):
                S.dma("sp", t_[:, :], s_.partition_broadcast(128), writes=[t_], sem_tile=t_)
            wub = [Bf("wub0", [128, 8, 256], BF16), Bf("wub1", [128, 8, 256], BF16)]
            ub = [Bf("ub0", [128, 514]), Bf("ub1", [128, 514])]
            ubs = Bf("ubs", [128, 16, 10])
            cvb = [Bf("cvb0", [128, 512]), Bf("cvb1", [128, 512])]
            slb = [Bf("slb0", [128, 512]), Bf("slb1", [128, 512])]
            w_up_v = w_up.rearrange("(kc p) n -> p kc n", p=128)

            def load_wub(fc, slot):
                S.dma("pool", wub[slot][:, :, 0:128], w_up_v[:, :, fc * 128:(fc + 1) * 128], writes=[wub[slot]],
                      sem_tile=wub[slot])
                S.dma("pool", wub[slot][:, :, 128:256], w_up_v[:, :, DFF + fc * 128:DFF + (fc + 1) * 128],
                      writes=[wub[slot]], sem_tile=wub[slot])

            HALVES = [
                dict(lo=0, hi=1040, groups=[(0, 347, "p"), (347, 694, "p"), (694, 1040, "p")], tiles=list(range(1, 9))),
                dict(lo=1040, hi=2192, groups=[(1040, 1552, "p"), (1552, 2064, "p"), (2064, 2192, "s")],
                     tiles=list(range(9, 18))),
            ]
            gi = [0]
            wslot = [0]
            nhalves = 2 if STAGE >= 3 else 0
            for hf_i in range(nhalves):
                hf = HALVES[hf_i]
                lo, hi = hf["lo"], hf["hi"]
                if hf_i > 0:
                    S.dma("sp", x1Th[:, :, 0:hi - lo], x1T_d[:, :, lo:hi], reads=[x1T_d], writes=[x1Th], sem_tile=x1Th)
                load_wub(0, wslot[0])
                if hf_i == 0:
                    w_dn_v = w_down.rearrange("(fc p) n -> p fc n", p=128)
                    for f0 in range(0, NFC, 11):
                        S.dma("pool", w_dn_t[:, f0:f0 + 11, :], w_dn_v[:, f0:f0 + 11, :], writes=[w_dn_t], sem_tile=w_dn_t)
                pend = [None]
                def grp(fc, c0, c1, kind, W, s2, pn):
                    n = c1 - c0
                    l0 = c0 - lo
                    P_ = lambda k: prm[:, k, fc:fc + 1]
                    U, Gp, CV, SL, UB = pu[s2], pg[s2], cvb[s2], slb[s2], ub[s2]
                    for kc in range(8):
                        mm(U[:, :n], W[:, kc, 0:128], x1Th[:, kc, l0:l0 + n], kc == 0, kc == 7, [W, x1Th], [U])
                    for kc in range(8):
                        mm(Gp[:, :n], W[:, kc, 128:256], x1Th[:, kc, l0:l0 + n], kc == 0, kc == 7, [W, x1Th], [Gp])
                    if kind == "p":
                        UBp = ub[s2 ^ 1]
                        if c0 == 0:
                            S.op("dve", lambda e: e.memset(UB[:, 0:2], 0.0), [], [UB])
                        elif c0 == 1040:
                            S.op("dve", lambda e: e.tensor_copy(out=UB[:, 0:2], in_=ucar[:, fc, :]), [ucar], [UB])
                        else:
                            npv = pn
                            S.op("dve", lambda e: e.tensor_copy(out=UB[:, 0:2], in_=UBp[:, npv:npv + 2]), [UBp], [UB])
                        S.op("act", lambda e: e.activation(out=UB[:, 2:2 + n], in_=U[:, :n], func=AF.Identity,
                                                           bias=P_(0), scale=1.0), [U, prm], [UB])
                        S.op("act", lambda e: e.activation(out=CV[:, :n], in_=UB[:, 2:2 + n], func=AF.Identity,
                                                           bias=P_(5), scale=P_(4)), [UB, prm], [CV])
                        S.op("dve", lambda e: e.scalar_tensor_tensor(out=CV[:, :n], in0=UB[:, 1:1 + n], scalar=P_(3),
                                                                     in1=CV[:, :n], op0=ALU.mult, op1=ALU.add),
                             [UB, prm, CV], [CV])
                        S.op("dve", lambda e: e.scalar_tensor_tensor(out=CV[:, :n], in0=UB[:, 0:n], scalar=P_(2),
                                                                     in1=CV[:, :n], op0=ALU.mult, op1=ALU.add),
                             [UB, prm, CV], [CV])
                        if c1 == 1040:
                            S.op("dve", lambda e: e.tensor_copy(out=ucar[:, fc, :], in_=UB[:, n:n + 2]), [UB], [ucar])
                        if c1 == 2064:
                            S.op("dve", lambda e: e.tensor_copy(out=ulast[:, fc, 32:34], in_=UB[:, n:n + 2]), [UB], [ulast])
                        yield
                        S.op("act", lambda e: e.activation(out=SL[:, :n], in_=CV[:, :n], func=AF.Silu), [CV], [SL])
                        S.op("dve", lambda e: e.scalar_tensor_tensor(out=hbuf[:, fc, l0:l0 + n], in0=Gp[:, :n], scalar=P_(1),
                                                                     in1=SL[:, :n], op0=ALU.add, op1=ALU.mult),
                             [Gp, prm, SL], [hbuf])
                    else:
                        v3 = lambda ap: ap.rearrange("p (j t) -> p j t", j=16)
                        S.op("dve", lambda e: e.tensor_copy(out=ubs[:, :, 0:2],
                                                            in_=cst[:, fc, :].rearrange("p (j r) -> p j r", j=16)),
                             [cst], [ubs])
                        S.op("act", lambda e: e.activation(out=ubs[:, :, 2:10], in_=v3(U[:, :128]), func=AF.Identity,
                                                           bias=P_(0), scale=1.0), [U, prm], [ubs])
                        S.op("act", lambda e: e.activation(out=v3(CV[:, :128]), in_=ubs[:, :, 2:10], func=AF.Identity,
                                                           bias=P_(5), scale=P_(4)), [ubs, prm], [CV])
                        S.op("dve", lambda e: e.scalar_tensor_tensor(out=v3(CV[:, :128]), in0=ubs[:, :, 1:9], scalar=P_(3),
                                                                     in1=v3(CV[:, :128]), op0=ALU.mult, op1=ALU.add),
                             [ubs, prm, CV], [CV])
                        S.op("dve", lambda e: e.scalar_tensor_tensor(out=v3(CV[:, :128]), in0=ubs[:, :, 0:8], scalar=P_(2),
                                                                     in1=v3(CV[:, :128]), op0=ALU.mult, op1=ALU.add),
                             [ubs, prm, CV], [CV])
                        S.op("dve", lambda e: e.tensor_copy(out=ulast[:, fc, 0:32].rearrange("p (j r) -> p j r", j=16),
                                                            in_=ubs[:, :, 8:10]), [ubs], [ulast])
                        yield
                        S.op("act", lambda e: e.activation(out=SL[:, :128], in_=CV[:, :128], func=AF.Silu), [CV], [SL])
                        S.op("dve", lambda e: e.scalar_tensor_tensor(out=hbuf[:, fc, l0:l0 + 128], in0=Gp[:, :128],
                                                                     scalar=P_(1), in1=SL[:, :128], op0=ALU.add,
                                                                     op1=ALU.mult), [Gp, prm, SL], [hbuf])

                def fin(g):
                    for _ in g:
                        pass

                for fc in range(NFC):
                    cur = wslot[0]
                    if fc + 1 < NFC:
                        load_wub(fc + 1, cur ^ 1)
                    W = wub[cur]
                    pn = 0
                    for (c0, c1, kind) in hf["groups"]:
                        s2 = gi[0] & 1
                        gi[0] += 1
                        g = grp(fc, c0, c1, kind, W, s2, pn)
                        next(g)
                        if pend[0] is not None:
                            fin(pend[0])
                        pend[0] = g
                        pn = c1 - c0
                    wslot[0] ^= 1

                if pend[0] is not None:
                    fin(pend[0])
                    pend[0] = None
                tiles = hf["tiles"]

                def load_x1(idx):
                    c = tiles[idx]
                    S.dma("sp", x1l[idx & 1][:, :], x1h_d[c * 128:(c + 1) * 128, :], reads=[x1h_d], writes=[x1l[idx & 1]],
                          sem_tile=x1l[idx & 1])

                load_x1(0)
                for idx, c in enumerate(tiles):
                    if idx + 1 < len(tiles):
                        load_x1(idx + 1)
                    col0 = 2064 if c == 17 else 16 + (c - 1) * 128
                    l0 = col0 - lo
                    X, Z = x1l[idx & 1], zb[idx & 1]
                    S.op("pool", lambda e: e.tensor_tensor(out=X[:, :], in0=X[:, :], in1=ln1g_bc[:, :], op=ALU.mult),
                         [X, ln1g_bc], [X])
                    S.op("pool", lambda e: e.tensor_tensor(out=X[:, :], in0=X[:, :], in1=ln1b_bc[:, :], op=ALU.add),
                         [X, ln1b_bc], [X])
                    for hh in range(2):
                        p = pc[hh]
                        for fc in range(NFC):
                            mm(p[:, :], hbuf[:, fc, l0:l0 + 128], w_dn_t[:, fc, hh * 512:(hh + 1) * 512], fc == 0, False,
                               [hbuf, w_dn_t], [p])
                        mm(p[:, :], onesB[0:2, :128], bdn_hl[0:2, hh * 512:(hh + 1) * 512], False, True, [onesB, bdn_hl], [p])
                        S.op("dve", lambda e, p=p, hh=hh: e.scalar_tensor_tensor(out=Z[:, hh * 512:(hh + 1) * 512],
                                                                                in0=X[:, hh * 512:(hh + 1) * 512],
                                                                                scalar=float(ALPHA), in1=p[:, :], op0=ALU.mult,
                                                                                op1=ALU.add), [X, p], [Z])
                    layer_norm_rows(Z, 128, Z, ln2g_bc, ln2b_bc, lnw)
                    dst = y_s if c == 17 else y_p[(c - 1) * 128:c * 128, :]
                    S.dma("act", dst, Z[:, :], reads=[Z], sem_tile=Z, final=True)

            if nhalves == 2:
                for f0 in range(0, NFC, 4):
                    nf = min(4, NFC - f0)
                    for fc in range(f0, f0 + nf):
                        S.op("pe", lambda e, fc=fc, f0=f0: e.transpose(pm[0:34, (fc - f0) * 128:(fc - f0 + 1) * 128],
                                                                      ulast[:, fc, :], ident[:, :]), [ulast, ident], [pm])
                    S.op("dve", lambda e, f0=f0, nf=nf: e.tensor_copy(out=cvbuf[0:34, f0 * 128:(f0 + nf) * 128],
                                                                     in_=pm[0:34, 0:nf * 128]), [pm], [cvbuf])
                S.dma("act", cv_so, cvbuf[0:32, :], reads=[cvbuf], sem_tile=cvbuf, final=True)
                S.dma("act", cv_po, cvbuf[32:34, :], reads=[cvbuf], sem_tile=cvbuf, final=True)
            S.finish()

        S.finish()
    return nc


_CACHE = {}


def _consts():
    t = np.arange(128)
    same = (t[:, None] // 8) == (t[None, :] // 8)
    triP = (t[:, None] <= t[None, :]).astype(np.float32)
    triS = (triP.astype(bool) & same).astype(np.float32)
    negBlk = np.where(same, 0.0, NEG).astype(np.float32)
    blkones = same.astype(np.float32)
    blkind = (t[:, None] // 8 == np.arange(16)[None, :]).astype(np.float32)
    sel0 = (t[:, None] == 8 * np.arange(16)[None, :]).astype(np.float32)
    return {"c_ident": np.eye(128, dtype=np.float32), "c_triP": triP, "c_triS": triS, "c_negBlk": negBlk,
            "c_blkones": blkones, "c_blkind": blkind, "c_blkindT": np.ascontiguousarray(blkind.T), "c_sel0": sel0}


def kernel(x_prompt, x_sample, state_mlstm_C, state_mlstm_n, state_mlstm_m, state_hgrn_S, state_ffn_conv,
           meta_tokens, ln_emb_g, ln_emb_b, w_in, b_in, b_fgate_a, g_norm_a, g_norm_b, hgrn_lb_logits,
           w_out, b_out, ln1_g, ln1_b, w_up, b_up, w_conv, b_conv, w_down, b_down, ln2_g, ln2_b):
    f = lambda a: np.ascontiguousarray(np.asarray(a, dtype=np.float32))
    if "nc" not in _CACHE:
        _CACHE["nc"] = build_program()
    nc = _CACHE["nc"]
    shared = {
        "meta": f(meta_tokens), "ln_emb_g": f(ln_emb_g), "ln_emb_b": f(ln_emb_b),
        "w_in": f(w_in[0]), "b_in": f(b_in[0]).reshape(1, DIN), "b_fg": f(b_fgate_a[0]),
        "g_a": f(g_norm_a[0]).reshape(512), "g_b": f(g_norm_b[0]).reshape(512), "lb_log": f(hgrn_lb_logits),
        "w_out": f(w_out[0]), "b_out": f(b_out[0]).reshape(1, D), "ln1_g": f(ln1_g[0]), "ln1_b": f(ln1_b[0]),
        "w_up": f(w_up[0]), "b_up": f(b_up[0]), "w_conv": f(w_conv[0]), "b_conv": f(b_conv[0]),
        "w_down": f(w_down[0]), "b_down": f(b_down[0]).reshape(1, D), "ln2_g": f(ln2_g[0]), "ln2_b": f(ln2_b[0]),
    }
    shared.update(_consts())
    in_maps = []
    for i in range(8):
        sl = slice(16 * i, 16 * i + 16)
        m = dict(shared)
        m["x_p"] = f(x_prompt[i])
        m["x_s"] = f(x_sample[sl]).reshape(128, D)
        m["C_s"] = f(np.asarray(state_mlstm_C[0, sl]).transpose(0, 2, 1, 3))
        m["n_s"] = f(np.asarray(state_mlstm_n[0, sl]).transpose(0, 2, 1))
        m["m_s"] = f(state_mlstm_m[0, sl]); m["S_s"] = f(np.asarray(state_hgrn_S[0, sl]).transpose(0, 2, 1, 3))
        m["cv_s"] = f(state_ffn_conv[0, sl]).reshape(32, DFF)
        in_maps.append(m)
    res = run_bass_kernel_spmd(nc, in_maps, core_ids=list(range(8)))
    R = res.results
    cat = lambda k: np.stack([np.asarray(r[k], dtype=np.float32) for r in R])
    y_prompt = cat("y_p")
    y_sample = cat("y_s").reshape(128, 8, D)
    C_p = cat("C_po")[None]; n_p = cat("n_po")[None]; m_p = cat("m_po").reshape(1, 8, 4)
    S_p = cat("S_po")[None]; cv_p = cat("cv_po")[None]
    C_so = np.ascontiguousarray(cat("C_so").transpose(0, 1, 3, 2, 4)).reshape(1, 128, 4, 128, 128)
    n_so = np.ascontiguousarray(cat("n_so").transpose(0, 1, 3, 2)).reshape(1, 128, 4, 128)
    m_so = cat("m_so").reshape(1, 128, 4)
    S_so = np.ascontiguousarray(cat("S_so").transpose(0, 1, 3, 2, 4)).reshape(1, 128, 4, 128, 128)
    cv_so = cat("cv_so").reshape(1, 128, 2, DFF)
    return (y_prompt, y_sample, C_p, n_p, m_p, S_p, cv_p, C_so, n_so, m_so, S_so, cv_so)
```

```python
import os
import threading
from contextlib import ExitStack

import numpy as np
import concourse.bass as bass
import concourse.mybir as mybir
from concourse.bass_utils import run_bass_kernel_spmd

F32 = mybir.dt.float32
BF16 = mybir.dt.bfloat16
AF = mybir.ActivationFunctionType
ALU = mybir.AluOpType
AX = mybir.AxisListType

D = 1024
DIN = 4104
DFF = 2816
NFC = DFF // 128
NCOL = 2192
ALPHA = 2.0 ** 0.25
LN_EPS = 1e-5
RMS_EPS = 1e-6
NEG = -1.0e30
NT = 18
STAGE = int(os.environ.get("KSTAGE", "99"))
LEAD = (0, 0)
SQ = (1, 1)
QUANTA = (6, 4)


class _St:
    __slots__ = ("w", "r", "dsem", "dcnt")

    def __init__(self):
        self.w = None
        self.r = {}
        self.dsem = {}
        self.dcnt = {}


class Tk:
    def __init__(self, h, name, st=None):
        self.h = h
        self.name = name
        self._s = st or _St()

    def view(self, fn):
        return Tk(fn(self.h), self.name + "_v", self._s)

    def __getitem__(self, k):
        return self.h[k]

    w = property(lambda self: self._s.w, lambda self, v: setattr(self._s, "w", v))
    r = property(lambda self: self._s.r, lambda self, v: setattr(self._s, "r", v))


class Sched:
    def __init__(self, nc, st):
        self.nc = nc
        self.st = st
        self.eng = {"pe": nc.tensor, "act": nc.scalar, "dve": nc.vector, "pool": nc.gpsimd, "sp": nc.sync}
        self.sem = {k: st.enter_context(nc.semaphore("s_" + k)) for k in self.eng}
        self.cnt = {k: 0 for k in self.eng}
        self.seen = {k: {} for k in self.eng}
        self.out_events = {}
        self.all_dma = {}
        self.nd = 0
        self.cur = None
        self.atomic = 0

    def sb(self, name, shape, dt=F32, st=None):
        h = (st or self.st).enter_context(self.nc.sbuf_tensor(name, list(shape), dt))
        return Tk(h, name)

    def ps(self, name, shape, dt=F32, st=None):
        h = (st or self.st).enter_context(self.nc.psum_tensor(name, list(shape), dt))
        return Tk(h, name)

    def dram(self, name, shape, dt=F32):
        h = self.nc.dram_tensor(name, list(shape), dt, kind="Internal")
        return Tk(h.ap(), name)

    def _wait(self, e, deps):
        for key, sem, val in deps:
            if self.seen[e].get(key, 0) >= val:
                continue
            self.eng[e].wait_ge(sem, val)
            self.seen[e][key] = val

    def _deps(self, e, reads, writes):
        deps = []
        for t in reads:
            if t.w is not None:
                deps.append(t.w)
        for t in writes:
            if t.w is not None and not (e == "pe" and t.w[0] == "pe"):
                deps.append(t.w)
            for k, (sem, val) in t.r.items():
                deps.append((k, sem, val))
        return deps

    def op(self, e, fn, reads=(), writes=()):
        self._wait(e, self._deps(e, reads, writes))
        ins = fn(self.eng[e])
        self.cnt[e] += 1
        ins.then_inc(self.sem[e], 1)
        ev = (e, self.sem[e], self.cnt[e])
        for t in writes:
            t.w = ev
            t.r = {}
        for t in reads:
            t.r[e] = (self.sem[e], self.cnt[e])
        self._sw()

    def dma(self, q, out, in_, reads=(), writes=(), sem_tile=None, final=False, **kw):
        self._wait(q, self._deps(q, reads, writes))
        kind = "sw" if q == "pool" else "hw"
        stt = sem_tile._s
        if kind not in stt.dsem:
            self.nd += 1
            stt.dsem[kind] = self.st.enter_context(self.nc.semaphore("d%d" % self.nd))
            stt.dcnt[kind] = 0
        ins = self.eng[q].dma_start(out=out, in_=in_, **kw)
        stt.dcnt[kind] += 1
        ins.then_inc(stt.dsem[kind], 16)
        key = ("d", id(stt), kind)
        ev = (key, stt.dsem[kind], 16 * stt.dcnt[kind])
        for t in writes:
            t.w = ev
            t.r = {}
        for t in reads:
            t.r[key] = (ev[1], ev[2])
        self.all_dma[key] = ev
        if final:
            self.out_events[key] = ev
        self._sw()

    def _sw(self):
        st = self.cur
        if st is not None:
            st.n += 1
            if self.atomic == 0 and st.n >= st.quantum:
                st.n = 0
                st.back.release()
                st.go.acquire()

    def run_streams(self, fns, quanta=None):
        class _Stream:
            pass
        streams = []
        for i, fn in enumerate(fns):
            st = _Stream()
            st.n = -(LEAD[i] if (quanta and i < len(LEAD)) else 0)
            st.quantum = quanta[i] if quanta else 1
            st.go = threading.Semaphore(0); st.back = threading.Semaphore(0); st.done = False; st.exc = None

            def body(st=st, fn=fn):
                st.go.acquire()
                try:
                    fn()
                except BaseException as e:
                    st.exc = e
                st.done = True
                st.back.release()
            st.th = threading.Thread(target=body)
            st.th.start()
            streams.append(st)
        live = list(streams)
        while live:
            for st in list(live):
                self.cur = st
                st.go.release()
                st.back.acquire()
                self.cur = None
                if st.done:
                    st.th.join()
                    live.remove(st)
                    if st.exc is not None:
                        for o in live:
                            o.done = True
                        raise st.exc
        self.cur = None

    def barrier(self):
        deps = list(self.all_dma.values())
        for e in ("pe", "act", "dve", "pool"):
            if self.cnt[e] > 0:
                deps.append((e, self.sem[e], self.cnt[e]))
        for e in self.eng:
            self._wait(e, [d for d in deps if d[0] != e])

    def finish(self):
        deps = list(self.out_events.values())
        for e in ("pe", "act", "dve", "pool"):
            if self.cnt[e] > 0:
                deps.append((e, self.sem[e], self.cnt[e]))
        self._wait("sp", deps)


def bc(ap, shape):
    return ap.to_broadcast(list(shape))


def build_program():
    nc = bass.Bass("TRN2", target_bir_lowering=False)

    def di(name, shape, dt=F32):
        return nc.dram_tensor(name, list(shape), dt, kind="ExternalInput").ap()

    def do(name, shape):
        return nc.dram_tensor(name, list(shape), F32, kind="ExternalOutput").ap()

    x_p = di("x_p", [2048, D]); x_s = di("x_s", [128, D]); meta = di("meta", [16, D])
    C_s = di("C_s", [16, 128, 4, 128]); n_s = di("n_s", [16, 128, 4]); m_s = di("m_s", [16, 4])
    S_s = di("S_s", [16, 128, 4, 128]); cv_s = di("cv_s", [32, DFF])
    ln_emb_g = di("ln_emb_g", [D]); ln_emb_b = di("ln_emb_b", [D])
    w_in = di("w_in", [D, DIN]); b_in = di("b_in", [1, DIN]); b_fg = di("b_fg", [4])
    g_a = di("g_a", [512]); g_b = di("g_b", [512]); lb_log = di("lb_log", [2, 512])
    w_out = di("w_out", [D, D]); b_out = di("b_out", [1, D])
    ln1_g = di("ln1_g", [D]); ln1_b = di("ln1_b", [D])
    w_up = di("w_up", [D, 2 * DFF]); b_up = di("b_up", [2 * DFF])
    w_conv = di("w_conv", [3, DFF]); b_conv = di("b_conv", [DFF])
    w_down = di("w_down", [DFF, D]); b_down = di("b_down", [1, D])
    ln2_g = di("ln2_g", [D]); ln2_b = di("ln2_b", [D])
    c_ident = di("c_ident", [128, 128]); c_triP = di("c_triP", [128, 128]); c_triS = di("c_triS", [128, 128])
    c_negBlk = di("c_negBlk", [128, 128]); c_blkones = di("c_blkones", [128, 128])
    c_blkind = di("c_blkind", [128, 16]); c_blkindT = di("c_blkindT", [16, 128]); c_sel0 = di("c_sel0", [128, 16])

    y_p = do("y_p", [2048, D]); y_s = do("y_s", [128, D])
    C_po = do("C_po", [4, 128, 128]); n_po = do("n_po", [4, 128]); m_po = do("m_po", [1, 4])
    S_po = do("S_po", [4, 128, 128]); cv_po = do("cv_po", [2, DFF])
    C_so = do("C_so", [16, 128, 4, 128]); n_so = do("n_so", [16, 128, 4]); m_so = do("m_so", [16, 4])
    S_so = do("S_so", [16, 128, 4, 128]); cv_so = do("cv_so", [32, DFF])

    with ExitStack() as st:
        S = Sched(nc, st)
        x1h_d = S.dram("x1h_d", [NT * 128, D], F32)
        x1T_d = S.dram("x1T_d", [128, 8, NCOL], BF16)

        ident = S.sb("ident", [128, 128]); identB = S.sb("identB", [128, 128], BF16)
        ones = S.sb("ones", [128, 128]); onesB = S.sb("onesB", [128, 128], BF16)
        S.dma("sp", ident[:, :], c_ident, writes=[ident], sem_tile=ident)
        S.op("dve", lambda e: e.tensor_copy(out=identB[:, :], in_=ident[:, :]), [ident], [identB])
        S.op("dve", lambda e: e.memset(ones[:, :], 1.0), [], [ones])
        S.op("dve", lambda e: e.memset(onesB[:, :], 1.0), [], [onesB])

        spa = ExitStack()
        psT = S.ps("psT", [128, 8, 128], BF16, st=spa)
        psP = [S.ps("psP0", [128, 512], st=spa), S.ps("psP1", [128, 512], st=spa)]
        psS = S.ps("psS", [128, 4, 128], st=spa)
        psO = S.ps("psO", [128, 2, 512], st=spa)
        psC = S.ps("psC", [128, 2, 512], st=spa)
        pp_i = [0]

        def pp():
            pp_i[0] ^= 1
            return psP[pp_i[0]]

        def mm(out, lhsT, rhs, start, stop, reads, writes, skip=False):
            kw = {"skip_group_check": True} if skip else {}
            S.op("pe", lambda e: e.matmul(out, lhsT, rhs, start=start, stop=stop, **kw), reads, writes)

        def bias_rows(hl, src, n, f, h32, tb):
            for c0 in range(0, n, 1024):
                w = min(1024, n - c0)
                S.dma("sp", f[0:1, :w], src[:, c0:c0 + w], writes=[f], sem_tile=f)
                S.op("dve", lambda e: e.tensor_copy(out=hl[0:1, c0:c0 + w], in_=f[0:1, :w]), [f], [hl])
                S.op("dve", lambda e: e.tensor_copy(out=h32[0:1, :w], in_=hl[0:1, c0:c0 + w]), [hl], [h32])
                S.op("dve", lambda e: e.tensor_tensor(out=tb[0:1, :w], in0=f[0:1, :w], in1=h32[0:1, :w],
                                                      op=ALU.subtract), [f, h32], [tb])
                S.dma("sp", hl[1:2, c0:c0 + w], tb[0:1, :w], reads=[tb], writes=[hl], sem_tile=hl)

        def layer_norm_rows(x, T, xo, g_bc, b_bc, wk, xo_bf=None):
            stt, mv, rs = wk
            for cch in range(2):
                S.op("dve", lambda e, cch=cch: e.bn_stats(out=stt[:T, cch * 6:cch * 6 + 6], in_=x[:T, cch * 512:(cch + 1) * 512]),
                     [x], [stt])
            S.op("dve", lambda e: e.bn_aggr(out=mv[:T, :], in_=stt[:T, :]), [stt], [mv])
            S.op("act", lambda e: e.activation(out=rs[:T, :], in_=mv[:T, 1:2], func=AF.Ln, bias=eps_ln[:T, :], scale=1.0),
                 [mv, eps_ln], [rs])
            S.op("act", lambda e: e.activation(out=rs[:T, :], in_=rs[:T, :], func=AF.Exp, scale=-0.5), [rs], [rs])
            if g_bc is None:
                if xo_bf is not None:
                    S.op("dve", lambda e: e.tensor_scalar(out=xo_bf[:T, :], in0=x[:T, :], scalar1=mv[:T, 0:1], scalar2=rs[:T, 0:1],
                                                          op0=ALU.subtract, op1=ALU.mult), [x, mv, rs], [xo_bf])
                S.op("dve", lambda e: e.tensor_scalar(out=xo[:T, :], in0=x[:T, :], scalar1=mv[:T, 0:1], scalar2=rs[:T, 0:1],
                                                      op0=ALU.subtract, op1=ALU.mult), [x, mv, rs], [xo])
                return
            S.op("dve", lambda e: e.tensor_scalar(out=xo[:T, :], in0=x[:T, :], scalar1=mv[:T, 0:1], scalar2=rs[:T, 0:1],
                                                  op0=ALU.subtract, op1=ALU.mult), [x, mv, rs], [xo])
            S.op("dve", lambda e: e.tensor_tensor(out=xo[:T, :], in0=xo[:T, :], in1=g_bc[:T, :], op=ALU.mult),
                 [xo, g_bc], [xo])
            if xo_bf is not None:
                S.op("dve", lambda e: e.tensor_tensor(out=xo_bf[:T, :], in0=xo[:T, :], in1=b_bc[:T, :], op=ALU.add),
                     [xo, b_bc], [xo_bf])
                S.op("dve", lambda e: e.tensor_tensor(out=xo[:T, :], in0=xo[:T, :], in1=b_bc[:T, :], op=ALU.add),
                     [xo, b_bc], [xo])
            else:
                S.op("dve", lambda e: e.tensor_tensor(out=xo[:T, :], in0=xo[:T, :], in1=b_bc[:T, :], op=ALU.add),
                     [xo, b_bc], [xo])

        eps_ln = S.sb("eps_ln", [128, 1]); eps_rms = S.sb("eps_rms", [128, 1])
        S.op("dve", lambda e: e.memset(eps_ln[:, :], LN_EPS), [], [eps_ln])
        S.op("dve", lambda e: e.memset(eps_rms[:, :], RMS_EPS), [], [eps_rms])
        lnw = (S.sb("ln_st", [128, 12]), S.sb("ln_mv", [128, 2]), S.sb("ln_rs", [128, 1]))

        with ExitStack() as sa:
            def A(name, shape, dt=F32):
                return S.sb(name, shape, dt, st=sa)

            def sigmoid_act(dst, src_ap, T, reads):
                S.op("act", lambda e: e.activation(out=dst[:T, :], in_=src_ap, func=AF.Exp, scale=-1.0), reads, [dst])
                S.op("act", lambda e: e.activation(out=dst[:T, :], in_=dst[:T, :], func=AF.Ln, bias=ones[:T, 0:1], scale=1.0),
                     [dst, ones], [dst])
                S.op("act", lambda e: e.activation(out=dst[:T, :], in_=dst[:T, :], func=AF.Exp, scale=-1.0), [dst], [dst])


            triP = A("triP", [128, 128]); triS = A("triS", [128, 128]); negBlk = A("negBlk", [128, 128])
            blkones = A("blkones", [128, 128]); blkind = A("blkind", [128, 16]); blkindT = A("blkindT", [16, 128])
            sel0 = A("sel0", [128, 16])
            GRP = {"qa": (0, 512), "ka": (512, 1024), "va": (1024, 1536), "oa": (1536, 2048), "g": (2048, 2056),
                   "qb": (2056, 2568), "fb": (2568, 3080), "ib": (3080, 3592), "gb": (3592, 4104)}
            w_in_v = w_in.rearrange("(kc p) n -> p kc n", p=128)
            w_in_g = {}
            for nm, lo_, hi_, keys in (("B", 2048, 3080, ("g", "qb", "fb")), ("A", 0, 2048, ("qa", "ka", "va", "oa")),
                                       ("C", 3080, 4104, ("ib", "gb"))):
                slab = A("w_in_" + nm, [128, 8, hi_ - lo_], BF16)
                S.dma("pool", slab[:, :, :], w_in_v[:, :, lo_:hi_], writes=[slab], sem_tile=slab)
                for key in keys:
                    c0_, c1_ = GRP[key]
                    w_in_g[key] = slab.view(lambda h, a=c0_ - lo_, b=c1_ - lo_: h[:, :, a:b])
            w_out_t = A("w_out_t", [128, 8, D], BF16)
            S.dma("pool", w_out_t[:, :, :], w_out.rearrange("(kc p) n -> p kc n", p=128), writes=[w_out_t],
                  sem_tile=w_out_t)
            bin_hl = A("bin_hl", [2, DIN], BF16)
            bout_hl = A("bout_hl", [2, D], BF16)

            embg = A("embg", [128, D]); embb = A("embb", [128, D])
            ga_bc = A("ga_bc", [128, 512]); gb_bc = A("gb_bc", [128, 512])
            lb_bc = A("lb_bc", [128, 512]); oml_bc = A("oml_bc", [128, 512]); bfg_bc = A("bfg_bc", [128, 4])
            ln1g_col = A("ln1g_col", [128, 8]); ln1b_col = A("ln1b_col", [128, 8])

            xin = A("xin", [128, D]); xnb = A("xnb", [128, D], BF16); xnT = A("xnT", [128, 8, 128], BF16)
            gates = A("gates", [128, 8]); g1s = A("g1s", [128, 12]); diag = A("diag", [128, 4, 128])
            qa_bf = A("qa_bf", [128, 512], BF16); qt_bf = A("qt_bf", [128, 512], BF16)
            t1 = A("t1", [128, 512]); Fb = A("Fb", [128, 512]); Eb = A("Eb", [128, 512]); Qb = A("Qb", [128, 512])
            nBc = A("nBc", [128, 4]); zero4 = A("zero4", [128, 4])

            def mkset(i):
                n = lambda x: "%s_%d" % (x, i)
                return dict(
                    xn=A(n("xn"), [128, D]), gq=A(n("gq"), [128, 8]), rmax=A(n("rmax"), [128, 4]), tot=A(n("tot"), [128, 4]),
                    va=A(n("va"), [128, 512]), ka_bf=A(n("ka_bf"), [128, 512], BF16), qkT=A(n("qkT"), [128, 8, 128], BF16),
                    SmT=A(n("SmT"), [128, 4, 128], BF16), og=A(n("og"), [128, 512]), kt_bf=A(n("kt_bf"), [128, 512], BF16),
                    iv_bf=A(n("iv_bf"), [128, 512], BF16), qkT2=A(n("qkT2"), [128, 8, 128], BF16),
                    ScT=A(n("ScT"), [128, 4, 128], BF16), gg=A(n("gg"), [128, 512]), eaT=A(n("eaT"), [128, 4, 16]))
            sets = [mkset(0), mkset(1)]
            Mc = A("Mc", [128, 4]); rr = A("rr", [128, 4]); alpha = A("alpha", [128, 4]); g2s = A("g2s", [128, 28])
            Vaug = A("Vaug", [128, 4, 129], BF16)
            hn = A("hn", [128, 4, 128]); mix = A("mix", [128, D], BF16); mixT = A("mixT", [128, 8, 128], BF16)
            yb = A("yb", [128, D]); x1hb = A("x1hb", [128, D], BF16); x1Tt = A("x1Tt", [128, 8, 128], BF16)
            Cst = A("Cst", [128, 4, 129]); Cp32 = A("Cp32", [128, 4, 129]); Cb = A("Cb", [128, 4, 129], BF16)
            Sst = A("Sst", [128, 4, 128]); Sb = A("Sb", [128, 4, 128], BF16); Stm = A("Stm", [128, 4, 128])
            nT_sb = A("nT_sb", [128, 64]); nld = A("nld", [64, 128]); alphaD = A("alphaD", [128, 64])
            selA = A("selA", [128, 16, 4])
            qTm = A("qTm", [128, 4, 128], BF16); Vm = A("Vm", [128, 4, 129], BF16); ivm = A("ivm", [128, 512], BF16)
            Cp32b = A("Cp32b", [128, 4, 129]); Cb2 = A("Cb2", [128, 4, 129], BF16); Sb2 = A("Sb2", [128, 4, 128], BF16)
            Cp32s = [Cp32, Cp32b]; Cbs = [Cb, Cb2]; Sbs = [Sb, Sb2]
            vq = lambda h: h[:, 0:512].rearrange("p (a b) -> p a b", a=4)
            qTmA = [qTm, qa_bf.view(vq)]
            qTmB = [qt_bf.view(vq), xnb.view(vq)]
            v129 = lambda h: h[:, 0:516].rearrange("p (a b) -> p a b", a=4)
            v128 = lambda h: h[:, 0:512].rearrange("p (a b) -> p a b", a=4)
            Cout1 = A("Cout1", [128, 4, 129])
            Cld = [xin.view(v128), sets[0]["xn"].view(v128), sets[0]["va"].view(v128), sets[0]["og"].view(v128)]
            nld4 = [A("nld4_%d" % i, [128, 4]) for i in range(4)]
            NCS, NSS = 4, 3
            Cout = [yb.view(v128), Cout1.view(lambda h: h[:, :, 0:128])]
            nout = [A("nout0", [128, 4]), A("nout1", [128, 4])]
            Sld = [t1.view(v128), Fb.view(v128), sets[0]["gg"].view(v128)]
            Sout = [Eb.view(v128), Qb.view(v128)]
            tmpS = xin.view(v128)
            tmp3 = Stm
            msl = A("msl", [16, 4]); mnew = A("mnew", [128, 4])
            lnw2 = (S.sb("ln_st2", [128, 12], st=sa), S.sb("ln_mv2", [128, 2], st=sa), S.sb("ln_rs2", [128, 1], st=sa))

            bias_rows(bin_hl, b_in, DIN, sets[1]["xn"], yb, x1hb)
            for t_, s_ in ((embg, ln_emb_g), (embb, ln_emb_b), (bfg_bc, b_fg)):
                S.dma("sp", t_[:, :], s_.partition_broadcast(128), writes=[t_], sem_tile=t_)
            ln_pre = [True]
            for t_, s_ in ((triP, c_triP), (blkones, c_blkones), (blkind, c_blkind)):
                S.dma("sp", t_[:, :], s_, writes=[t_], sem_tile=t_)
            lbl = yb.view(lambda h: h[:, :].rearrange("p (r c) -> p r c", r=2))
            for r in range(2):
                S.dma("sp", lbl[:, r, :], lb_log[r, :].partition_broadcast(128), writes=[lbl], sem_tile=lbl)
            S.op("dve", lambda e: e.tensor_tensor(out=lb_bc[:, :], in0=lbl[:, 0, :], in1=lbl[:, 1, :], op=ALU.subtract),
                 [lbl], [lb_bc])
            sigmoid_act(lb_bc, lb_bc[:, :], 128, [lb_bc])
            S.op("dve", lambda e: e.tensor_scalar(out=oml_bc[:, :], in0=lb_bc[:, :], scalar1=-1.0, scalar2=1.0,
                                                  op0=ALU.mult, op1=ALU.add), [lb_bc], [oml_bc])
            for t_, s_ in ((ga_bc, g_a), (gb_bc, g_b)):
                S.dma("sp", t_[:, :], s_.partition_broadcast(128), writes=[t_], sem_tile=t_)
            bias_rows(bout_hl, b_out, D, sets[1]["xn"], yb, x1hb)
            for t_, s_ in ((triS, c_triS), (negBlk, c_negBlk), (blkindT, c_blkindT), (sel0, c_sel0)):
                S.dma("sp", t_[:, :], s_, writes=[t_], sem_tile=t_)
            S.dma("sp", ln1g_col[:, :], ln1_g.rearrange("(kc p) -> p kc", p=128), writes=[ln1g_col], sem_tile=ln1g_col,
                  allow_slow_non_contiguous=True)
            S.dma("sp", ln1b_col[:, :], ln1_b.rearrange("(kc p) -> p kc", p=128), writes=[ln1b_col], sem_tile=ln1b_col,
                  allow_slow_non_contiguous=True)
            S.op("dve", lambda e: e.memset(nBc[:, :], 0.0), [], [nBc])
            S.op("dve", lambda e: e.memset(zero4[:, :], 0.0), [], [zero4])
            S.op("dve", lambda e: e.memset(Mc[:, :], 0.0), [], [Mc])
            S.op("pool", lambda e: e.memset(qTm[:, :, :], 0.0), [], [qTm])

            def proj(T, key):
                c0, c1 = GRP[key]
                p = pp()
                n = c1 - c0
                wg = w_in_g[key]
                for kc in range(8):
                    mm(p[:T, :n], xnT[:, kc, :T], wg[:, kc, :], kc == 0, False, [xnT, wg], [p])
                mm(p[:T, :n], onesB[0:2, :T], bin_hl[0:2, c0:c1], False, True, [onesB, bin_hl], [p])
                return p

            def transposes(src, T, n, dst_off, pst):
                for i in range(n):
                    S.op("pe", lambda e, i=i: e.transpose(pst[:, dst_off + i, :T], src[:T, i * 128:(i + 1) * 128],
                                                          identB[:T, :T]), [src, identB], [pst])

            def tile_info(c):
                sm = (c == 17)
                T = 16 if c == 0 else 128
                if c == 0:
                    src, col0 = meta, 0
                elif sm:
                    src, col0 = x_s, 2064
                else:
                    src, col0 = x_p[(c - 1) * 128:c * 128, :], 16 + (c - 1) * 128
                return sm, T, src, col0

            def ln_stage(c):
                sm_, T_, src_, _ = tile_info(c)
                xn_ = sets[c & 1]["xn"]
                S.dma("sp", xin[:T_, :], src_, writes=[xin], sem_tile=xin)
                layer_norm_rows(xin, T_, xn_, embg, embb, lnw, xo_bf=xnb)

            def stage1(c):
                sm, T, src, col0 = tile_info(c)
                J = 16 if sm else 1
                B = sets[c & 1]
                xn, gq, rmax, tot = B["xn"], B["gq"], B["rmax"], B["tot"]
                kt_bf, eaT = B["kt_bf"], B["eaT"]
                ka_bf, va, iv_bf, og, gg = B["ka_bf"], B["va"], B["iv_bf"], B["og"], B["gg"]
                qkT, SmT, qkT2, ScT = B["qkT"], B["SmT"], B["qkT2"], B["ScT"]
                tri = triS if sm else triP
                fab, e1, l1 = (g1s[:, 4 * i:4 * i + 4] for i in range(3))
                nBt, G = gq[:, 0:4], gq[:, 4:8]
                S.atomic += 1
                transposes(xnb, T, 8, 0, psT)
                S.atomic -= 1
                S.op("act", lambda e: e.copy(out=xnT[:, :, :T], in_=psT[:, :, :T]), [psT], [xnT])
                yield
                p = proj(T, "g")
                S.op("dve", lambda e: e.tensor_copy(out=gates[:T, :], in_=p[:T, 0:8]), [p], [gates])
                S.op("dve", lambda e: e.tensor_tensor(out=fab[:T], in0=gates[:T, 4:8], in1=bfg_bc[:T, :], op=ALU.add),
                     [gates, bfg_bc], [g1s])
                S.op("act", lambda e: e.activation(out=e1[:T], in_=fab[:T], func=AF.Exp, scale=-1.0), [g1s], [g1s])
                S.op("act", lambda e: e.activation(out=l1[:T], in_=e1[:T], func=AF.Ln, bias=ones[:T, 0:1], scale=1.0),
                     [g1s, ones], [g1s])
                yield
                p = proj(T, "fb")
                sigmoid_act(t1, p[:T, :], T, [p])
                S.op("dve", lambda e: e.tensor_tensor(out=t1[:T, :], in0=t1[:T, :], in1=oml_bc[:T, :], op=ALU.mult),
                     [t1, oml_bc], [t1])
                S.op("dve", lambda e: e.tensor_tensor(out=Fb[:T, :], in0=t1[:T, :], in1=lb_bc[:T, :], op=ALU.add),
                     [t1, lb_bc], [Fb])
                S.op("act", lambda e: e.activation(out=Fb[:T, :], in_=Fb[:T, :], func=AF.Ln), [Fb], [Fb])
                S.op("dve", lambda e: e.tensor_tensor(out=t1[:T, :], in0=oml_bc[:T, :], in1=t1[:T, :], op=ALU.subtract),
                     [t1, oml_bc], [t1])
                yield
                p3 = pp()
                mm(p3[:T, 0:4], tri[:T, :T], l1[:T], True, True, [tri, g1s], [p3])
                tot_l = blkones if sm else ones
                mm(p3[:128, 4:8], tot_l[:T, :128], l1[:T], True, True, [tot_l, g1s], [p3], skip=True)
                nBsrc = zero4 if sm else nBc
                S.op("dve", lambda e: e.tensor_tensor(out=nBt[:T], in0=p3[:T, 0:4], in1=nBsrc[:T, :], op=ALU.add),
                     [p3, nBsrc], [gq])
                S.op("dve", lambda e: e.tensor_copy(out=tot[:, :], in_=p3[:, 4:8]), [p3], [tot])
                if not sm:
                    S.op("dve", lambda e: e.tensor_tensor(out=nBc[:, :], in0=nBc[:, :], in1=tot[:, :], op=ALU.add),
                         [nBc, tot], [nBc])
                S.op("dve", lambda e: e.tensor_tensor(out=G[:T], in0=gates[:T, 0:4], in1=nBt[:T], op=ALU.add),
                     [gates, gq], [gq])
                S.op("dve", lambda e: e.tensor_tensor(out=diag[:T, :, :T], in0=bc(ident[:T, :T].unsqueeze(1), [T, 4, T]),
                                                      in1=bc(G[:T].unsqueeze(2), [T, 4, T]), op=ALU.mult),
                     [ident, gq], [diag])
                yield
                pq = proj(T, "qb")
                sigmoid_act(Qb, pq[:T, :], T, [pq])
                S.op("dve", lambda e: e.tensor_tensor(out=Qb[:T, :], in0=pq[:T, :], in1=Qb[:T, :], op=ALU.mult),
                     [pq, Qb], [Qb])
                yield
                pa = pp()
                mm(pa[:T, :], tri[:T, :T], Fb[:T, :], True, True, [tri, Fb], [pa])
                for h in range(4):
                    mm(psS[:, h, :T], ones[:T, :128], diag[:T, h, :T], True, True, [ones, diag], [psS])
                if sm:
                    S.op("dve", lambda e: e.tensor_tensor(out=tmpS[:, :, :], in0=psS[:, :, :],
                                                          in1=bc(negBlk[:, :].unsqueeze(1), [128, 4, 128]), op=ALU.add),
                         [psS, negBlk], [tmpS])
                    S.op("dve", lambda e: e.tensor_reduce(out=rmax[:, :], in_=tmpS[:, :, :], axis=AX.X, op=ALU.max),
                         [tmpS], [rmax])
                else:
                    S.op("dve", lambda e: e.tensor_reduce(out=rmax[:, :], in_=psS[:, :, :T], axis=AX.X, op=ALU.max),
                         [psS], [rmax])
                S.op("act", lambda e: e.activation(out=Eb[:T, :], in_=pa[:T, :], func=AF.Exp), [pa], [Eb])
                S.op("dve", lambda e: e.tensor_tensor(out=qt_bf[:T, :], in0=Qb[:T, :], in1=Eb[:T, :], op=ALU.mult),
                     [Qb, Eb], [qt_bf])
                S.op("act", lambda e: e.activation(out=Eb[:T, :], in_=pa[:T, :], func=AF.Exp, scale=-1.0), [pa], [Eb])
                S.op("dve", lambda e: e.tensor_tensor(out=kt_bf[:T, :], in0=t1[:T, :], in1=Eb[:T, :], op=ALU.mult),
                     [t1, Eb], [kt_bf])
                yield
                p = proj(T, "ka")
                S.op("act", lambda e: e.activation(out=ka_bf[:T, :], in_=p[:T, :], func=AF.Identity, scale=float(128 ** -0.5)),
                     [p], [ka_bf])
                yield
                p = proj(T, "qa")
                S.op("dve", lambda e: e.tensor_copy(out=qa_bf[:T, :], in_=p[:T, :]), [p], [qa_bf])
                yield
                pe_ = pp()
                rsel = blkind if sm else ones
                for h in range(4):
                    mm(pe_[:, h * J:(h + 1) * J], Fb[:T, h * 128:(h + 1) * 128], rsel[:T, 0:J], True, True,
                       [Fb, rsel], [pe_], skip=True)
                S.op("act", lambda e: e.activation(out=eaT[:, :, 0:J], in_=pe_[:, 0:4 * J].rearrange("p (h j) -> p h j", h=4),
                                                   func=AF.Exp), [pe_], [eaT])
                yield
                p = proj(T, "va")
                S.op("act", lambda e: e.copy(out=va[:T, :], in_=p[:T, :]), [p], [va])
                yield
                S.atomic += 1
                transposes(qa_bf, T, 4, 0, psT)
                transposes(ka_bf, T, 4, 4, psT)
                S.atomic -= 1
                S.op("act", lambda e: e.copy(out=qkT[:, :, :T], in_=psT[:, :, :T]), [psT], [qkT])
                yield
                p = proj(T, "ib")
                S.op("act", lambda e: e.copy(out=iv_bf[:T, :], in_=p[:T, :]), [p], [iv_bf])
                for h in range(4):
                    mm(psS[:T, h, :T], qkT[:, 4 + h, :T], qkT[:, h, :T], True, True, [qkT], [psS])
                S.op("dve", lambda e: e.tensor_tensor(out=SmT[:T, :, :T], in0=psS[:T, :, :T],
                                                      in1=bc(tri[:T, :T].unsqueeze(1), [T, 4, T]), op=ALU.mult),
                     [psS, tri], [SmT])
                yield
                S.atomic += 1
                transposes(qt_bf, T, 4, 0, psT)
                transposes(kt_bf, T, 4, 4, psT)
                S.atomic -= 1
                S.op("act", lambda e: e.copy(out=qkT2[:, :, :T], in_=psT[:, :, :T]), [psT], [qkT2])
                yield
                p = proj(T, "oa")
                sigmoid_act(og, p[:T, :], T, [p])
                S.op("pool", lambda e: e.tensor_tensor(out=og[:T, :], in0=og[:T, :], in1=ga_bc[:T, :], op=ALU.mult),
                     [og, ga_bc], [og])
                for h in range(4):
                    mm(psS[:T, h, :T], qkT2[:, 4 + h, :T], qkT2[:, h, :T], True, True, [qkT2], [psS])
                S.op("dve", lambda e: e.tensor_tensor(out=ScT[:T, :, :T], in0=psS[:T, :, :T],
                                                      in1=bc(tri[:T, :T].unsqueeze(1), [T, 4, T]), op=ALU.mult),
                     [psS, tri], [ScT])
                yield
                p = proj(T, "gb")
                sigmoid_act(gg, p[:T, :], T, [p])
                S.op("pool", lambda e: e.tensor_tensor(out=gg[:T, :], in0=gg[:T, :], in1=gb_bc[:T, :], op=ALU.mult),
                     [gg, gb_bc], [gg])
                yield
                if c + 1 < ntiles:
                    ln_stage(c + 1)

            def stage2(c):
                sm, T, src, col0 = tile_info(c)
                J = 16 if sm else 1
                has_state = (c > 0)
                B = sets[c & 1]
                xn, gq, rmax, tot = B["xn"], B["gq"], B["rmax"], B["tot"]
                ka_bf, va, iv_bf, og, gg = B["ka_bf"], B["va"], B["iv_bf"], B["og"], B["gg"]
                qkT, SmT, qkT2, ScT, kt_bf, eaT = B["qkT"], B["SmT"], B["qkT2"], B["ScT"], B["kt_bf"], B["eaT"]
                nBt, G = gq[:, 0:4], gq[:, 4:8]
                d1, u, thr, den, rec, ssq, rstd = (g2s[:, 4 * i:4 * i + 4] for i in range(7))
                psC0 = psC.view(lambda h: h[:, 0, :])
                psO0 = psO.view(lambda h: h[:, 0, :])
                def prescale():
                    S.op("dve", lambda e: e.tensor_scalar(out=yb[:T, :], in0=xn[:T, :], scalar1=float(ALPHA), scalar2=None,
                                                          op0=ALU.mult), [xn], [yb])

                def x1_tail(cp):
                    _, Tp, _, colp = tile_info(cp)
                    S.atomic += 1
                    transposes(x1hb, Tp, 8, 0, psT)
                    for kc in range(8):
                        if kc == 7:
                            S.atomic -= 1
                        S.op("act", lambda e, kc=kc: e.activation(out=x1Tt[:, kc, :Tp], in_=psT[:, kc, :Tp], func=AF.Identity,
                                                                  bias=ln1b_col[:, kc:kc + 1], scale=ln1g_col[:, kc:kc + 1]),
                             [psT, ln1b_col, ln1g_col], [x1Tt])
                    S.dma("pool", x1T_d[:, :, colp:colp + Tp], x1Tt[:, :, :Tp], reads=[x1Tt], writes=[x1T_d], sem_tile=x1Tt)

                if not sm:
                    prescale()
                if c > 0:
                    x1_tail(c - 1)
                if sm:
                    S.dma("sp", msl[:, :], m_s, writes=[msl], sem_tile=msl)
                    p2 = psC0
                    mm(p2[:128, 0:4], blkindT[:16, :128], msl[:16, :4], True, True, [blkindT, msl], [p2])
                    S.op("dve", lambda e: e.tensor_copy(out=Mc[:, :], in_=p2[:, 0:4]), [p2], [Mc])
                S.op("dve", lambda e: e.tensor_tensor(out=rr[:, :], in0=rmax[:, :], in1=Mc[:, :], op=ALU.max), [rmax, Mc], [rr])
                S.op("dve", lambda e: e.tensor_tensor(out=d1[:T], in0=G[:T], in1=rr[:T, :], op=ALU.subtract), [gq, rr], [g2s])
                S.op("act", lambda e: e.activation(out=u[:T], in_=d1[:T], func=AF.Exp), [g2s], [g2s])
                S.op("dve", lambda e: e.tensor_tensor(out=d1[:T], in0=nBt[:T], in1=rr[:T, :], op=ALU.subtract), [gq, rr], [g2s])
                S.op("act", lambda e: e.activation(out=thr[:T], in_=d1[:T], func=AF.Exp), [g2s], [g2s])
                S.op("dve", lambda e: e.tensor_tensor(out=alpha[:, :], in0=Mc[:, :], in1=rr[:, :], op=ALU.subtract),
                     [Mc, rr], [alpha])
                S.op("act", lambda e: e.activation(out=alpha[:, :], in_=alpha[:, :], func=AF.Exp), [alpha], [alpha])
                if sm:
                    S.op("dve", lambda e: e.tensor_tensor(out=mnew[:, :], in0=rr[:, :], in1=tot[:, :], op=ALU.subtract),
                         [rr, tot], [mnew])
                    S.dma("pool", m_so, mnew[0:128:8, :], reads=[mnew], sem_tile=mnew, final=True)
                    S.op("dve", lambda e: e.tensor_tensor(out=selA[:, :, :], in0=bc(sel0[:, :].unsqueeze(2), [128, 16, 4]),
                                                          in1=bc(alpha[:, :].unsqueeze(1), [128, 16, 4]), op=ALU.mult),
                         [sel0, alpha], [selA])
                    p4 = psC0
                    mm(p4[:, 0:64], ones[:, :], selA[:, :, :], True, True, [ones, selA], [p4])
                    S.op("dve", lambda e: e.tensor_copy(out=alphaD[:, :], in_=p4[:, 0:64]), [p4], [alphaD])
                else:
                    S.op("dve", lambda e: e.tensor_copy(out=Mc[:, :], in_=rr[:, :]), [rr], [Mc])
                yield
                S.op("dve", lambda e: e.tensor_tensor(out=Vaug[:T, :, 0:128], in0=va[:T, :].rearrange("p (h e) -> p h e", h=4),
                                                      in1=bc(u[:T].unsqueeze(2), [T, 4, 128]), op=ALU.mult),
                     [va, g2s], [Vaug])
                S.op("dve", lambda e: e.tensor_copy(out=Vaug[:T, :, 128], in_=u[:T]), [g2s, Vaug], [Vaug])

                yield
                def oreg(h, n=129):
                    return psO[:T, h // 2, (h % 2) * 129:(h % 2) * 129 + n]

                def creg(h):
                    return psC[:, h // 2, (h % 2) * 129:(h % 2) * 129 + 129]

                started = [False, False]

                def omm(h, lhsT, rhs, reads, last):
                    b = h // 2
                    mm(oreg(h), lhsT, rhs, not started[b], last, reads, [psO], skip=True)
                    started[b] = True

                def rms_gate(src, gate, dst_lo):
                    S.op("dve", lambda e: e.tensor_tensor(out=tmp3[:T, :, :], in0=src[:T, :, :], in1=src[:T, :, :],
                                                           op=ALU.mult), [src], [tmp3])
                    S.op("dve", lambda e: e.tensor_reduce(out=ssq[:T], in_=tmp3[:T, :, :], axis=AX.X, op=ALU.add),
                         [tmp3], [g2s])
                    S.op("act", lambda e: e.activation(out=rstd[:T], in_=ssq[:T], func=AF.Ln, bias=eps_rms[:T, :],
                                                       scale=1.0 / 128.0), [g2s, eps_rms], [g2s])
                    S.op("act", lambda e: e.activation(out=rstd[:T], in_=rstd[:T], func=AF.Exp, scale=-0.5), [g2s], [g2s])
                    S.op("dve", lambda e: e.tensor_tensor(out=src[:T, :, :], in0=src[:T, :, :],
                                                          in1=bc(rstd[:T].unsqueeze(2), [T, 4, 128]), op=ALU.mult),
                         [src, g2s], [src])
                    S.op("dve", lambda e: e.tensor_tensor(out=mix[:T, dst_lo:dst_lo + 512].rearrange("p (h e) -> p h e", h=4),
                                                          in0=src[:T, :, :], in1=gate[:T, :].rearrange("p (h e) -> p h e", h=4),
                                                          op=ALU.mult), [src, gate], [mix])

                def load_C(j):
                    k = j % NCS
                    S.dma("sp", Cld[k][:, :, :], C_s[j], writes=[Cld[k]], sem_tile=Cld[k])
                    S.dma("sp", nld4[k][:, :], n_s[j], writes=[nld4[k]], sem_tile=nld4[k])

                def mloop():
                    def X(j):
                        Cp, Cbb = Cp32s[j & 1], Cbs[j & 1]
                        if sm:
                            Csrc, al, al_r = Cld[j % NCS], alphaD[:, 4 * j:4 * j + 4], [alphaD]
                            nsrc = nld4[j % NCS]
                            S.op("dve", lambda e: e.tensor_tensor(out=Cp[:, :, 0:128], in0=Csrc[:, :, :],
                                                                  in1=bc(al.unsqueeze(2), [128, 4, 128]), op=ALU.mult),
                                 [Csrc] + al_r, [Cp])
                            S.op("dve", lambda e: e.tensor_tensor(out=Cp[:, :, 128], in0=nsrc[:, :], in1=al, op=ALU.mult),
                                 [nsrc, Cp] + al_r, [Cp])
                        else:
                            Csrc, al, al_r = Cst, alpha[:, :], [alpha]
                            if not has_state:
                                return
                            S.op("dve", lambda e: e.tensor_tensor(out=Cp[:, :, :], in0=Csrc[:, :, :],
                                                                  in1=bc(al.unsqueeze(2), [128, 4, 129]), op=ALU.mult),
                                 [Csrc] + al_r, [Cp])
                        S.op("act", lambda e: e.copy(out=Cbb[:, :, :], in_=Cp[:, :, :]), [Cp], [Cbb])
                        if sm:
                            qm = qTmA[j & 1]
                            if j > 1:
                                S.op("act", lambda e: e.mul(out=qm[:, :, 8 * (j - 2):8 * (j - 1)],
                                                            in_=qkT[:, 0:4, 8 * (j - 2):8 * (j - 1)], mul=0.0), [qkT], [qm])
                            S.op("act", lambda e: e.copy(out=qm[:, :, 8 * j:8 * j + 8],
                                                         in_=qkT[:, 0:4, 8 * j:8 * j + 8]), [qkT], [qm])
                            qop, qr = qm, [qm]
                        else:
                            qop, qr = qkT, [qkT]
                        for h in range(4):
                            omm(h, qop[:, h, :T], Cbb[:, h, :], qr + [Cbb], False)

                    def Y(j):
                        Cp = Cp32s[j & 1]
                        Cdst = Cout[j & 1] if sm else Cst
                        if sm:
                            S.op("dve", lambda e: e.tensor_scalar(out=Vm[:, :, :], in0=Vaug[:, :, :], scalar1=blkind[:, j:j + 1],
                                                                  scalar2=None, op0=ALU.mult), [Vaug, blkind], [Vm])
                            Vop = Vm
                        else:
                            Vop = Vaug
                        for h in range(4):
                            mm(creg(h), ka_bf[:T, h * 128:(h + 1) * 128], Vop[:T, h, :], True, True, [ka_bf, Vop], [psC])
                        cview = psC[:, :, 0:258].rearrange("p g (i e) -> p g i e", i=2)
                        if sm:
                            nd_ = nout[j & 1]
                            S.op("dve", lambda e: e.tensor_tensor(out=Cdst[:, :, :].rearrange("p (g i) e -> p g i e", g=2),
                                                                  in0=Cp[:, :, 0:128].rearrange("p (g i) e -> p g i e", g=2),
                                                                  in1=cview[:, :, :, 0:128], op=ALU.add), [Cp, psC], [Cdst])
                            S.op("dve", lambda e: e.tensor_tensor(out=nd_[:, :].rearrange("p (g i) -> p g i", g=2),
                                                                  in0=Cp[:, :, 128].rearrange("p (g i) -> p g i", g=2),
                                                                  in1=cview[:, :, :, 128], op=ALU.add), [Cp, psC], [nd_])
                            S.dma("pool", C_so[j], Cdst[:, :, :], reads=[Cdst], sem_tile=Cdst, final=True)
                            S.dma("pool", n_so[j], nd_[:, :], reads=[nd_], sem_tile=nd_, final=True)
                        elif has_state:
                            S.op("dve", lambda e: e.tensor_tensor(out=Cdst[:, :, :].rearrange("p (g i) e -> p g i e", g=2),
                                                                  in0=Cp[:, :, :].rearrange("p (g i) e -> p g i e", g=2),
                                                                  in1=cview, op=ALU.add), [Cp, psC], [Cdst])
                        else:
                            S.op("dve", lambda e: e.tensor_copy(out=Cdst[:, :, :].rearrange("p (g i) e -> p g i e", g=2),
                                                                in_=cview), [psC], [Cdst])

                    if not sm:
                        yield
                        X(0)
                        Y(0)
                        return
                    for j0 in range(NCS - 1):
                        load_C(j0)
                    for qb_ in qTmA:
                        S.op("pool", lambda e: e.memset(qb_[:, :, :], 0.0), [], [qb_])
                    X(0)
                    for j in range(J):
                        yield
                        if j + NCS - 1 < J:
                            load_C(j + NCS - 1)
                        if j + 1 < J:
                            X(j + 1)
                        Y(j)

                if sm:
                    pOb_t, pCb_t, Stm_ = psP[0], psP[1], diag
                    psOb = psP[0][:, :].rearrange("p (h e) -> p h e", h=4)
                    psCb = psP[1][:, :].rearrange("p (h e) -> p h e", h=4)
                else:
                    pOb_t, pCb_t, Stm_ = psO, psC, Stm
                    psOb = psO[:, 0, :].rearrange("p (h e) -> p h e", h=4)
                    psCb = psC[:, 0, :].rearrange("p (h e) -> p h e", h=4)
                startedB = [False]

                def obmm(h, lhsT, rhs, reads, last):
                    mm(psOb[:T, h, :], lhsT, rhs, not startedB[0], last, reads, [pOb_t], skip=True)
                    startedB[0] = True

                def load_S(j):
                    k = j % NSS
                    S.dma("sp", Sld[k][:, :, :], S_s[j], writes=[Sld[k]], sem_tile=Sld[k])

                def hloop():
                    def X(j):
                        if sm:
                            Ssrc, Sbb = Sld[j % NSS], Sbs[j & 1]
                            S.op("act", lambda e: e.copy(out=Sbb[:, :, :], in_=Ssrc[:, :, :]), [Ssrc], [Sbb])
                            qm = qTmB[j & 1]
                            if j > 1:
                                S.op("act", lambda e: e.mul(out=qm[:, :, 8 * (j - 2):8 * (j - 1)],
                                                            in_=qkT2[:, 0:4, 8 * (j - 2):8 * (j - 1)], mul=0.0), [qkT2], [qm])
                            S.op("act", lambda e: e.copy(out=qm[:, :, 8 * j:8 * j + 8],
                                                         in_=qkT2[:, 0:4, 8 * j:8 * j + 8]), [qkT2], [qm])
                            qop, qr = qm, [qm]
                        else:
                            Sbb = Sb
                            qop, qr = qkT2, [qkT2]
                        if has_state:
                            for h in range(4):
                                obmm(h, qop[:, h, :T], Sbb[:, h, :], qr + [Sbb], False)

                    def Y(j):
                        if sm:
                            Ssrc, Sdst = Sld[j % NSS], Sout[j & 1]
                            S.op("dve", lambda e: e.tensor_scalar(out=ivm[:, :], in0=iv_bf[:, :], scalar1=blkind[:, j:j + 1],
                                                                  scalar2=None, op0=ALU.mult), [iv_bf, blkind], [ivm])
                            ivop = ivm
                        else:
                            Ssrc, Sdst = Sst, Sst
                            ivop = iv_bf
                        for h in range(4):
                            mm(psCb[:, h, :], kt_bf[:T, h * 128:(h + 1) * 128], ivop[:T, h * 128:(h + 1) * 128], True, True,
                               [kt_bf, ivop], [pCb_t])
                        ea_j = bc(eaT[:, :, j:j + 1], [128, 4, 128])
                        if has_state:
                            S.op("dve", lambda e: e.tensor_tensor(out=Stm_[:, :, :], in0=Ssrc[:, :, :], in1=psCb, op=ALU.add),
                                 [Ssrc, pCb_t], [Stm_])
                            S.op("dve", lambda e: e.tensor_tensor(out=Sdst[:, :, :], in0=Stm_[:, :, :], in1=ea_j, op=ALU.mult),
                                 [Stm_, eaT], [Sdst])
                        else:
                            S.op("dve", lambda e: e.tensor_tensor(out=Sdst[:, :, :], in0=psCb, in1=ea_j, op=ALU.mult),
                                 [pCb_t, eaT], [Sdst])
                        if sm:
                            S.dma("pool", S_so[j], Sdst[:, :, :], reads=[Sdst], sem_tile=Sdst, final=True)
                        else:
                            S.op("act", lambda e: e.copy(out=Sb[:, :, :], in_=Sst[:, :, :]), [Sst], [Sb])

                    if not sm:
                        yield
                        X(0)
                        Y(0)
                        return
                    for j0 in range(NSS):
                        load_S(j0)
                    for qb_ in qTmB:
                        S.op("pool", lambda e: e.memset(qb_[:, :, :], 0.0), [], [qb_])
                    X(0)
                    for j in range(J):
                        yield
                        if j + 1 < J:
                            X(j + 1)
                        Y(j)
                        if j + NSS < J:
                            load_S(j + NSS)

                def mpost():
                    for h in range(4):
                        omm(h, SmT[:T, h, :T], Vaug[:T, h, :], [SmT, Vaug], True)
                    yield
                    oden = psO[:T, :, 0:258].rearrange("p g (i e) -> p g i e", i=2)[:, :, :, 128]
                    onum = psO[:T, :, 0:258].rearrange("p g (i e) -> p g i e", i=2)[:, :, :, 0:128]
                    S.op("dve", lambda e: e.tensor_tensor(out=rec[:T].rearrange("p (g i) -> p g i", g=2), in0=oden,
                                                          in1=thr[:T].rearrange("p (g i) -> p g i", g=2), op=ALU.max),
                         [psO, g2s], [g2s])
                    S.op("dve", lambda e: e.scalar_tensor_tensor(out=den[:T].rearrange("p (g i) -> p g i", g=2), in0=oden,
                                                                 scalar=-1.0, in1=rec[:T].rearrange("p (g i) -> p g i", g=2),
                                                                 op0=ALU.mult, op1=ALU.max), [psO, g2s], [g2s])
                    S.op("dve", lambda e: e.reciprocal(out=rec[:T], in_=den[:T]), [g2s], [g2s])
                    S.op("dve", lambda e: e.tensor_tensor(out=hn[:T, :, :].rearrange("p (g i) e -> p g i e", g=2), in0=onum,
                                                          in1=bc(rec[:T].rearrange("p (g i) -> p g i", g=2).unsqueeze(3),
                                                                 [T, 2, 2, 128]), op=ALU.mult), [psO, g2s], [hn])
                    yield
                    rms_gate(hn, og, 0)

                def hpost():
                    for h in range(4):
                        obmm(h, ScT[:T, h, :T], iv_bf[:T, h * 128:(h + 1) * 128], [ScT, iv_bf], True)
                    S.op("act", lambda e: e.copy(out=hn[:T, :, :], in_=psOb[:T, :, :]), [pOb_t], [hn])
                    yield
                    rms_gate(hn, gg, 512)

                def il(*gens):
                    gens = list(gens)
                    while gens:
                        for g in list(gens):
                            try:
                                next(g)
                                yield
                            except StopIteration:
                                gens.remove(g)

                if sm:
                    S.run_streams([exhaust(mloop()), exhaust(hloop())], quanta=SQ)
                    yield from mpost()
                    yield
                    yield from hpost()
                else:
                    yield from mloop()
                    yield
                    yield from mpost()
                    yield
                    yield from hloop()
                    yield
                    yield from hpost()
                yield
                if sm:
                    prescale()
                S.atomic += 1
                transposes(mix, T, 8, 0, psT)
                S.atomic -= 1
                S.op("act", lambda e: e.copy(out=mixT[:, :, :T], in_=psT[:, :, :T]), [psT], [mixT])
                for half in range(2):
                    c0 = half * 512
                    p = (psC0, psO0)[half]
                    for kc in range(8):
                        mm(p[:T, :], mixT[:, kc, :T], w_out_t[:, kc, c0:c0 + 512], kc == 0, False, [mixT, w_out_t], [p])
                    mm(p[:T, :], onesB[0:2, :T], bout_hl[0:2, c0:c0 + 512], False, True, [onesB, bout_hl], [p])
                    S.op("dve", lambda e, p=p, c0=c0: e.tensor_tensor(out=yb[:T, c0:c0 + 512], in0=yb[:T, c0:c0 + 512],
                                                                     in1=p[:T, :], op=ALU.add), [yb, p], [yb])
                yield
                layer_norm_rows(yb, T, yb, None, None, lnw2, xo_bf=x1hb)
                S.dma("pool", x1h_d[c * 128:c * 128 + T, :], yb[:T, :], reads=[yb], writes=[x1h_d], sem_tile=yb)
                if c == ntiles - 1:
                    x1_tail(c)

                yield
                if c == 16:
                    S.dma("pool", C_po.rearrange("h d e -> d h e"), Cst[:, :, 0:128], reads=[Cst], sem_tile=Cst, final=True)
                    S.dma("pool", n_po.rearrange("h d -> d h"), Cst[:, :, 128], reads=[Cst], sem_tile=Cst, final=True,
                          allow_slow_non_contiguous=True)
                    S.dma("pool", S_po.rearrange("h d e -> d h e"), Sst[:, :, :], reads=[Sst], sem_tile=Sst, final=True)
                    S.op("dve", lambda e: e.tensor_tensor(out=mnew[:, :], in0=Mc[:, :], in1=nBc[:, :], op=ALU.subtract),
                         [Mc, nBc], [mnew])
                    S.dma("pool", m_po, mnew[0:1, :], reads=[mnew], sem_tile=mnew, final=True)

            ntiles = NT if STAGE >= 2 else 2
            def run_interleaved(*gens):
                gens = [g for g in gens if g is not None]
                while gens:
                    for g in list(gens):
                        try:
                            next(g)
                        except StopIteration:
                            gens.remove(g)

            def exhaust(gen):
                def f():
                    for _ in gen:
                        pass
                return f

            ln_stage(0)
            run_interleaved(stage1(0))
            for c in range(1, ntiles):
                S.run_streams([exhaust(stage1(c)), exhaust(stage2(c - 1))], quanta=QUANTA)
            run_interleaved(stage2(ntiles - 1))
            print("phase A sbuf bytes remaining:", nc.sbuf_bytes_remaining)

        spa.close()
        S.barrier()

        with ExitStack() as sb_:
            def Bf(name, shape, dt=F32):
                return S.sb(name, shape, dt, st=sb_)

            pu = [S.ps("pu0", [128, 512], st=sb_), S.ps("pu1", [128, 512], st=sb_)]
            pg = [S.ps("pg0", [128, 512], st=sb_), S.ps("pg1", [128, 512], st=sb_)]
            pc = [S.ps("pc0", [128, 512], st=sb_), S.ps("pc1", [128, 512], st=sb_)]
            pm = S.ps("pm", [128, 512], st=sb_)

            w_dn_t = Bf("w_dn_t", [128, NFC, D], BF16)
            ln1g_bc = Bf("ln1g_bc", [128, D]); ln1b_bc = Bf("ln1b_bc", [128, D])
            ln2g_bc = Bf("ln2g_bc", [128, D]); ln2b_bc = Bf("ln2b_bc", [128, D])
            x1Th = Bf("x1Th", [128, 8, 1152], BF16)
            S.dma("sp", x1Th[:, :, 0:1040], x1T_d[:, :, 0:1040], reads=[x1T_d], writes=[x1Th], sem_tile=x1Th)
            x1l = [Bf("x1l0", [128, D]), Bf("x1l1", [128, D])]
            zb = [Bf("zb0", [128, D]), Bf("zb1", [128, D])]
            bdn_hl = Bf("bdn_hl", [2, D], BF16)

            prm_ld = Bf("prm_ld", [NFC, 6, 128]); prm = Bf("prm", [128, 6, NFC])
            srcs = [b_up[0:DFF], b_up[DFF:2 * DFF], w_conv[0, :], w_conv[1, :], w_conv[2, :], b_conv]
            for k, s_ in enumerate(srcs):
                S.dma("sp", prm_ld[:, k, :], s_.rearrange("(fc p) -> fc p", p=128), writes=[prm_ld], sem_tile=prm_ld)
            for k in range(6):
                S.op("pe", lambda e, k=k: e.transpose(pm[:, k * NFC:(k + 1) * NFC], prm_ld[:NFC, k, :], ident[:NFC, :NFC]),
                     [prm_ld, ident], [pm])
            S.op("dve", lambda e: e.tensor_copy(out=prm[:, :, :], in_=pm[:, 0:6 * NFC].rearrange("p (k f) -> p k f", k=6)),
                 [pm], [prm])
            cvbuf = Bf("cvbuf", [34, DFF]); cst = Bf("cst", [128, NFC, 32])
            S.dma("sp", cvbuf[0:32, :], cv_s, writes=[cvbuf], sem_tile=cvbuf)
            for f0 in range(0, NFC, 11):
                for fc in range(f0, f0 + 11):
                    S.op("pe", lambda e, fc=fc, f0=f0: e.transpose(pm[:, (fc - f0) * 32:(fc - f0) * 32 + 32],
                                                                  cvbuf[0:32, fc * 128:(fc + 1) * 128], ident[:32, :32]),
                         [cvbuf, ident], [pm])
                S.op("dve", lambda e, f0=f0: e.tensor_copy(out=cst[:, f0:f0 + 11, :],
                                                           in_=pm[:, 0:352].rearrange("p (f r) -> p f r", f=11)),
                     [pm], [cst])
            ulast = Bf("ulast", [128, NFC, 34]); ucar = Bf("ucar", [128, NFC, 2])
            hbuf = Bf("hbuf", [128, NFC, 1152], BF16)
            bias_rows(bdn_hl, b_down, D, zb[0], zb[1], hbuf.view(lambda h: h[:, 0, :]))
            for t_, s_ in ((ln1g_bc, ln1_g), (ln1b_bc, ln1_b), (ln2g_bc, ln2_g), (ln2b_bc, ln2_b)):
                S.dma("sp", t_[:, :], s_.partition_broadcast(128), writes=[t_], sem_tile=t_)
            wub = [Bf("wub0", [128, 8, 256], BF16), Bf("wub1", [128, 8, 256], BF16)]
            ub = [Bf("ub0", [128, 514]), Bf("ub1", [128, 514])]
            ubs = Bf("ubs", [128, 16, 10])
            cvb = [Bf("cvb0", [128, 512]), Bf("cvb1", [128, 512])]
            slb = [Bf("slb0", [128, 512]), Bf("slb1", [128, 512])]
            w_up_v = w_up.rearrange("(kc p) n -> p kc n", p=128)

            def load_wub(fc, slot):
                S.dma("pool", wub[slot][:, :, 0:128], w_up_v[:, :, fc * 128:(fc + 1) * 128], writes=[wub[slot]],
                      sem_tile=wub[slot])
                S.dma("pool", wub[slot][:, :, 128:256], w_up_v[:, :, DFF + fc * 128:DFF + (fc + 1) * 128],
                      writes=[wub[slot]], sem_tile=wub[slot])

            HALVES = [
                dict(lo=0, hi=1040, groups=[(0, 347, "p"), (347, 694, "p"), (694, 1040, "p")], tiles=list(range(1, 9))),
                dict(lo=1040, hi=2192, groups=[(1040, 1552, "p"), (1552, 2064, "p"), (2064, 2192, "s")],
                     tiles=list(range(9, 18))),
            ]
            gi = [0]
            wslot = [0]
            nhalves = 2 if STAGE >= 3 else 0
            for hf_i in range(nhalves):
                hf = HALVES[hf_i]
                lo, hi = hf["lo"], hf["hi"]
                if hf_i > 0:
                    S.dma("sp", x1Th[:, :, 0:hi - lo], x1T_d[:, :, lo:hi], reads=[x1T_d], writes=[x1Th], sem_tile=x1Th)
                load_wub(0, wslot[0])
                if hf_i == 0:
                    w_dn_v = w_down.rearrange("(fc p) n -> p fc n", p=128)
                    for f0 in range(0, NFC, 11):
                        S.dma("pool", w_dn_t[:, f0:f0 + 11, :], w_dn_v[:, f0:f0 + 11, :], writes=[w_dn_t], sem_tile=w_dn_t)
                pend = [None]
                def grp(fc, c0, c1, kind, W, s2, pn):
                    n = c1 - c0
                    l0 = c0 - lo
                    P_ = lambda k: prm[:, k, fc:fc + 1]
                    U, Gp, CV, SL, UB = pu[s2], pg[s2], cvb[s2], slb[s2], ub[s2]
                    for kc in range(8):
                        mm(U[:, :n], W[:, kc, 0:128], x1Th[:, kc, l0:l0 + n], kc == 0, kc == 7, [W, x1Th], [U])
                    for kc in range(8):
                        mm(Gp[:, :n], W[:, kc, 128:256], x1Th[:, kc, l0:l0 + n], kc == 0, kc == 7, [W, x1Th], [Gp])
                    if kind == "p":
                        UBp = ub[s2 ^ 1]
                        if c0 == 0:
                            S.op("dve", lambda e: e.memset(UB[:, 0:2], 0.0), [], [UB])
                        elif c0 == 1040:
                            S.op("dve", lambda e: e.tensor_copy(out=UB[:, 0:2], in_=ucar[:, fc, :]), [ucar], [UB])
                        else:
                            npv = pn
                            S.op("dve", lambda e: e.tensor_copy(out=UB[:, 0:2], in_=UBp[:, npv:npv + 2]), [UBp], [UB])
                        S.op("act", lambda e: e.activation(out=UB[:, 2:2 + n], in_=U[:, :n], func=AF.Identity,
                                                           bias=P_(0), scale=1.0), [U, prm], [UB])
                        S.op("act", lambda e: e.activation(out=CV[:, :n], in_=UB[:, 2:2 + n], func=AF.Identity,
                                                           bias=P_(5), scale=P_(4)), [UB, prm], [CV])
                        S.op("dve", lambda e: e.scalar_tensor_tensor(out=CV[:, :n], in0=UB[:, 1:1 + n], scalar=P_(3),
                                                                     in1=CV[:, :n], op0=ALU.mult, op1=ALU.add),
                             [UB, prm, CV], [CV])
                        S.op("dve", lambda e: e.scalar_tensor_tensor(out=CV[:, :n], in0=UB[:, 0:n], scalar=P_(2),
                                                                     in1=CV[:, :n], op0=ALU.mult, op1=ALU.add),
                             [UB, prm, CV], [CV])
                        if c1 == 1040:
                            S.op("dve", lambda e: e.tensor_copy(out=ucar[:, fc, :], in_=UB[:, n:n + 2]), [UB], [ucar])
                        if c1 == 2064:
                            S.op("dve", lambda e: e.tensor_copy(out=ulast[:, fc, 32:34], in_=UB[:, n:n + 2]), [UB], [ulast])
                        yield
                        S.op("act", lambda e: e.activation(out=SL[:, :n], in_=CV[:, :n], func=AF.Silu), [CV], [SL])
                        S.op("dve", lambda e: e.scalar_tensor_tensor(out=hbuf[:, fc, l0:l0 + n], in0=Gp[:, :n], scalar=P_(1),
                                                                     in1=SL[:, :n], op0=ALU.add, op1=ALU.mult),
                             [Gp, prm, SL], [hbuf])
                    else:
                        v3 = lambda ap: ap.rearrange("p (j t) -> p j t", j=16)
                        S.op("dve", lambda e: e.tensor_copy(out=ubs[:, :, 0:2],
                                                            in_=cst[:, fc, :].rearrange("p (j r) -> p j r", j=16)),
                             [cst], [ubs])
                        S.op("act", lambda e: e.activation(out=ubs[:, :, 2:10], in_=v3(U[:, :128]), func=AF.Identity,
                                                           bias=P_(0), scale=1.0), [U, prm], [ubs])
                        S.op("act", lambda e: e.activation(out=v3(CV[:, :128]), in_=ubs[:, :, 2:10], func=AF.Identity,
                                                           bias=P_(5), scale=P_(4)), [ubs, prm], [CV])
                        S.op("dve", lambda e: e.scalar_tensor_tensor(out=v3(CV[:, :128]), in0=ubs[:, :, 1:9], scalar=P_(3),
                                                                     in1=v3(CV[:, :128]), op0=ALU.mult, op1=ALU.add),
                             [ubs, prm, CV], [CV])
                        S.op("dve", lambda e: e.scalar_tensor_tensor(out=v3(CV[:, :128]), in0=ubs[:, :, 0:8], scalar=P_(2),
                                                                     in1=v3(CV[:, :128]), op0=ALU.mult, op1=ALU.add),
                             [ubs, prm, CV], [CV])
                        S.op("dve", lambda e: e.tensor_copy(out=ulast[:, fc, 0:32].rearrange("p (j r) -> p j r", j=16),
                                                            in_=ubs[:, :, 8:10]), [ubs], [ulast])
                        yield
                        S.op("act", lambda e: e.activation(out=SL[:, :128], in_=CV[:, :128], func=AF.Silu), [CV], [SL])
                        S.op("dve", lambda e: e.scalar_tensor_tensor(out=hbuf[:, fc, l0:l0 + 128], in0=Gp[:, :128],
                                                                     scalar=P_(1), in1=SL[:, :128], op0=ALU.add,
                                                                     op1=ALU.mult), [Gp, prm, SL], [hbuf])

                def fin(g):
                    for _ in g:
                        pass

                for fc in range(NFC):
                    cur = wslot[0]
                    if fc + 1 < NFC:
                        load_wub(fc + 1, cur ^ 1)
                    W = wub[cur]
                    pn = 0
                    for (c0, c1, kind) in hf["groups"]:
                        s2 = gi[0] & 1
                        gi[0] += 1
                        g = grp(fc, c0, c1, kind, W, s2, pn)
                        next(g)
                        if pend[0] is not None:
                            fin(pend[0])
                        pend[0] = g
                        pn = c1 - c0
                    wslot[0] ^= 1

                if pend[0] is not None:
                    fin(pend[0])
                    pend[0] = None
                tiles = hf["tiles"]

                def load_x1(idx):
                    c = tiles[idx]
                    S.dma("sp", x1l[idx & 1][:, :], x1h_d[c * 128:(c + 1) * 128, :], reads=[x1h_d], writes=[x1l[idx & 1]],
                          sem_tile=x1l[idx & 1])

                load_x1(0)
                for idx, c in enumerate(tiles):
                    if idx + 1 < len(tiles):
                        load_x1(idx + 1)
                    col0 = 2064 if c == 17 else 16 + (c - 1) * 128
                    l0 = col0 - lo
                    X, Z = x1l[idx & 1], zb[idx & 1]
                    S.op("pool", lambda e: e.tensor_tensor(out=X[:, :], in0=X[:, :], in1=ln1g_bc[:, :], op=ALU.mult),
                         [X, ln1g_bc], [X])
                    S.op("pool", lambda e: e.tensor_tensor(out=X[:, :], in0=X[:, :], in1=ln1b_bc[:, :], op=ALU.add),
                         [X, ln1b_bc], [X])
                    for hh in range(2):
                        p = pc[hh]
                        for fc in range(NFC):
                            mm(p[:, :], hbuf[:, fc, l0:l0 + 128], w_dn_t[:, fc, hh * 512:(hh + 1) * 512], fc == 0, False,
                               [hbuf, w_dn_t], [p])
                        mm(p[:, :], onesB[0:2, :128], bdn_hl[0:2, hh * 512:(hh + 1) * 512], False, True, [onesB, bdn_hl], [p])
                        S.op("dve", lambda e, p=p, hh=hh: e.scalar_tensor_tensor(out=Z[:, hh * 512:(hh + 1) * 512],
                                                                                in0=X[:, hh * 512:(hh + 1) * 512],
                                                                                scalar=float(ALPHA), in1=p[:, :], op0=ALU.mult,
                                                                                op1=ALU.add), [X, p], [Z])
                    layer_norm_rows(Z, 128, Z, ln2g_bc, ln2b_bc, lnw)
                    dst = y_s if c == 17 else y_p[(c - 1) * 128:c * 128, :]
                    S.dma("act", dst, Z[:, :], reads=[Z], sem_tile=Z, final=True)

            if nhalves == 2:
                for f0 in range(0, NFC, 4):
                    nf = min(4, NFC - f0)
                    for fc in range(f0, f0 + nf):
                        S.op("pe", lambda e, fc=fc, f0=f0: e.transpose(pm[0:34, (fc - f0) * 128:(fc - f0 + 1) * 128],
                                                                      ulast[:, fc, :], ident[:, :]), [ulast, ident], [pm])
                    S.op("dve", lambda e, f0=f0, nf=nf: e.tensor_copy(out=cvbuf[0:34, f0 * 128:(f0 + nf) * 128],
                                                                     in_=pm[0:34, 0:nf * 128]), [pm], [cvbuf])
                S.dma("act", cv_so, cvbuf[0:32, :], reads=[cvbuf], sem_tile=cvbuf, final=True)
                S.dma("act", cv_po, cvbuf[32:34, :], reads=[cvbuf], sem_tile=cvbuf, final=True)
            S.finish()

        S.finish()
    return nc


_CACHE = {}


def _consts():
    t = np.arange(128)
    same = (t[:, None] // 8) == (t[None, :] // 8)
    triP = (t[:, None] <= t[None, :]).astype(np.float32)
    triS = (triP.astype(bool) & same).astype(np.float32)
    negBlk = np.where(same, 0.0, NEG).astype(np.float32)
    blkones = same.astype(np.float32)
    blkind = (t[:, None] // 8 == np.arange(16)[None, :]).astype(np.float32)
    sel0 = (t[:, None] == 8 * np.arange(16)[None, :]).astype(np.float32)
    return {"c_ident": np.eye(128, dtype=np.float32), "c_triP": triP, "c_triS": triS, "c_negBlk": negBlk,
            "c_blkones": blkones, "c_blkind": blkind, "c_blkindT": np.ascontiguousarray(blkind.T), "c_sel0": sel0}


def kernel(x_prompt, x_sample, state_mlstm_C, state_mlstm_n, state_mlstm_m, state_hgrn_S, state_ffn_conv,
           meta_tokens, ln_emb_g, ln_emb_b, w_in, b_in, b_fgate_a, g_norm_a, g_norm_b, hgrn_lb_logits,
           w_out, b_out, ln1_g, ln1_b, w_up, b_up, w_conv, b_conv, w_down, b_down, ln2_g, ln2_b):
    f = lambda a: np.ascontiguousarray(np.asarray(a, dtype=np.float32))
    if "nc" not in _CACHE:
        _CACHE["nc"] = build_program()
    nc = _CACHE["nc"]
    shared = {
        "meta": f(meta_tokens), "ln_emb_g": f(ln_emb_g), "ln_emb_b": f(ln_emb_b),
        "w_in": f(w_in[0]), "b_in": f(b_in[0]).reshape(1, DIN), "b_fg": f(b_fgate_a[0]),
        "g_a": f(g_norm_a[0]).reshape(512), "g_b": f(g_norm_b[0]).reshape(512), "lb_log": f(hgrn_lb_logits),
        "w_out": f(w_out[0]), "b_out": f(b_out[0]).reshape(1, D), "ln1_g": f(ln1_g[0]), "ln1_b": f(ln1_b[0]),
        "w_up": f(w_up[0]), "b_up": f(b_up[0]), "w_conv": f(w_conv[0]), "b_conv": f(b_conv[0]),
        "w_down": f(w_down[0]), "b_down": f(b_down[0]).reshape(1, D), "ln2_g": f(ln2_g[0]), "ln2_b": f(ln2_b[0]),
    }
    shared.update(_consts())
    in_maps = []
    for i in range(8):
        sl = slice(16 * i, 16 * i + 16)
        m = dict(shared)
        m["x_p"] = f(x_prompt[i])
        m["x_s"] = f(x_sample[sl]).reshape(128, D)
        m["C_s"] = f(np.asarray(state_mlstm_C[0, sl]).transpose(0, 2, 1, 3))
        m["n_s"] = f(np.asarray(state_mlstm_n[0, sl]).transpose(0, 2, 1))
        m["m_s"] = f(state_mlstm_m[0, sl]); m["S_s"] = f(np.asarray(state_hgrn_S[0, sl]).transpose(0, 2, 1, 3))
        m["cv_s"] = f(state_ffn_conv[0, sl]).reshape(32, DFF)
        in_maps.append(m)
    res = run_bass_kernel_spmd(nc, in_maps, core_ids=list(range(8)))
    R = res.results
    cat = lambda k: np.stack([np.asarray(r[k], dtype=np.float32) for r in R])
    y_prompt = cat("y_p")
    y_sample = cat("y_s").reshape(128, 8, D)
    C_p = cat("C_po")[None]; n_p = cat("n_po")[None]; m_p = cat("m_po").reshape(1, 8, 4)
    S_p = cat("S_po")[None]; cv_p = cat("cv_po")[None]
    C_so = np.ascontiguousarray(cat("C_so").transpose(0, 1, 3, 2, 4)).reshape(1, 128, 4, 128, 128)
    n_so = np.ascontiguousarray(cat("n_so").transpose(0, 1, 3, 2)).reshape(1, 128, 4, 128)
    m_so = cat("m_so").reshape(1, 128, 4)
    S_so = np.ascontiguousarray(cat("S_so").transpose(0, 1, 3, 2, 4)).reshape(1, 128, 4, 128, 128)
    cv_so = cat("cv_so").reshape(1, 128, 2, DFF)
    return (y_prompt, y_sample, C_p, n_p, m_p, S_p, cv_p, C_so, n_so, m_so, S_so, cv_so)
```

```python
import os
import threading
from contextlib import ExitStack

import numpy as np
import concourse.bass as bass
import concourse.mybir as mybir
from concourse.bass_utils import run_bass_kernel_spmd

F32 = mybir.dt.float32
BF16 = mybir.dt.bfloat16
AF = mybir.ActivationFunctionType
ALU = mybir.AluOpType
AX = mybir.AxisListType

D = 1024
DIN = 4104
DFF = 2816
NFC = DFF // 128
NCOL = 2192
ALPHA = 2.0 ** 0.25
LN_EPS = 1e-5
RMS_EPS = 1e-6
NEG = -1.0e30
NT = 18
STAGE = int(os.environ.get("KSTAGE", "99"))
LEAD = (0, 0)
SQ = (1, 1)
QUANTA = (6, 4)


class _St:
    __slots__ = ("w", "r", "dsem", "dcnt")

    def __init__(self):
        self.w = None
        self.r = {}
        self.dsem = {}
        self.dcnt = {}


class Tk:
    def __init__(self, h, name, st=None):
        self.h = h
        self.name = name
        self._s = st or _St()

    def view(self, fn):
        return Tk(fn(self.h), self.name + "_v", self._s)

    def __getitem__(self, k):
        return self.h[k]

    w = property(lambda self: self._s.w, lambda self, v: setattr(self._s, "w", v))
    r = property(lambda self: self._s.r, lambda self, v: setattr(self._s, "r", v))


class Sched:
    def __init__(self, nc, st):
        self.nc = nc
        self.st = st
        self.eng = {"pe": nc.tensor, "act": nc.scalar, "dve": nc.vector, "pool": nc.gpsimd, "sp": nc.sync}
        self.sem = {k: st.enter_context(nc.semaphore("s_" + k)) for k in self.eng}
        self.cnt = {k: 0 for k in self.eng}
        self.seen = {k: {} for k in self.eng}
        self.out_events = {}
        self.all_dma = {}
        self.nd = 0
        self.cur = None
        self.atomic = 0

    def sb(self, name, shape, dt=F32, st=None):
        h = (st or self.st).enter_context(self.nc.sbuf_tensor(name, list(shape), dt))
        return Tk(h, name)

    def ps(self, name, shape, dt=F32, st=None):
        h = (st or self.st).enter_context(self.nc.psum_tensor(name, list(shape), dt))
        return Tk(h, name)

    def dram(self, name, shape, dt=F32):
        h = self.nc.dram_tensor(name, list(shape), dt, kind="Internal")
        return Tk(h.ap(), name)

    def _wait(self, e, deps):
        for key, sem, val in deps:
            if self.seen[e].get(key, 0) >= val:
                continue
            self.eng[e].wait_ge(sem, val)
            self.seen[e][key] = val

    def _deps(self, e, reads, writes):
        deps = []
        for t in reads:
            if t.w is not None:
                deps.append(t.w)
        for t in writes:
            if t.w is not None and not (e == "pe" and t.w[0] == "pe"):
                deps.append(t.w)
            for k, (sem, val) in t.r.items():
                deps.append((k, sem, val))
        return deps

    def op(self, e, fn, reads=(), writes=()):
        self._wait(e, self._deps(e, reads, writes))
        ins = fn(self.eng[e])
        self.cnt[e] += 1
        ins.then_inc(self.sem[e], 1)
        ev = (e, self.sem[e], self.cnt[e])
        for t in writes:
            t.w = ev
            t.r = {}
        for t in reads:
            t.r[e] = (self.sem[e], self.cnt[e])
        self._sw()

    def dma(self, q, out, in_, reads=(), writes=(), sem_tile=None, final=False, **kw):
        self._wait(q, self._deps(q, reads, writes))
        kind = "sw" if q == "pool" else "hw"
        stt = sem_tile._s
        if kind not in stt.dsem:
            self.nd += 1
            stt.dsem[kind] = self.st.enter_context(self.nc.semaphore("d%d" % self.nd))
            stt.dcnt[kind] = 0
        ins = self.eng[q].dma_start(out=out, in_=in_, **kw)
        stt.dcnt[kind] += 1
        ins.then_inc(stt.dsem[kind], 16)
        key = ("d", id(stt), kind)
        ev = (key, stt.dsem[kind], 16 * stt.dcnt[kind])
        for t in writes:
            t.w = ev
            t.r = {}
        for t in reads:
            t.r[key] = (ev[1], ev[2])
        self.all_dma[key] = ev
        if final:
            self.out_events[key] = ev
        self._sw()

    def _sw(self):
        st = self.cur
        if st is not None:
            st.n += 1
            if self.atomic == 0 and st.n >= st.quantum:
                st.n = 0
                st.back.release()
                st.go.acquire()

    def run_streams(self, fns, quanta=None):
        class _Stream:
            pass
        streams = []
        for i, fn in enumerate(fns):
            st = _Stream()
            st.n = -(LEAD[i] if (quanta and i < len(LEAD)) else 0)
            st.quantum = quanta[i] if quanta else 1
            st.go = threading.Semaphore(0); st.back = threading.Semaphore(0); st.done = False; st.exc = None

            def body(st=st, fn=fn):
                st.go.acquire()
                try:
                    fn()
                except BaseException as e:
                    st.exc = e
                st.done = True
                st.back.release()
            st.th = threading.Thread(target=body)
            st.th.start()
            streams.append(st)
        live = list(streams)
        while live:
            for st in list(live):
                self.cur = st
                st.go.release()
                st.back.acquire()
                self.cur = None
                if st.done:
                    st.th.join()
                    live.remove(st)
                    if st.exc is not None:
                        for o in live:
                            o.done = True
                        raise st.exc
        self.cur = None

    def barrier(self):
        deps = list(self.all_dma.values())
        for e in ("pe", "act", "dve", "pool"):
            if self.cnt[e] > 0:
                deps.append((e, self.sem[e], self.cnt[e]))
        for e in self.eng:
            self._wait(e, [d for d in deps if d[0] != e])

    def finish(self):
        deps = list(self.out_events.values())
        for e in ("pe", "act", "dve", "pool"):
            if self.cnt[e] > 0:
                deps.append((e, self.sem[e], self.cnt[e]))
        self._wait("sp", deps)


def bc(ap, shape):
    return ap.to_broadcast(list(shape))


def build_program():
    nc = bass.Bass("TRN2", target_bir_lowering=False)

    def di(name, shape, dt=F32):
        return nc.dram_tensor(name, list(shape), dt, kind="ExternalInput").ap()

    def do(name, shape):
        return nc.dram_tensor(name, list(shape), F32, kind="ExternalOutput").ap()

    x_p = di("x_p", [2048, D]); x_s = di("x_s", [128, D]); meta = di("meta", [16, D])
    C_s = di("C_s", [16, 128, 4, 128]); n_s = di("n_s", [16, 128, 4]); m_s = di("m_s", [16, 4])
    S_s = di("S_s", [16, 128, 4, 128]); cv_s = di("cv_s", [32, DFF])
    ln_emb_g = di("ln_emb_g", [D]); ln_emb_b = di("ln_emb_b", [D])
    w_in = di("w_in", [D, DIN]); b_in = di("b_in", [1, DIN]); b_fg = di("b_fg", [4])
    g_a = di("g_a", [512]); g_b = di("g_b", [512]); lb_log = di("lb_log", [2, 512])
    w_out = di("w_out", [D, D]); b_out = di("b_out", [1, D])
    ln1_g = di("ln1_g", [D]); ln1_b = di("ln1_b", [D])
    w_up = di("w_up", [D, 2 * DFF]); b_up = di("b_up", [2 * DFF])
    w_conv = di("w_conv", [3, DFF]); b_conv = di("b_conv", [DFF])
    w_down = di("w_down", [DFF, D]); b_down = di("b_down", [1, D])
    ln2_g = di("ln2_g", [D]); ln2_b = di("ln2_b", [D])
    c_ident = di("c_ident", [128, 128]); c_triP = di("c_triP", [128, 128]); c_triS = di("c_triS", [128, 128])
    c_negBlk = di("c_negBlk", [128, 128]); c_blkones = di("c_blkones", [128, 128])
    c_blkind = di("c_blkind", [128, 16]); c_blkindT = di("c_blkindT", [16, 128]); c_sel0 = di("c_sel0", [128, 16])

    y_p = do("y_p", [2048, D]); y_s = do("y_s", [128, D])
    C_po = do("C_po", [4, 128, 128]); n_po = do("n_po", [4, 128]); m_po = do("m_po", [1, 4])
    S_po = do("S_po", [4, 128, 128]); cv_po = do("cv_po", [2, DFF])
    C_so = do("C_so", [16, 128, 4, 128]); n_so = do("n_so", [16, 128, 4]); m_so = do("m_so", [16, 4])
    S_so = do("S_so", [16, 128, 4, 128]); cv_so = do("cv_so", [32, DFF])

    with ExitStack() as st:
        S = Sched(nc, st)
        x1h_d = S.dram("x1h_d", [NT * 128, D], F32)
        x1T_d = S.dram("x1T_d", [128, 8, NCOL], BF16)

        ident = S.sb("ident", [128, 128]); identB = S.sb("identB", [128, 128], BF16)
        ones = S.sb("ones", [128, 128]); onesB = S.sb("onesB", [128, 128], BF16)
        S.dma("sp", ident[:, :], c_ident, writes=[ident], sem_tile=ident)
        S.op("dve", lambda e: e.tensor_copy(out=identB[:, :], in_=ident[:, :]), [ident], [identB])
        S.op("dve", lambda e: e.memset(ones[:, :], 1.0), [], [ones])
        S.op("dve", lambda e: e.memset(onesB[:, :], 1.0), [], [onesB])

        spa = ExitStack()
        psT = S.ps("psT", [128, 8, 128], BF16, st=spa)
        psP = [S.ps("psP0", [128, 512], st=spa), S.ps("psP1", [128, 512], st=spa)]
        psS = S.ps("psS", [128, 4, 128], st=spa)
        psO = S.ps("psO", [128, 2, 512], st=spa)
        psC = S.ps("psC", [128, 2, 512], st=spa)
        pp_i = [0]

        def pp():
            pp_i[0] ^= 1
            return psP[pp_i[0]]

        def mm(out, lhsT, rhs, start, stop, reads, writes, skip=False):
            kw = {"skip_group_check": True} if skip else {}
            S.op("pe", lambda e: e.matmul(out, lhsT, rhs, start=start, stop=stop, **kw), reads, writes)

        def bias_rows(hl, src, n, f, h32, tb):
            for c0 in range(0, n, 1024):
                w = min(1024, n - c0)
                S.dma("sp", f[0:1, :w], src[:, c0:c0 + w], writes=[f], sem_tile=f)
                S.op("dve", lambda e: e.tensor_copy(out=hl[0:1, c0:c0 + w], in_=f[0:1, :w]), [f], [hl])
                S.op("dve", lambda e: e.tensor_copy(out=h32[0:1, :w], in_=hl[0:1, c0:c0 + w]), [hl], [h32])
                S.op("dve", lambda e: e.tensor_tensor(out=tb[0:1, :w], in0=f[0:1, :w], in1=h32[0:1, :w],
                                                      op=ALU.subtract), [f, h32], [tb])
                S.dma("sp", hl[1:2, c0:c0 + w], tb[0:1, :w], reads=[tb], writes=[hl], sem_tile=hl)

        def layer_norm_rows(x, T, xo, g_bc, b_bc, wk, xo_bf=None):
            stt, mv, rs = wk
            for cch in range(2):
                S.op("dve", lambda e, cch=cch: e.bn_stats(out=stt[:T, cch * 6:cch * 6 + 6], in_=x[:T, cch * 512:(cch + 1) * 512]),
                     [x], [stt])
            S.op("dve", lambda e: e.bn_aggr(out=mv[:T, :], in_=stt[:T, :]), [stt], [mv])
            S.op("act", lambda e: e.activation(out=rs[:T, :], in_=mv[:T, 1:2], func=AF.Ln, bias=eps_ln[:T, :], scale=1.0),
                 [mv, eps_ln], [rs])
            S.op("act", lambda e: e.activation(out=rs[:T, :], in_=rs[:T, :], func=AF.Exp, scale=-0.5), [rs], [rs])
            if g_bc is None:
                if xo_bf is not None:
                    S.op("dve", lambda e: e.tensor_scalar(out=xo_bf[:T, :], in0=x[:T, :], scalar1=mv[:T, 0:1], scalar2=rs[:T, 0:1],
                                                          op0=ALU.subtract, op1=ALU.mult), [x, mv, rs], [xo_bf])
                S.op("dve", lambda e: e.tensor_scalar(out=xo[:T, :], in0=x[:T, :], scalar1=mv[:T, 0:1], scalar2=rs[:T, 0:1],
                                                      op0=ALU.subtract, op1=ALU.mult), [x, mv, rs], [xo])
                return
            S.op("dve", lambda e: e.tensor_scalar(out=xo[:T, :], in0=x[:T, :], scalar1=mv[:T, 0:1], scalar2=rs[:T, 0:1],
                                                  op0=ALU.subtract, op1=ALU.mult), [x, mv, rs], [xo])
            S.op("dve", lambda e: e.tensor_tensor(out=xo[:T, :], in0=xo[:T, :], in1=g_bc[:T, :], op=ALU.mult),
                 [xo, g_bc], [xo])
            if xo_bf is not None:
                S.op("dve", lambda e: e.tensor_tensor(out=xo_bf[:T, :], in0=xo[:T, :], in1=b_bc[:T, :], op=ALU.add),
                     [xo, b_bc], [xo_bf])
                S.op("pool", lambda e: e.tensor_tensor(out=xo[:T, :], in0=xo[:T, :], in1=b_bc[:T, :], op=ALU.add),
                     [xo, b_bc], [xo])
            else:
                S.op("dve", lambda e: e.tensor_tensor(out=xo[:T, :], in0=xo[:T, :], in1=b_bc[:T, :], op=ALU.add),
                     [xo, b_bc], [xo])

        eps_ln = S.sb("eps_ln", [128, 1]); eps_rms = S.sb("eps_rms", [128, 1])
        S.op("dve", lambda e: e.memset(eps_ln[:, :], LN_EPS), [], [eps_ln])
        S.op("dve", lambda e: e.memset(eps_rms[:, :], RMS_EPS), [], [eps_rms])
        lnw = (S.sb("ln_st", [128, 12]), S.sb("ln_mv", [128, 2]), S.sb("ln_rs", [128, 1]))

        with ExitStack() as sa:
            def A(name, shape, dt=F32):
                return S.sb(name, shape, dt, st=sa)

            def sigmoid_act(dst, src_ap, T, reads):
                S.op("act", lambda e: e.activation(out=dst[:T, :], in_=src_ap, func=AF.Exp, scale=-1.0), reads, [dst])
                S.op("act", lambda e: e.activation(out=dst[:T, :], in_=dst[:T, :], func=AF.Ln, bias=ones[:T, 0:1], scale=1.0),
                     [dst, ones], [dst])
                S.op("act", lambda e: e.activation(out=dst[:T, :], in_=dst[:T, :], func=AF.Exp, scale=-1.0), [dst], [dst])


            triP = A("triP", [128, 128]); triS = A("triS", [128, 128]); negBlk = A("negBlk", [128, 128])
            blkones = A("blkones", [128, 128]); blkind = A("blkind", [128, 16]); blkindT = A("blkindT", [16, 128])
            sel0 = A("sel0", [128, 16])
            GRP = {"qa": (0, 512), "ka": (512, 1024), "va": (1024, 1536), "oa": (1536, 2048), "g": (2048, 2056),
                   "qb": (2056, 2568), "fb": (2568, 3080), "ib": (3080, 3592), "gb": (3592, 4104)}
            w_in_v = w_in.rearrange("(kc p) n -> p kc n", p=128)
            w_in_g = {}
            for nm, lo_, hi_, keys in (("B", 2048, 3080, ("g", "qb", "fb")), ("A", 0, 2048, ("qa", "ka", "va", "oa")),
                                       ("C", 3080, 4104, ("ib", "gb"))):
                slab = A("w_in_" + nm, [128, 8, hi_ - lo_], BF16)
                S.dma("pool", slab[:, :, :], w_in_v[:, :, lo_:hi_], writes=[slab], sem_tile=slab)
                for key in keys:
                    c0_, c1_ = GRP[key]
                    w_in_g[key] = slab.view(lambda h, a=c0_ - lo_, b=c1_ - lo_: h[:, :, a:b])
            w_out_t = A("w_out_t", [128, 8, D], BF16)
            S.dma("pool", w_out_t[:, :, :], w_out.rearrange("(kc p) n -> p kc n", p=128), writes=[w_out_t],
                  sem_tile=w_out_t)
            bin_hl = A("bin_hl", [2, DIN], BF16)
            bout_hl = A("bout_hl", [2, D], BF16)

            embg = A("embg", [128, D]); embb = A("embb", [128, D])
            ga_bc = A("ga_bc", [128, 512]); gb_bc = A("gb_bc", [128, 512])
            lb_bc = A("lb_bc", [128, 512]); oml_bc = A("oml_bc", [128, 512]); bfg_bc = A("bfg_bc", [128, 4])
            ln1g_col = A("ln1g_col", [128, 8]); ln1b_col = A("ln1b_col", [128, 8])

            xin = A("xin", [128, D]); xnb = A("xnb", [128, D], BF16); xnT = A("xnT", [128, 8, 128], BF16)
            gates = A("gates", [128, 8]); g1s = A("g1s", [128, 12]); diag = A("diag", [128, 4, 128])
            qa_bf = A("qa_bf", [128, 512], BF16); qt_bf = A("qt_bf", [128, 512], BF16)
            t1 = A("t1", [128, 512]); Fb = A("Fb", [128, 512]); Eb = A("Eb", [128, 512]); Qb = A("Qb", [128, 512])
            nBc = A("nBc", [128, 4]); zero4 = A("zero4", [128, 4])

            def mkset(i):
                n = lambda x: "%s_%d" % (x, i)
                return dict(
                    xn=A(n("xn"), [128, D]), gq=A(n("gq"), [128, 8]), rmax=A(n("rmax"), [128, 4]), tot=A(n("tot"), [128, 4]),
                    va=A(n("va"), [128, 512]), ka_bf=A(n("ka_bf"), [128, 512], BF16), qkT=A(n("qkT"), [128, 8, 128], BF16),
                    SmT=A(n("SmT"), [128, 4, 128], BF16), og=A(n("og"), [128, 512]), kt_bf=A(n("kt_bf"), [128, 512], BF16),
                    iv_bf=A(n("iv_bf"), [128, 512], BF16), qkT2=A(n("qkT2"), [128, 8, 128], BF16),
                    ScT=A(n("ScT"), [128, 4, 128], BF16), gg=A(n("gg"), [128, 512]), eaT=A(n("eaT"), [128, 4, 16]))
            sets = [mkset(0), mkset(1)]
            Mc = A("Mc", [128, 4]); rr = A("rr", [128, 4]); alpha = A("alpha", [128, 4]); g2s = A("g2s", [128, 28])
            Vaug = A("Vaug", [128, 4, 129], BF16)
            hn = A("hn", [128, 4, 128]); mix = A("mix", [128, D], BF16); mixT = A("mixT", [128, 8, 128], BF16)
            yb = A("yb", [128, D]); x1hb = A("x1hb", [128, D], BF16); x1Tt = A("x1Tt", [128, 8, 128], BF16)
            Cst = A("Cst", [128, 4, 129]); Cp32 = A("Cp32", [128, 4, 129]); Cb = A("Cb", [128, 4, 129], BF16)
            Sst = A("Sst", [128, 4, 128]); Sb = A("Sb", [128, 4, 128], BF16); Stm = A("Stm", [128, 4, 128])
            nT_sb = A("nT_sb", [128, 64]); nld = A("nld", [64, 128]); alphaD = A("alphaD", [128, 64])
            selA = A("selA", [128, 16, 4])
            qTm = A("qTm", [128, 4, 128], BF16); Vm = A("Vm", [128, 4, 129], BF16); ivm = A("ivm", [128, 512], BF16)
            Cp32b = A("Cp32b", [128, 4, 129]); Cb2 = A("Cb2", [128, 4, 129], BF16); Sb2 = A("Sb2", [128, 4, 128], BF16)
            Cp32s = [Cp32, Cp32b]; Cbs = [Cb, Cb2]; Sbs = [Sb, Sb2]
            vq = lambda h: h[:, 0:512].rearrange("p (a b) -> p a b", a=4)
            qTmA = [qTm, qa_bf.view(vq)]
            qTmB = [qt_bf.view(vq), xnb.view(vq)]
            v129 = lambda h: h[:, 0:516].rearrange("p (a b) -> p a b", a=4)
            v128 = lambda h: h[:, 0:512].rearrange("p (a b) -> p a b", a=4)
            Cout1 = A("Cout1", [128, 4, 129])
            Cld = [xin.view(v128), sets[0]["xn"].view(v128), sets[0]["va"].view(v128), sets[0]["og"].view(v128)]
            nld4 = [A("nld4_%d" % i, [128, 4]) for i in range(4)]
            NCS, NSS = 4, 3
            Cout = [yb.view(v128), Cout1.view(lambda h: h[:, :, 0:128])]
            nout = [A("nout0", [128, 4]), A("nout1", [128, 4])]
            Sld = [t1.view(v128), Fb.view(v128), sets[0]["gg"].view(v128)]
            Sout = [Eb.view(v128), Qb.view(v128)]
            tmpS = xin.view(v128)
            tmp3 = Stm
            msl = A("msl", [16, 4]); mnew = A("mnew", [128, 4])
            lnw2 = (S.sb("ln_st2", [128, 12], st=sa), S.sb("ln_mv2", [128, 2], st=sa), S.sb("ln_rs2", [128, 1], st=sa))

            bias_rows(bin_hl, b_in, DIN, sets[1]["xn"], yb, x1hb)
            for t_, s_ in ((embg, ln_emb_g), (embb, ln_emb_b), (bfg_bc, b_fg)):
                S.dma("sp", t_[:, :], s_.partition_broadcast(128), writes=[t_], sem_tile=t_)
            ln_pre = [True]
            for t_, s_ in ((triP, c_triP), (blkones, c_blkones), (blkind, c_blkind)):
                S.dma("sp", t_[:, :], s_, writes=[t_], sem_tile=t_)
            lbl = yb.view(lambda h: h[:, :].rearrange("p (r c) -> p r c", r=2))
            for r in range(2):
                S.dma("sp", lbl[:, r, :], lb_log[r, :].partition_broadcast(128), writes=[lbl], sem_tile=lbl)
            S.op("dve", lambda e: e.tensor_tensor(out=lb_bc[:, :], in0=lbl[:, 0, :], in1=lbl[:, 1, :], op=ALU.subtract),
                 [lbl], [lb_bc])
            sigmoid_act(lb_bc, lb_bc[:, :], 128, [lb_bc])
            S.op("dve", lambda e: e.tensor_scalar(out=oml_bc[:, :], in0=lb_bc[:, :], scalar1=-1.0, scalar2=1.0,
                                                  op0=ALU.mult, op1=ALU.add), [lb_bc], [oml_bc])
            for t_, s_ in ((ga_bc, g_a), (gb_bc, g_b)):
                S.dma("sp", t_[:, :], s_.partition_broadcast(128), writes=[t_], sem_tile=t_)
            bias_rows(bout_hl, b_out, D, sets[1]["xn"], yb, x1hb)
            for t_, s_ in ((triS, c_triS), (negBlk, c_negBlk), (blkindT, c_blkindT), (sel0, c_sel0)):
                S.dma("sp", t_[:, :], s_, writes=[t_], sem_tile=t_)
            S.dma("sp", ln1g_col[:, :], ln1_g.rearrange("(kc p) -> p kc", p=128), writes=[ln1g_col], sem_tile=ln1g_col,
                  allow_slow_non_contiguous=True)
            S.dma("sp", ln1b_col[:, :], ln1_b.rearrange("(kc p) -> p kc", p=128), writes=[ln1b_col], sem_tile=ln1b_col,
                  allow_slow_non_contiguous=True)
            S.op("dve", lambda e: e.memset(nBc[:, :], 0.0), [], [nBc])
            S.op("dve", lambda e: e.memset(zero4[:, :], 0.0), [], [zero4])
            S.op("dve", lambda e: e.memset(Mc[:, :], 0.0), [], [Mc])
            S.op("pool", lambda e: e.memset(qTm[:, :, :], 0.0), [], [qTm])

            def proj(T, key):
                c0, c1 = GRP[key]
                p = pp()
                n = c1 - c0
                wg = w_in_g[key]
                for kc in range(8):
                    mm(p[:T, :n], xnT[:, kc, :T], wg[:, kc, :], kc == 0, False, [xnT, wg], [p])
                mm(p[:T, :n], onesB[0:2, :T], bin_hl[0:2, c0:c1], False, True, [onesB, bin_hl], [p])
                return p

            def transposes(src, T, n, dst_off, pst):
                for i in range(n):
                    S.op("pe", lambda e, i=i: e.transpose(pst[:, dst_off + i, :T], src[:T, i * 128:(i + 1) * 128],
                                                          identB[:T, :T]), [src, identB], [pst])

            def tile_info(c):
                sm = (c == 17)
                T = 16 if c == 0 else 128
                if c == 0:
                    src, col0 = meta, 0
                elif sm:
                    src, col0 = x_s, 2064
                else:
                    src, col0 = x_p[(c - 1) * 128:c * 128, :], 16 + (c - 1) * 128
                return sm, T, src, col0

            def ln_stage(c):
                sm_, T_, src_, _ = tile_info(c)
                xn_ = sets[c & 1]["xn"]
                S.dma("sp", xin[:T_, :], src_, writes=[xin], sem_tile=xin)
                layer_norm_rows(xin, T_, xn_, embg, embb, lnw, xo_bf=xnb)

            def stage1(c):
                sm, T, src, col0 = tile_info(c)
                J = 16 if sm else 1
                B = sets[c & 1]
                xn, gq, rmax, tot = B["xn"], B["gq"], B["rmax"], B["tot"]
                kt_bf, eaT = B["kt_bf"], B["eaT"]
                ka_bf, va, iv_bf, og, gg = B["ka_bf"], B["va"], B["iv_bf"], B["og"], B["gg"]
                qkT, SmT, qkT2, ScT = B["qkT"], B["SmT"], B["qkT2"], B["ScT"]
                tri = triS if sm else triP
                fab, e1, l1 = (g1s[:, 4 * i:4 * i + 4] for i in range(3))
                nBt, G = gq[:, 0:4], gq[:, 4:8]
                S.atomic += 1
                transposes(xnb, T, 8, 0, psT)
                S.atomic -= 1
                S.op("act", lambda e: e.copy(out=xnT[:, :, :T], in_=psT[:, :, :T]), [psT], [xnT])
                yield
                p = proj(T, "g")
                S.op("dve", lambda e: e.tensor_copy(out=gates[:T, :], in_=p[:T, 0:8]), [p], [gates])
                S.op("dve", lambda e: e.tensor_tensor(out=fab[:T], in0=gates[:T, 4:8], in1=bfg_bc[:T, :], op=ALU.add),
                     [gates, bfg_bc], [g1s])
                S.op("act", lambda e: e.activation(out=e1[:T], in_=fab[:T], func=AF.Exp, scale=-1.0), [g1s], [g1s])
                S.op("act", lambda e: e.activation(out=l1[:T], in_=e1[:T], func=AF.Ln, bias=ones[:T, 0:1], scale=1.0),
                     [g1s, ones], [g1s])
                yield
                p = proj(T, "fb")
                sigmoid_act(t1, p[:T, :], T, [p])
                S.op("dve", lambda e: e.tensor_tensor(out=t1[:T, :], in0=t1[:T, :], in1=oml_bc[:T, :], op=ALU.mult),
                     [t1, oml_bc], [t1])
                S.op("dve", lambda e: e.tensor_tensor(out=Fb[:T, :], in0=t1[:T, :], in1=lb_bc[:T, :], op=ALU.add),
                     [t1, lb_bc], [Fb])
                S.op("act", lambda e: e.activation(out=Fb[:T, :], in_=Fb[:T, :], func=AF.Ln), [Fb], [Fb])
                S.op("dve", lambda e: e.tensor_tensor(out=t1[:T, :], in0=oml_bc[:T, :], in1=t1[:T, :], op=ALU.subtract),
                     [t1, oml_bc], [t1])
                yield
                p3 = pp()
                mm(p3[:T, 0:4], tri[:T, :T], l1[:T], True, True, [tri, g1s], [p3])
                tot_l = blkones if sm else ones
                mm(p3[:128, 4:8], tot_l[:T, :128], l1[:T], True, True, [tot_l, g1s], [p3], skip=True)
                nBsrc = zero4 if sm else nBc
                S.op("dve", lambda e: e.tensor_tensor(out=nBt[:T], in0=p3[:T, 0:4], in1=nBsrc[:T, :], op=ALU.add),
                     [p3, nBsrc], [gq])
                S.op("dve", lambda e: e.tensor_copy(out=tot[:, :], in_=p3[:, 4:8]), [p3], [tot])
                if not sm:
                    S.op("dve", lambda e: e.tensor_tensor(out=nBc[:, :], in0=nBc[:, :], in1=tot[:, :], op=ALU.add),
                         [nBc, tot], [nBc])
                S.op("dve", lambda e: e.tensor_tensor(out=G[:T], in0=gates[:T, 0:4], in1=nBt[:T], op=ALU.add),
                     [gates, gq], [gq])
                S.op("dve", lambda e: e.tensor_tensor(out=diag[:T, :, :T], in0=bc(ident[:T, :T].unsqueeze(1), [T, 4, T]),
                                                      in1=bc(G[:T].unsqueeze(2), [T, 4, T]), op=ALU.mult),
                     [ident, gq], [diag])
                yield
                pq = proj(T, "qb")
                sigmoid_act(Qb, pq[:T, :], T, [pq])
                S.op("dve", lambda e: e.tensor_tensor(out=Qb[:T, :], in0=pq[:T, :], in1=Qb[:T, :], op=ALU.mult),
                     [pq, Qb], [Qb])
                yield
                pa = pp()
                mm(pa[:T, :], tri[:T, :T], Fb[:T, :], True, True, [tri, Fb], [pa])
                for h in range(4):
                    mm(psS[:, h, :T], ones[:T, :128], diag[:T, h, :T], True, True, [ones, diag], [psS])
                if sm:
                    S.op("dve", lambda e: e.tensor_tensor(out=tmpS[:, :, :], in0=psS[:, :, :],
                                                          in1=bc(negBlk[:, :].unsqueeze(1), [128, 4, 128]), op=ALU.add),
                         [psS, negBlk], [tmpS])
                    S.op("dve", lambda e: e.tensor_reduce(out=rmax[:, :], in_=tmpS[:, :, :], axis=AX.X, op=ALU.max),
                         [tmpS], [rmax])
                else:
                    S.op("dve", lambda e: e.tensor_reduce(out=rmax[:, :], in_=psS[:, :, :T], axis=AX.X, op=ALU.max),
                         [psS], [rmax])
                S.op("act", lambda e: e.activation(out=Eb[:T, :], in_=pa[:T, :], func=AF.Exp), [pa], [Eb])
                S.op("dve", lambda e: e.tensor_tensor(out=qt_bf[:T, :], in0=Qb[:T, :], in1=Eb[:T, :], op=ALU.mult),
                     [Qb, Eb], [qt_bf])
                S.op("act", lambda e: e.activation(out=Eb[:T, :], in_=pa[:T, :], func=AF.Exp, scale=-1.0), [pa], [Eb])
                S.op("dve", lambda e: e.tensor_tensor(out=kt_bf[:T, :], in0=t1[:T, :], in1=Eb[:T, :], op=ALU.mult),
                     [t1, Eb], [kt_bf])
                yield
                p = proj(T, "ka")
                S.op("act", lambda e: e.activation(out=ka_bf[:T, :], in_=p[:T, :], func=AF.Identity, scale=float(128 ** -0.5)),
                     [p], [ka_bf])
                yield
                p = proj(T, "qa")
                S.op("dve", lambda e: e.tensor_copy(out=qa_bf[:T, :], in_=p[:T, :]), [p], [qa_bf])
                yield
                pe_ = pp()
                rsel = blkind if sm else ones
                for h in range(4):
                    mm(pe_[:, h * J:(h + 1) * J], Fb[:T, h * 128:(h + 1) * 128], rsel[:T, 0:J], True, True,
                       [Fb, rsel], [pe_], skip=True)
                S.op("act", lambda e: e.activation(out=eaT[:, :, 0:J], in_=pe_[:, 0:4 * J].rearrange("p (h j) -> p h j", h=4),
                                                   func=AF.Exp), [pe_], [eaT])
                yield
                p = proj(T, "va")
                S.op("act", lambda e: e.copy(out=va[:T, :], in_=p[:T, :]), [p], [va])
                yield
                S.atomic += 1
                transposes(qa_bf, T, 4, 0, psT)
                transposes(ka_bf, T, 4, 4, psT)
                S.atomic -= 1
                S.op("act", lambda e: e.copy(out=qkT[:, :, :T], in_=psT[:, :, :T]), [psT], [qkT])
                yield
                p = proj(T, "ib")
                S.op("act", lambda e: e.copy(out=iv_bf[:T, :], in_=p[:T, :]), [p], [iv_bf])
                for h in range(4):
                    mm(psS[:T, h, :T], qkT[:, 4 + h, :T], qkT[:, h, :T], True, True, [qkT], [psS])
                S.op("dve", lambda e: e.tensor_tensor(out=SmT[:T, :, :T], in0=psS[:T, :, :T],
                                                      in1=bc(tri[:T, :T].unsqueeze(1), [T, 4, T]), op=ALU.mult),
                     [psS, tri], [SmT])
                yield
                S.atomic += 1
                transposes(qt_bf, T, 4, 0, psT)
                transposes(kt_bf, T, 4, 4, psT)
                S.atomic -= 1
                S.op("act", lambda e: e.copy(out=qkT2[:, :, :T], in_=psT[:, :, :T]), [psT], [qkT2])
                yield
                p = proj(T, "oa")
                sigmoid_act(og, p[:T, :], T, [p])
                S.op("pool", lambda e: e.tensor_tensor(out=og[:T, :], in0=og[:T, :], in1=ga_bc[:T, :], op=ALU.mult),
                     [og, ga_bc], [og])
                for h in range(4):
                    mm(psS[:T, h, :T], qkT2[:, 4 + h, :T], qkT2[:, h, :T], True, True, [qkT2], [psS])
                S.op("dve", lambda e: e.tensor_tensor(out=ScT[:T, :, :T], in0=psS[:T, :, :T],
                                                      in1=bc(tri[:T, :T].unsqueeze(1), [T, 4, T]), op=ALU.mult),
                     [psS, tri], [ScT])
                yield
                p = proj(T, "gb")
                sigmoid_act(gg, p[:T, :], T, [p])
                S.op("pool", lambda e: e.tensor_tensor(out=gg[:T, :], in0=gg[:T, :], in1=gb_bc[:T, :], op=ALU.mult),
                     [gg, gb_bc], [gg])
                yield
                if c + 1 < ntiles:
                    ln_stage(c + 1)

            def stage2(c):
                sm, T, src, col0 = tile_info(c)
                J = 16 if sm else 1
                has_state = (c > 0)
                B = sets[c & 1]
                xn, gq, rmax, tot = B["xn"], B["gq"], B["rmax"], B["tot"]
                ka_bf, va, iv_bf, og, gg = B["ka_bf"], B["va"], B["iv_bf"], B["og"], B["gg"]
                qkT, SmT, qkT2, ScT, kt_bf, eaT = B["qkT"], B["SmT"], B["qkT2"], B["ScT"], B["kt_bf"], B["eaT"]
                nBt, G = gq[:, 0:4], gq[:, 4:8]
                d1, u, thr, den, rec, ssq, rstd = (g2s[:, 4 * i:4 * i + 4] for i in range(7))
                psC0 = psC.view(lambda h: h[:, 0, :])
                psO0 = psO.view(lambda h: h[:, 0, :])
                def prescale():
                    S.op("dve", lambda e: e.tensor_scalar(out=yb[:T, :], in0=xn[:T, :], scalar1=float(ALPHA), scalar2=None,
                                                          op0=ALU.mult), [xn], [yb])

                def x1_tail(cp):
                    _, Tp, _, colp = tile_info(cp)
                    S.atomic += 1
                    transposes(x1hb, Tp, 8, 0, psT)
                    for kc in range(8):
                        if kc == 7:
                            S.atomic -= 1
                        S.op("act", lambda e, kc=kc: e.activation(out=x1Tt[:, kc, :Tp], in_=psT[:, kc, :Tp], func=AF.Identity,
                                                                  bias=ln1b_col[:, kc:kc + 1], scale=ln1g_col[:, kc:kc + 1]),
                             [psT, ln1b_col, ln1g_col], [x1Tt])
                    S.dma("pool", x1T_d[:, :, colp:colp + Tp], x1Tt[:, :, :Tp], reads=[x1Tt], writes=[x1T_d], sem_tile=x1Tt)

                if not sm:
                    prescale()
                if c > 0:
                    x1_tail(c - 1)
                if sm:
                    S.dma("sp", msl[:, :], m_s, writes=[msl], sem_tile=msl)
                    p2 = psC0
                    mm(p2[:128, 0:4], blkindT[:16, :128], msl[:16, :4], True, True, [blkindT, msl], [p2])
                    S.op("dve", lambda e: e.tensor_copy(out=Mc[:, :], in_=p2[:, 0:4]), [p2], [Mc])
                S.op("dve", lambda e: e.tensor_tensor(out=rr[:, :], in0=rmax[:, :], in1=Mc[:, :], op=ALU.max), [rmax, Mc], [rr])
                S.op("dve", lambda e: e.tensor_tensor(out=d1[:T], in0=G[:T], in1=rr[:T, :], op=ALU.subtract), [gq, rr], [g2s])
                S.op("act", lambda e: e.activation(out=u[:T], in_=d1[:T], func=AF.Exp), [g2s], [g2s])
                S.op("dve", lambda e: e.tensor_tensor(out=d1[:T], in0=nBt[:T], in1=rr[:T, :], op=ALU.subtract), [gq, rr], [g2s])
                S.op("act", lambda e: e.activation(out=thr[:T], in_=d1[:T], func=AF.Exp), [g2s], [g2s])
                S.op("dve", lambda e: e.tensor_tensor(out=alpha[:, :], in0=Mc[:, :], in1=rr[:, :], op=ALU.subtract),
                     [Mc, rr], [alpha])
                S.op("act", lambda e: e.activation(out=alpha[:, :], in_=alpha[:, :], func=AF.Exp), [alpha], [alpha])
                if sm:
                    S.op("dve", lambda e: e.tensor_tensor(out=mnew[:, :], in0=rr[:, :], in1=tot[:, :], op=ALU.subtract),
                         [rr, tot], [mnew])
                    S.dma("pool", m_so, mnew[0:128:8, :], reads=[mnew], sem_tile=mnew, final=True)
                    S.op("dve", lambda e: e.tensor_tensor(out=selA[:, :, :], in0=bc(sel0[:, :].unsqueeze(2), [128, 16, 4]),
                                                          in1=bc(alpha[:, :].unsqueeze(1), [128, 16, 4]), op=ALU.mult),
                         [sel0, alpha], [selA])
                    p4 = psC0
                    mm(p4[:, 0:64], ones[:, :], selA[:, :, :], True, True, [ones, selA], [p4])
                    S.op("dve", lambda e: e.tensor_copy(out=alphaD[:, :], in_=p4[:, 0:64]), [p4], [alphaD])
                else:
                    S.op("dve", lambda e: e.tensor_copy(out=Mc[:, :], in_=rr[:, :]), [rr], [Mc])
                yield
                S.op("dve", lambda e: e.tensor_tensor(out=Vaug[:T, :, 0:128], in0=va[:T, :].rearrange("p (h e) -> p h e", h=4),
                                                      in1=bc(u[:T].unsqueeze(2), [T, 4, 128]), op=ALU.mult),
                     [va, g2s], [Vaug])
                S.op("dve", lambda e: e.tensor_copy(out=Vaug[:T, :, 128], in_=u[:T]), [g2s, Vaug], [Vaug])

                yield
                def oreg(h, n=129):
                    return psO[:T, h // 2, (h % 2) * 129:(h % 2) * 129 + n]

                def creg(h):
                    return psC[:, h // 2, (h % 2) * 129:(h % 2) * 129 + 129]

                started = [False, False]

                def omm(h, lhsT, rhs, reads, last):
                    b = h // 2
                    mm(oreg(h), lhsT, rhs, not started[b], last, reads, [psO], skip=True)
                    started[b] = True

                def rms_gate(src, gate, dst_lo):
                    S.op("dve", lambda e: e.tensor_tensor(out=tmp3[:T, :, :], in0=src[:T, :, :], in1=src[:T, :, :],
                                                           op=ALU.mult), [src], [tmp3])
                    S.op("dve", lambda e: e.tensor_reduce(out=ssq[:T], in_=tmp3[:T, :, :], axis=AX.X, op=ALU.add),
                         [tmp3], [g2s])
                    S.op("act", lambda e: e.activation(out=rstd[:T], in_=ssq[:T], func=AF.Ln, bias=eps_rms[:T, :],
                                                       scale=1.0 / 128.0), [g2s, eps_rms], [g2s])
                    S.op("act", lambda e: e.activation(out=rstd[:T], in_=rstd[:T], func=AF.Exp, scale=-0.5), [g2s], [g2s])
                    S.op("dve", lambda e: e.tensor_tensor(out=src[:T, :, :], in0=src[:T, :, :],
                                                          in1=bc(rstd[:T].unsqueeze(2), [T, 4, 128]), op=ALU.mult),
                         [src, g2s], [src])
                    S.op("dve", lambda e: e.tensor_tensor(out=mix[:T, dst_lo:dst_lo + 512].rearrange("p (h e) -> p h e", h=4),
                                                          in0=src[:T, :, :], in1=gate[:T, :].rearrange("p (h e) -> p h e", h=4),
                                                          op=ALU.mult), [src, gate], [mix])

                def load_C(j):
                    k = j % NCS
                    S.dma("sp", Cld[k][:, :, :], C_s[j], writes=[Cld[k]], sem_tile=Cld[k])
                    S.dma("sp", nld4[k][:, :], n_s[j], writes=[nld4[k]], sem_tile=nld4[k])

                def mloop():
                    def X(j):
                        Cp, Cbb = Cp32s[j & 1], Cbs[j & 1]
                        if sm:
                            Csrc, al, al_r = Cld[j % NCS], alphaD[:, 4 * j:4 * j + 4], [alphaD]
                            nsrc = nld4[j % NCS]
                            S.op("dve", lambda e: e.tensor_tensor(out=Cp[:, :, 0:128], in0=Csrc[:, :, :],
                                                                  in1=bc(al.unsqueeze(2), [128, 4, 128]), op=ALU.mult),
                                 [Csrc] + al_r, [Cp])
                            S.op("dve", lambda e: e.tensor_tensor(out=Cp[:, :, 128], in0=nsrc[:, :], in1=al, op=ALU.mult),
                                 [nsrc, Cp] + al_r, [Cp])
                        else:
                            Csrc, al, al_r = Cst, alpha[:, :], [alpha]
                            if not has_state:
                                return
                            S.op("dve", lambda e: e.tensor_tensor(out=Cp[:, :, :], in0=Csrc[:, :, :],
                                                                  in1=bc(al.unsqueeze(2), [128, 4, 129]), op=ALU.mult),
                                 [Csrc] + al_r, [Cp])
                        S.op("act", lambda e: e.copy(out=Cbb[:, :, :], in_=Cp[:, :, :]), [Cp], [Cbb])
                        if sm:
                            qm = qTmA[j & 1]
                            if j > 1:
                                S.op("act", lambda e: e.mul(out=qm[:, :, 8 * (j - 2):8 * (j - 1)],
                                                            in_=qkT[:, 0:4, 8 * (j - 2):8 * (j - 1)], mul=0.0), [qkT], [qm])
                            S.op("act", lambda e: e.copy(out=qm[:, :, 8 * j:8 * j + 8],
                                                         in_=qkT[:, 0:4, 8 * j:8 * j + 8]), [qkT], [qm])
                            qop, qr = qm, [qm]
                        else:
                            qop, qr = qkT, [qkT]
                        for h in range(4):
                            omm(h, qop[:, h, :T], Cbb[:, h, :], qr + [Cbb], False)

                    def Y(j):
                        Cp = Cp32s[j & 1]
                        Cdst = Cout[j & 1] if sm else Cst
                        if sm:
                            S.op("dve", lambda e: e.tensor_scalar(out=Vm[:, :, :], in0=Vaug[:, :, :], scalar1=blkind[:, j:j + 1],
                                                                  scalar2=None, op0=ALU.mult), [Vaug, blkind], [Vm])
                            Vop = Vm
                        else:
                            Vop = Vaug
                        for h in range(4):
                            mm(creg(h), ka_bf[:T, h * 128:(h + 1) * 128], Vop[:T, h, :], True, True, [ka_bf, Vop], [psC])
                        cview = psC[:, :, 0:258].rearrange("p g (i e) -> p g i e", i=2)
                        if sm:
                            nd_ = nout[j & 1]
                            S.op("dve", lambda e: e.tensor_tensor(out=Cdst[:, :, :].rearrange("p (g i) e -> p g i e", g=2),
                                                                  in0=Cp[:, :, 0:128].rearrange("p (g i) e -> p g i e", g=2),
                                                                  in1=cview[:, :, :, 0:128], op=ALU.add), [Cp, psC], [Cdst])
                            S.op("dve", lambda e: e.tensor_tensor(out=nd_[:, :].rearrange("p (g i) -> p g i", g=2),
                                                                  in0=Cp[:, :, 128].rearrange("p (g i) -> p g i", g=2),
                                                                  in1=cview[:, :, :, 128], op=ALU.add), [Cp, psC], [nd_])
                            S.dma("pool", C_so[j], Cdst[:, :, :], reads=[Cdst], sem_tile=Cdst, final=True)
                            S.dma("pool", n_so[j], nd_[:, :], reads=[nd_], sem_tile=nd_, final=True)
                        elif has_state:
                            S.op("dve", lambda e: e.tensor_tensor(out=Cdst[:, :, :].rearrange("p (g i) e -> p g i e", g=2),
                                                                  in0=Cp[:, :, :].rearrange("p (g i) e -> p g i e", g=2),
                                                                  in1=cview, op=ALU.add), [Cp, psC], [Cdst])
                        else:
                            S.op("dve", lambda e: e.tensor_copy(out=Cdst[:, :, :].rearrange("p (g i) e -> p g i e", g=2),
                                                                in_=cview), [psC], [Cdst])

                    if not sm:
                        yield
                        X(0)
                        Y(0)
                        return
                    for j0 in range(NCS - 1):
                        load_C(j0)
                    for qb_ in qTmA:
                        S.op("pool", lambda e: e.memset(qb_[:, :, :], 0.0), [], [qb_])
                    X(0)
                    for j in range(J):
                        yield
                        if j + NCS - 1 < J:
                            load_C(j + NCS - 1)
                        if j + 1 < J:
                            X(j + 1)
                        Y(j)

                if sm:
                    pOb_t, pCb_t, Stm_ = psP[0], psP[1], diag
                    psOb = psP[0][:, :].rearrange("p (h e) -> p h e", h=4)
                    psCb = psP[1][:, :].rearrange("p (h e) -> p h e", h=4)
                else:
                    pOb_t, pCb_t, Stm_ = psO, psC, Stm
                    psOb = psO[:, 0, :].rearrange("p (h e) -> p h e", h=4)
                    psCb = psC[:, 0, :].rearrange("p (h e) -> p h e", h=4)
                startedB = [False]

                def obmm(h, lhsT, rhs, reads, last):
                    mm(psOb[:T, h, :], lhsT, rhs, not startedB[0], last, reads, [pOb_t], skip=True)
                    startedB[0] = True

                def load_S(j):
                    k = j % NSS
                    S.dma("sp", Sld[k][:, :, :], S_s[j], writes=[Sld[k]], sem_tile=Sld[k])

                def hloop():
                    def X(j):
                        if sm:
                            Ssrc, Sbb = Sld[j % NSS], Sbs[j & 1]
                            S.op("act", lambda e: e.copy(out=Sbb[:, :, :], in_=Ssrc[:, :, :]), [Ssrc], [Sbb])
                            qm = qTmB[j & 1]
                            if j > 1:
                                S.op("act", lambda e: e.mul(out=qm[:, :, 8 * (j - 2):8 * (j - 1)],
                                                            in_=qkT2[:, 0:4, 8 * (j - 2):8 * (j - 1)], mul=0.0), [qkT2], [qm])
                            S.op("act", lambda e: e.copy(out=qm[:, :, 8 * j:8 * j + 8],
                                                         in_=qkT2[:, 0:4, 8 * j:8 * j + 8]), [qkT2], [qm])
                            qop, qr = qm, [qm]
                        else:
                            Sbb = Sb
                            qop, qr = qkT2, [qkT2]
                        if has_state:
                            for h in range(4):
                                obmm(h, qop[:, h, :T], Sbb[:, h, :], qr + [Sbb], False)

                    def Y(j):
                        if sm:
                            Ssrc, Sdst = Sld[j % NSS], Sout[j & 1]
                            S.op("dve", lambda e: e.tensor_scalar(out=ivm[:, :], in0=iv_bf[:, :], scalar1=blkind[:, j:j + 1],
                                                                  scalar2=None, op0=ALU.mult), [iv_bf, blkind], [ivm])
                            ivop = ivm
                        else:
                            Ssrc, Sdst = Sst, Sst
                            ivop = iv_bf
                        for h in range(4):
                            mm(psCb[:, h, :], kt_bf[:T, h * 128:(h + 1) * 128], ivop[:T, h * 128:(h + 1) * 128], True, True,
                               [kt_bf, ivop], [pCb_t])
                        ea_j = bc(eaT[:, :, j:j + 1], [128, 4, 128])
                        if has_state:
                            S.op("dve", lambda e: e.tensor_tensor(out=Stm_[:, :, :], in0=Ssrc[:, :, :], in1=psCb, op=ALU.add),
                                 [Ssrc, pCb_t], [Stm_])
                            S.op("dve", lambda e: e.tensor_tensor(out=Sdst[:, :, :], in0=Stm_[:, :, :], in1=ea_j, op=ALU.mult),
                                 [Stm_, eaT], [Sdst])
                        else:
                            S.op("dve", lambda e: e.tensor_tensor(out=Sdst[:, :, :], in0=psCb, in1=ea_j, op=ALU.mult),
                                 [pCb_t, eaT], [Sdst])
                        if sm:
                            S.dma("pool", S_so[j], Sdst[:, :, :], reads=[Sdst], sem_tile=Sdst, final=True)
                        else:
                            S.op("act", lambda e: e.copy(out=Sb[:, :, :], in_=Sst[:, :, :]), [Sst], [Sb])

                    if not sm:
                        yield
                        X(0)
                        Y(0)
                        return
                    for j0 in range(NSS):
                        load_S(j0)
                    for qb_ in qTmB:
                        S.op("pool", lambda e: e.memset(qb_[:, :, :], 0.0), [], [qb_])
                    X(0)
                    for j in range(J):
                        yield
                        if j + 1 < J:
                            X(j + 1)
                        Y(j)
                        if j + NSS < J:
                            load_S(j + NSS)

                def mpost():
                    for h in range(4):
                        omm(h, SmT[:T, h, :T], Vaug[:T, h, :], [SmT, Vaug], True)
                    yield
                    oden = psO[:T, :, 0:258].rearrange("p g (i e) -> p g i e", i=2)[:, :, :, 128]
                    onum = psO[:T, :, 0:258].rearrange("p g (i e) -> p g i e", i=2)[:, :, :, 0:128]
                    S.op("dve", lambda e: e.tensor_tensor(out=rec[:T].rearrange("p (g i) -> p g i", g=2), in0=oden,
                                                          in1=thr[:T].rearrange("p (g i) -> p g i", g=2), op=ALU.max),
                         [psO, g2s], [g2s])
                    S.op("dve", lambda e: e.scalar_tensor_tensor(out=den[:T].rearrange("p (g i) -> p g i", g=2), in0=oden,
                                                                 scalar=-1.0, in1=rec[:T].rearrange("p (g i) -> p g i", g=2),
                                                                 op0=ALU.mult, op1=ALU.max), [psO, g2s], [g2s])
                    S.op("dve", lambda e: e.reciprocal(out=rec[:T], in_=den[:T]), [g2s], [g2s])
                    S.op("dve", lambda e: e.tensor_tensor(out=hn[:T, :, :].rearrange("p (g i) e -> p g i e", g=2), in0=onum,
                                                          in1=bc(rec[:T].rearrange("p (g i) -> p g i", g=2).unsqueeze(3),
                                                                 [T, 2, 2, 128]), op=ALU.mult), [psO, g2s], [hn])
                    yield
                    rms_gate(hn, og, 0)

                def hpost():
                    for h in range(4):
                        obmm(h, ScT[:T, h, :T], iv_bf[:T, h * 128:(h + 1) * 128], [ScT, iv_bf], True)
                    S.op("act", lambda e: e.copy(out=hn[:T, :, :], in_=psOb[:T, :, :]), [pOb_t], [hn])
                    yield
                    rms_gate(hn, gg, 512)

                def il(*gens):
                    gens = list(gens)
                    while gens:
                        for g in list(gens):
                            try:
                                next(g)
                                yield
                            except StopIteration:
                                gens.remove(g)

                if sm:
                    S.run_streams([exhaust(mloop()), exhaust(hloop())], quanta=SQ)
                    yield from mpost()
                    yield
                    yield from hpost()
                else:
                    yield from mloop()
                    yield
                    yield from mpost()
                    yield
                    yield from hloop()
                    yield
                    yield from hpost()
                yield
                if sm:
                    prescale()
                S.atomic += 1
                transposes(mix, T, 8, 0, psT)
                S.atomic -= 1
                S.op("act", lambda e: e.copy(out=mixT[:, :, :T], in_=psT[:, :, :T]), [psT], [mixT])
                for half in range(2):
                    c0 = half * 512
                    p = (psC0, psO0)[half]
                    for kc in range(8):
                        mm(p[:T, :], mixT[:, kc, :T], w_out_t[:, kc, c0:c0 + 512], kc == 0, False, [mixT, w_out_t], [p])
                    mm(p[:T, :], onesB[0:2, :T], bout_hl[0:2, c0:c0 + 512], False, True, [onesB, bout_hl], [p])
                    S.op("dve", lambda e, p=p, c0=c0: e.tensor_tensor(out=yb[:T, c0:c0 + 512], in0=yb[:T, c0:c0 + 512],
                                                                     in1=p[:T, :], op=ALU.add), [yb, p], [yb])
                yield
                layer_norm_rows(yb, T, yb, None, None, lnw2, xo_bf=x1hb)
                S.dma("pool", x1h_d[c * 128:c * 128 + T, :], yb[:T, :], reads=[yb], writes=[x1h_d], sem_tile=yb)
                if c == ntiles - 1:
                    x1_tail(c)

                yield
                if c == 16:
                    S.dma("pool", C_po.rearrange("h d e -> d h e"), Cst[:, :, 0:128], reads=[Cst], sem_tile=Cst, final=True)
                    S.dma("pool", n_po.rearrange("h d -> d h"), Cst[:, :, 128], reads=[Cst], sem_tile=Cst, final=True,
                          allow_slow_non_contiguous=True)
                    S.dma("pool", S_po.rearrange("h d e -> d h e"), Sst[:, :, :], reads=[Sst], sem_tile=Sst, final=True)
                    S.op("dve", lambda e: e.tensor_tensor(out=mnew[:, :], in0=Mc[:, :], in1=nBc[:, :], op=ALU.subtract),
                         [Mc, nBc], [mnew])
                    S.dma("pool", m_po, mnew[0:1, :], reads=[mnew], sem_tile=mnew, final=True)

            ntiles = NT if STAGE >= 2 else 2
            def run_interleaved(*gens):
                gens = [g for g in gens if g is not None]
                while gens:
                    for g in list(gens):
                        try:
                            next(g)
                        except StopIteration:
                            gens.remove(g)

            def exhaust(gen):
                def f():
                    for _ in gen:
                        pass
                return f

            ln_stage(0)
            run_interleaved(stage1(0))
            for c in range(1, ntiles):
                S.run_streams([exhaust(stage1(c)), exhaust(stage2(c - 1))], quanta=QUANTA)
            run_interleaved(stage2(ntiles - 1))
            print("phase A sbuf bytes remaining:", nc.sbuf_bytes_remaining)

        spa.close()
        S.barrier()

        with ExitStack() as sb_:
            def Bf(name, shape, dt=F32):
                return S.sb(name, shape, dt, st=sb_)

            pu = [S.ps("pu0", [128, 512], st=sb_), S.ps("pu1", [128, 512], st=sb_)]
            pg = [S.ps("pg0", [128, 512], st=sb_), S.ps("pg1", [128, 512], st=sb_)]
            pc = [S.ps("pc0", [128, 512], st=sb_), S.ps("pc1", [128, 512], st=sb_)]
            pm = S.ps("pm", [128, 512], st=sb_)

            w_dn_t = Bf("w_dn_t", [128, NFC, D], BF16)
            ln1g_bc = Bf("ln1g_bc", [128, D]); ln1b_bc = Bf("ln1b_bc", [128, D])
            ln2g_bc = Bf("ln2g_bc", [128, D]); ln2b_bc = Bf("ln2b_bc", [128, D])
            x1Th = Bf("x1Th", [128, 8, 1152], BF16)
            S.dma("sp", x1Th[:, :, 0:1040], x1T_d[:, :, 0:1040], reads=[x1T_d], writes=[x1Th], sem_tile=x1Th)
            x1l = [Bf("x1l0", [128, D]), Bf("x1l1", [128, D])]
            zb = [Bf("zb0", [128, D]), Bf("zb1", [128, D])]
            bdn_hl = Bf("bdn_hl", [2, D], BF16)

            prm_ld = Bf("prm_ld", [NFC, 6, 128]); prm = Bf("prm", [128, 6, NFC])
            srcs = [b_up[0:DFF], b_up[DFF:2 * DFF], w_conv[0, :], w_conv[1, :], w_conv[2, :], b_conv]
            for k, s_ in enumerate(srcs):
                S.dma("sp", prm_ld[:, k, :], s_.rearrange("(fc p) -> fc p", p=128), writes=[prm_ld], sem_tile=prm_ld)
            for k in range(6):
                S.op("pe", lambda e, k=k: e.transpose(pm[:, k * NFC:(k + 1) * NFC], prm_ld[:NFC, k, :], ident[:NFC, :NFC]),
                     [prm_ld, ident], [pm])
            S.op("dve", lambda e: e.tensor_copy(out=prm[:, :, :], in_=pm[:, 0:6 * NFC].rearrange("p (k f) -> p k f", k=6)),
                 [pm], [prm])
            cvbuf = Bf("cvbuf", [34, DFF]); cst = Bf("cst", [128, NFC, 32])
            S.dma("sp", cvbuf[0:32, :], cv_s, writes=[cvbuf], sem_tile=cvbuf)
            for f0 in range(0, NFC, 11):
                for fc in range(f0, f0 + 11):
                    S.op("pe", lambda e, fc=fc, f0=f0: e.transpose(pm[:, (fc - f0) * 32:(fc - f0) * 32 + 32],
                                                                  cvbuf[0:32, fc * 128:(fc + 1) * 128], ident[:32, :32]),
                         [cvbuf, ident], [pm])
                S.op("dve", lambda e, f0=f0: e.tensor_copy(out=cst[:, f0:f0 + 11, :],
                                                           in_=pm[:, 0:352].rearrange("p (f r) -> p f r", f=11)),
                     [pm], [cst])
            ulast = Bf("ulast", [128, NFC, 34]); ucar = Bf("ucar", [128, NFC, 2])
            hbuf = Bf("hbuf", [128, NFC, 1152], BF16)
            bias_rows(bdn_hl, b_down, D, zb[0], zb[1], hbuf.view(lambda h: h[:, 0, :]))
            for t_, s_ in ((ln1g_bc, ln1_g), (ln1b_bc, ln1_b), (ln2g_bc, ln2_g), (ln2b_bc, ln2_b)):
                S.dma("sp", t_[:, :], s_.partition_broadcast(128), writes=[t_], sem_tile=t_)
            wub = [Bf("wub0", [128, 8, 256], BF16), Bf("wub1", [128, 8, 256], BF16)]
            ub = [Bf("ub0", [128, 514]), Bf("ub1", [128, 514])]
            ubs = Bf("ubs", [128, 16, 10])
            cvb = [Bf("cvb0", [128, 512]), Bf("cvb1", [128, 512])]
            slb = [Bf("slb0", [128, 512]), Bf("slb1", [128, 512])]
            w_up_v = w_up.rearrange("(kc p) n -> p kc n", p=128)

            def load_wub(fc, slot):
                S.dma("pool", wub[slot][:, :, 0:128], w_up_v[:, :, fc * 128:(fc + 1) * 128], writes=[wub[slot]],
                      sem_tile=wub[slot])
                S.dma("pool", wub[slot][:, :, 128:256], w_up_v[:, :, DFF + fc * 128:DFF + (fc + 1) * 128],
                      writes=[wub[slot]], sem_tile=wub[slot])

            HALVES = [
                dict(lo=0, hi=1040, groups=[(0, 347, "p"), (347, 694, "p"), (694, 1040, "p")], tiles=list(range(1, 9))),
                dict(lo=1040, hi=2192, groups=[(1040, 1552, "p"), (1552, 2064, "p"), (2064, 2192, "s")],
                     tiles=list(range(9, 18))),
            ]
            gi = [0]
            wslot = [0]
            nhalves = 2 if STAGE >= 3 else 0
            for hf_i in range(nhalves):
                hf = HALVES[hf_i]
                lo, hi = hf["lo"], hf["hi"]
                if hf_i > 0:
                    S.dma("sp", x1Th[:, :, 0:hi - lo], x1T_d[:, :, lo:hi], reads=[x1T_d], writes=[x1Th], sem_tile=x1Th)
                load_wub(0, wslot[0])
                if hf_i == 0:
                    w_dn_v = w_down.rearrange("(fc p) n -> p fc n", p=128)
                    for f0 in range(0, NFC, 11):
                        S.dma("pool", w_dn_t[:, f0:f0 + 11, :], w_dn_v[:, f0:f0 + 11, :], writes=[w_dn_t], sem_tile=w_dn_t)
                pend = [None]
                def grp(fc, c0, c1, kind, W, s2, pn):
                    n = c1 - c0
                    l0 = c0 - lo
                    P_ = lambda k: prm[:, k, fc:fc + 1]
                    U, Gp, CV, SL, UB = pu[s2], pg[s2], cvb[s2], slb[s2], ub[s2]
                    for kc in range(8):
                        mm(U[:, :n], W[:, kc, 0:128], x1Th[:, kc, l0:l0 + n], kc == 0, kc == 7, [W, x1Th], [U])
                    for kc in range(8):
                        mm(Gp[:, :n], W[:, kc, 128:256], x1Th[:, kc, l0:l0 + n], kc == 0, kc == 7, [W, x1Th], [Gp])
                    if kind == "p":
                        UBp = ub[s2 ^ 1]
                        if c0 == 0:
                            S.op("dve", lambda e: e.memset(UB[:, 0:2], 0.0), [], [UB])
                        elif c0 == 1040:
                            S.op("dve", lambda e: e.tensor_copy(out=UB[:, 0:2], in_=ucar[:, fc, :]), [ucar], [UB])
                        else:
                            npv = pn
                            S.op("dve", lambda e: e.tensor_copy(out=UB[:, 0:2], in_=UBp[:, npv:npv + 2]), [UBp], [UB])
                        S.op("act", lambda e: e.activation(out=UB[:, 2:2 + n], in_=U[:, :n], func=AF.Identity,
                                                           bias=P_(0), scale=1.0), [U, prm], [UB])
                        S.op("act", lambda e: e.activation(out=CV[:, :n], in_=UB[:, 2:2 + n], func=AF.Identity,
                                                           bias=P_(5), scale=P_(4)), [UB, prm], [CV])
                        S.op("dve", lambda e: e.scalar_tensor_tensor(out=CV[:, :n], in0=UB[:, 1:1 + n], scalar=P_(3),
                                                                     in1=CV[:, :n], op0=ALU.mult, op1=ALU.add),
                             [UB, prm, CV], [CV])
                        S.op("dve", lambda e: e.scalar_tensor_tensor(out=CV[:, :n], in0=UB[:, 0:n], scalar=P_(2),
                                                                     in1=CV[:, :n], op0=ALU.mult, op1=ALU.add),
                             [UB, prm, CV], [CV])
                        if c1 == 1040:
                            S.op("dve", lambda e: e.tensor_copy(out=ucar[:, fc, :], in_=UB[:, n:n + 2]), [UB], [ucar])
                        if c1 == 2064:
                            S.op("dve", lambda e: e.tensor_copy(out=ulast[:, fc, 32:34], in_=UB[:, n:n + 2]), [UB], [ulast])
                        yield
                        S.op("act", lambda e: e.activation(out=SL[:, :n], in_=CV[:, :n], func=AF.Silu), [CV], [SL])
                        S.op("dve", lambda e: e.scalar_tensor_tensor(out=hbuf[:, fc, l0:l0 + n], in0=Gp[:, :n], scalar=P_(1),
                                                                     in1=SL[:, :n], op0=ALU.add, op1=ALU.mult),
                             [Gp, prm, SL], [hbuf])
                    else:
                        v3 = lambda ap: ap.rearrange("p (j t) -> p j t", j=16)
                        S.op("dve", lambda e: e.tensor_copy(out=ubs[:, :, 0:2],
                                                            in_=cst[:, fc, :].rearrange("p (j r) -> p j r", j=16)),
                             [cst], [ubs])
                        S.op("act", lambda e: e.activation(out=ubs[:, :, 2:10], in_=v3(U[:, :128]), func=AF.Identity,
                                                           bias=P_(0), scale=1.0), [U, prm], [ubs])
                        S.op("act", lambda e: e.activation(out=v3(CV[:, :128]), in_=ubs[:, :, 2:10], func=AF.Identity,
                                                           bias=P_(5), scale=P_(4)), [ubs, prm], [CV])
                        S.op("dve", lambda e: e.scalar_tensor_tensor(out=v3(CV[:, :128]), in0=ubs[:, :, 1:9], scalar=P_(3),
                                                                     in1=v3(CV[:, :128]), op0=ALU.mult, op1=ALU.add),
                             [ubs, prm, CV], [CV])
                        S.op("dve", lambda e: e.scalar_tensor_tensor(out=v3(CV[:, :128]), in0=ubs[:, :, 0:8], scalar=P_(2),
                                                                     in1=v3(CV[:, :128]), op0=ALU.mult, op1=ALU.add),
                             [ubs, prm, CV], [CV])
                        S.op("dve", lambda e: e.tensor_copy(out=ulast[:, fc, 0:32].rearrange("p (j r) -> p j r", j=16),
                                                            in_=ubs[:, :, 8:10]), [ubs], [ulast])
                        yield
                        S.op("act", lambda e: e.activation(out=SL[:, :128], in_=CV[:, :128], func=AF.Silu), [CV], [SL])
                        S.op("dve", lambda e: e.scalar_tensor_tensor(out=hbuf[:, fc, l0:l0 + 128], in0=Gp[:, :128],
                                                                     scalar=P_(1), in1=SL[:, :128], op0=ALU.add,
                                                                     op1=ALU.mult), [Gp, prm, SL], [hbuf])

                def fin(g):
                    for _ in g:
                        pass

                for fc in range(NFC):
                    cur = wslot[0]
                    if fc + 1 < NFC:
                        load_wub(fc + 1, cur ^ 1)
                    W = wub[cur]
                    pn = 0
                    for (c0, c1, kind) in hf["groups"]:
                        s2 = gi[0] & 1
                        gi[0] += 1
                        g = grp(fc, c0, c1, kind, W, s2, pn)
                        next(g)
                        if pend[0] is not None:
                            fin(pend[0])
                        pend[0] = g
                        pn = c1 - c0
                    wslot[0] ^= 1

                if pend[0] is not None:
                    fin(pend[0])
                    pend[0] = None
                tiles = hf["tiles"]

                def load_x1(idx):
                    c = tiles[idx]
                    S.dma("sp", x1l[idx & 1][:, :], x1h_d[c * 128:(c + 1) * 128, :], reads=[x1h_d], writes=[x1l[idx & 1]],
                          sem_tile=x1l[idx & 1])

                load_x1(0)
                for idx, c in enumerate(tiles):
                    if idx + 1 < len(tiles):
                        load_x1(idx + 1)
                    col0 = 2064 if c == 17 else 16 + (c - 1) * 128
                    l0 = col0 - lo
                    X, Z = x1l[idx & 1], zb[idx & 1]
                    S.op("pool", lambda e: e.tensor_tensor(out=X[:, :], in0=X[:, :], in1=ln1g_bc[:, :], op=ALU.mult),
                         [X, ln1g_bc], [X])
                    S.op("pool", lambda e: e.tensor_tensor(out=X[:, :], in0=X[:, :], in1=ln1b_bc[:, :], op=ALU.add),
                         [X, ln1b_bc], [X])
                    for hh in range(2):
                        p = (pc, pu, pg)[idx % 3][hh]
                        for fc in range(NFC):
                            mm(p[:, :], hbuf[:, fc, l0:l0 + 128], w_dn_t[:, fc, hh * 512:(hh + 1) * 512], fc == 0, False,
                               [hbuf, w_dn_t], [p])
                        mm(p[:, :], onesB[0:2, :128], bdn_hl[0:2, hh * 512:(hh + 1) * 512], False, True, [onesB, bdn_hl], [p])
                        S.op("dve", lambda e, p=p, hh=hh: e.scalar_tensor_tensor(out=Z[:, hh * 512:(hh + 1) * 512],
                                                                                in0=X[:, hh * 512:(hh + 1) * 512],
                                                                                scalar=float(ALPHA), in1=p[:, :], op0=ALU.mult,
                                                                                op1=ALU.add), [X, p], [Z])
                    layer_norm_rows(Z, 128, Z, ln2g_bc, ln2b_bc, lnw)
                    dst = y_s if c == 17 else y_p[(c - 1) * 128:c * 128, :]
                    S.dma("act", dst, Z[:, :], reads=[Z], sem_tile=Z, final=True)

            if nhalves == 2:
                for f0 in range(0, NFC, 4):
                    nf = min(4, NFC - f0)
                    for fc in range(f0, f0 + nf):
                        S.op("pe", lambda e, fc=fc, f0=f0: e.transpose(pm[0:34, (fc - f0) * 128:(fc - f0 + 1) * 128],
                                                                      ulast[:, fc, :], ident[:, :]), [ulast, ident], [pm])
                    S.op("dve", lambda e, f0=f0, nf=nf: e.tensor_copy(out=cvbuf[0:34, f0 * 128:(f0 + nf) * 128],
                                                                     in_=pm[0:34, 0:nf * 128]), [pm], [cvbuf])
                S.dma("act", cv_so, cvbuf[0:32, :], reads=[cvbuf], sem_tile=cvbuf, final=True)
                S.dma("act", cv_po, cvbuf[32:34, :], reads=[cvbuf], sem_tile=cvbuf, final=True)
            S.finish()

        S.finish()
    return nc


_CACHE = {}


def _consts():
    t = np.arange(128)
    same = (t[:, None] // 8) == (t[None, :] // 8)
    triP = (t[:, None] <= t[None, :]).astype(np.float32)
    triS = (triP.astype(bool) & same).astype(np.float32)
    negBlk = np.where(same, 0.0, NEG).astype(np.float32)
    blkones = same.astype(np.float32)
    blkind = (t[:, None] // 8 == np.arange(16)[None, :]).astype(np.float32)
    sel0 = (t[:, None] == 8 * np.arange(16)[None, :]).astype(np.float32)
    return {"c_ident": np.eye(128, dtype=np.float32), "c_triP": triP, "c_triS": triS, "c_negBlk": negBlk,
            "c_blkones": blkones, "c_blkind": blkind, "c_blkindT": np.ascontiguousarray(blkind.T), "c_sel0": sel0}


def kernel(x_prompt, x_sample, state_mlstm_C, state_mlstm_n, state_mlstm_m, state_hgrn_S, state_ffn_conv,
           meta_tokens, ln_emb_g, ln_emb_b, w_in, b_in, b_fgate_a, g_norm_a, g_norm_b, hgrn_lb_logits,
           w_out, b_out, ln1_g, ln1_b, w_up, b_up, w_conv, b_conv, w_down, b_down, ln2_g, ln2_b):
    f = lambda a: np.ascontiguousarray(np.asarray(a, dtype=np.float32))
    if "nc" not in _CACHE:
        _CACHE["nc"] = build_program()
    nc = _CACHE["nc"]
    shared = {
        "meta": f(meta_tokens), "ln_emb_g": f(ln_emb_g), "ln_emb_b": f(ln_emb_b),
        "w_in": f(w_in[0]), "b_in": f(b_in[0]).reshape(1, DIN), "b_fg": f(b_fgate_a[0]),
        "g_a": f(g_norm_a[0]).reshape(512), "g_b": f(g_norm_b[0]).reshape(512), "lb_log": f(hgrn_lb_logits),
        "w_out": f(w_out[0]), "b_out": f(b_out[0]).reshape(1, D), "ln1_g": f(ln1_g[0]), "ln1_b": f(ln1_b[0]),
        "w_up": f(w_up[0]), "b_up": f(b_up[0]), "w_conv": f(w_conv[0]), "b_conv": f(b_conv[0]),
        "w_down": f(w_down[0]), "b_down": f(b_down[0]).reshape(1, D), "ln2_g": f(ln2_g[0]), "ln2_b": f(ln2_b[0]),
    }
    shared.update(_consts())
    in_maps = []
    for i in range(8):
        sl = slice(16 * i, 16 * i + 16)
        m = dict(shared)
        m["x_p"] = f(x_prompt[i])
        m["x_s"] = f(x_sample[sl]).reshape(128, D)
        m["C_s"] = f(np.asarray(state_mlstm_C[0, sl]).transpose(0, 2, 1, 3))
        m["n_s"] = f(np.asarray(state_mlstm_n[0, sl]).transpose(0, 2, 1))
        m["m_s"] = f(state_mlstm_m[0, sl]); m["S_s"] = f(np.asarray(state_hgrn_S[0, sl]).transpose(0, 2, 1, 3))
        m["cv_s"] = f(state_ffn_conv[0, sl]).reshape(32, DFF)
        in_maps.append(m)
    res = run_bass_kernel_spmd(nc, in_maps, core_ids=list(range(8)))
    R = res.results
    cat = lambda k: np.stack([np.asarray(r[k], dtype=np.float32) for r in R])
    y_prompt = cat("y_p")
    y_sample = cat("y_s").reshape(128, 8, D)
    C_p = cat("C_po")[None]; n_p = cat("n_po")[None]; m_p = cat("m_po").reshape(1, 8, 4)
    S_p = cat("S_po")[None]; cv_p = cat("cv_po")[None]
    C_so = np.ascontiguousarray(cat("C_so").transpose(0, 1, 3, 2, 4)).reshape(1, 128, 4, 128, 128)
    n_so = np.ascontiguousarray(cat("n_so").transpose(0, 1, 3, 2)).reshape(1, 128, 4, 128)
    m_so = cat("m_so").reshape(1, 128, 4)
    S_so = np.ascontiguousarray(cat("S_so").transpose(0, 1, 3, 2, 4)).reshape(1, 128, 4, 128, 128)
    cv_so = cat("cv_so").reshape(1, 128, 2, DFF)
    return (y_prompt, y_sample, C_p, n_p, m_p, S_p, cv_p, C_so, n_so, m_so, S_so, cv_so)
```

```python
import os
import threading
from contextlib import ExitStack

import numpy as np
import concourse.bass as bass
import concourse.mybir as mybir
from concourse.bass_utils import run_bass_kernel_spmd

F32 = mybir.dt.float32
BF16 = mybir.dt.bfloat16
AF = mybir.ActivationFunctionType
ALU = mybir.AluOpType
AX = mybir.AxisListType

D = 1024
DIN = 4104
DFF = 2816
NFC = DFF // 128
NCOL = 2192
ALPHA = 2.0 ** 0.25
LN_EPS = 1e-5
RMS_EPS = 1e-6
NEG = -1.0e30
NT = 18
STAGE = int(os.environ.get("KSTAGE", "99"))
LEAD = (0, 0)
SQ = (1, 1)
QUANTA = (6, 4)


class _St:
    __slots__ = ("w", "r", "dsem", "dcnt")

    def __init__(self):
        self.w = None
        self.r = {}
        self.dsem = {}
        self.dcnt = {}


class Tk:
    def __init__(self, h, name, st=None):
        self.h = h
        self.name = name
        self._s = st or _St()

    def view(self, fn):
        return Tk(fn(self.h), self.name + "_v", self._s)

    def __getitem__(self, k):
        return self.h[k]

    w = property(lambda self: self._s.w, lambda self, v: setattr(self._s, "w", v))
    r = property(lambda self: self._s.r, lambda self, v: setattr(self._s, "r", v))


class Sched:
    def __init__(self, nc, st):
        self.nc = nc
        self.st = st
        self.eng = {"pe": nc.tensor, "act": nc.scalar, "dve": nc.vector, "pool": nc.gpsimd, "sp": nc.sync}
        self.sem = {k: st.enter_context(nc.semaphore("s_" + k)) for k in self.eng}
        self.cnt = {k: 0 for k in self.eng}
        self.seen = {k: {} for k in self.eng}
        self.out_events = {}
        self.all_dma = {}
        self.nd = 0
        self.cur = None
        self.atomic = 0

    def sb(self, name, shape, dt=F32, st=None):
        h = (st or self.st).enter_context(self.nc.sbuf_tensor(name, list(shape), dt))
        return Tk(h, name)

    def ps(self, name, shape, dt=F32, st=None):
        h = (st or self.st).enter_context(self.nc.psum_tensor(name, list(shape), dt))
        return Tk(h, name)

    def dram(self, name, shape, dt=F32):
        h = self.nc.dram_tensor(name, list(shape), dt, kind="Internal")
        return Tk(h.ap(), name)

    def _wait(self, e, deps):
        for key, sem, val in deps:
            if self.seen[e].get(key, 0) >= val:
                continue
            self.eng[e].wait_ge(sem, val)
            self.seen[e][key] = val

    def _deps(self, e, reads, writes):
        deps = []
        for t in reads:
            if t.w is not None:
                deps.append(t.w)
        for t in writes:
            if t.w is not None and not (e == "pe" and t.w[0] == "pe"):
                deps.append(t.w)
            for k, (sem, val) in t.r.items():
                deps.append((k, sem, val))
        return deps

    def op(self, e, fn, reads=(), writes=()):
        self._wait(e, self._deps(e, reads, writes))
        ins = fn(self.eng[e])
        self.cnt[e] += 1
        ins.then_inc(self.sem[e], 1)
        ev = (e, self.sem[e], self.cnt[e])
        for t in writes:
            t.w = ev
            t.r = {}
        for t in reads:
            t.r[e] = (self.sem[e], self.cnt[e])
        self._sw()

    def dma(self, q, out, in_, reads=(), writes=(), sem_tile=None, final=False, **kw):
        self._wait(q, self._deps(q, reads, writes))
        kind = "sw" if q == "pool" else "hw"
        stt = sem_tile._s
        if kind not in stt.dsem:
            self.nd += 1
            stt.dsem[kind] = self.st.enter_context(self.nc.semaphore("d%d" % self.nd))
            stt.dcnt[kind] = 0
        ins = self.eng[q].dma_start(out=out, in_=in_, **kw)
        stt.dcnt[kind] += 1
        ins.then_inc(stt.dsem[kind], 16)
        key = ("d", id(stt), kind)
        ev = (key, stt.dsem[kind], 16 * stt.dcnt[kind])
        for t in writes:
            t.w = ev
            t.r = {}
        for t in reads:
            t.r[key] = (ev[1], ev[2])
        self.all_dma[key] = ev
        if final:
            self.out_events[key] = ev
        self._sw()

    def _sw(self):
        st = self.cur
        if st is not None:
            st.n += 1
            if self.atomic == 0 and st.n >= st.quantum:
                st.n = 0
                st.back.release()
                st.go.acquire()

    def run_streams(self, fns, quanta=None):
        class _Stream:
            pass
        streams = []
        for i, fn in enumerate(fns):
            st = _Stream()
            st.n = -(LEAD[i] if (quanta and i < len(LEAD)) else 0)
            st.quantum = quanta[i] if quanta else 1
            st.go = threading.Semaphore(0); st.back = threading.Semaphore(0); st.done = False; st.exc = None

            def body(st=st, fn=fn):
                st.go.acquire()
                try:
                    fn()
                except BaseException as e:
                    st.exc = e
                st.done = True
                st.back.release()
            st.th = threading.Thread(target=body)
            st.th.start()
            streams.append(st)
        live = list(streams)
        while live:
            for st in list(live):
                self.cur = st
                st.go.release()
                st.back.acquire()
                self.cur = None
                if st.done:
                    st.th.join()
                    live.remove(st)
                    if st.exc is not None:
                        for o in live:
                            o.done = True
                        raise st.exc
        self.cur = None

    def barrier(self):
        deps = list(self.all_dma.values())
        for e in ("pe", "act", "dve", "pool"):
            if self.cnt[e] > 0:
                deps.append((e, self.sem[e], self.cnt[e]))
        for e in self.eng:
            self._wait(e, [d for d in deps if d[0] != e])

    def finish(self):
        deps = list(self.out_events.values())
        for e in ("pe", "act", "dve", "pool"):
            if self.cnt[e] > 0:
                deps.append((e, self.sem[e], self.cnt[e]))
        self._wait("sp", deps)


def bc(ap, shape):
    return ap.to_broadcast(list(shape))


def build_program():
    nc = bass.Bass("TRN2", target_bir_lowering=False)

    def di(name, shape, dt=F32):
        return nc.dram_tensor(name, list(shape), dt, kind="ExternalInput").ap()

    def do(name, shape):
        return nc.dram_tensor(name, list(shape), F32, kind="ExternalOutput").ap()

    x_p = di("x_p", [2048, D]); x_s = di("x_s", [128, D]); meta = di("meta", [16, D])
    C_s = di("C_s", [16, 128, 4, 128]); n_s = di("n_s", [16, 128, 4]); m_s = di("m_s", [16, 4])
    S_s = di("S_s", [16, 128, 4, 128]); cv_s = di("cv_s", [32, DFF])
    ln_emb_g = di("ln_emb_g", [D]); ln_emb_b = di("ln_emb_b", [D])
    w_in = di("w_in", [D, DIN]); b_in = di("b_in", [1, DIN]); b_fg = di("b_fg", [4])
    g_a = di("g_a", [512]); g_b = di("g_b", [512]); lb_log = di("lb_log", [2, 512])
    w_out = di("w_out", [D, D]); b_out = di("b_out", [1, D])
    ln1_g = di("ln1_g", [D]); ln1_b = di("ln1_b", [D])
    w_up = di("w_up", [D, 2 * DFF]); b_up = di("b_up", [2 * DFF])
    w_conv = di("w_conv", [3, DFF]); b_conv = di("b_conv", [DFF])
    w_down = di("w_down", [DFF, D]); b_down = di("b_down", [1, D])
    ln2_g = di("ln2_g", [D]); ln2_b = di("ln2_b", [D])
    c_ident = di("c_ident", [128, 128]); c_triP = di("c_triP", [128, 128]); c_triS = di("c_triS", [128, 128])
    c_negBlk = di("c_negBlk", [128, 128]); c_blkones = di("c_blkones", [128, 128])
    c_blkind = di("c_blkind", [128, 16]); c_blkindT = di("c_blkindT", [16, 128]); c_sel0 = di("c_sel0", [128, 16])

    y_p = do("y_p", [2048, D]); y_s = do("y_s", [128, D])
    C_po = do("C_po", [4, 128, 128]); n_po = do("n_po", [4, 128]); m_po = do("m_po", [1, 4])
    S_po = do("S_po", [4, 128, 128]); cv_po = do("cv_po", [2, DFF])
    C_so = do("C_so", [16, 128, 4, 128]); n_so = do("n_so", [16, 128, 4]); m_so = do("m_so", [16, 4])
    S_so = do("S_so", [16, 128, 4, 128]); cv_so = do("cv_so", [32, DFF])

    with ExitStack() as st:
        S = Sched(nc, st)
        x1h_d = S.dram("x1h_d", [NT * 128, D], F32)
        x1T_d = S.dram("x1T_d", [128, 8, NCOL], BF16)

        ident = S.sb("ident", [128, 128]); identB = S.sb("identB", [128, 128], BF16)
        ones = S.sb("ones", [128, 128]); onesB = S.sb("onesB", [128, 128], BF16)
        S.dma("sp", ident[:, :], c_ident, writes=[ident], sem_tile=ident)
        S.op("dve", lambda e: e.tensor_copy(out=identB[:, :], in_=ident[:, :]), [ident], [identB])
        S.op("dve", lambda e: e.memset(ones[:, :], 1.0), [], [ones])
        S.op("dve", lambda e: e.memset(onesB[:, :], 1.0), [], [onesB])

        spa = ExitStack()
        psT = S.ps("psT", [128, 8, 128], BF16, st=spa)
        psP = [S.ps("psP0", [128, 512], st=spa), S.ps("psP1", [128, 512], st=spa)]
        psS = S.ps("psS", [128, 4, 128], st=spa)
        psO = S.ps("psO", [128, 2, 512], st=spa)
        psC = S.ps("psC", [128, 2, 512], st=spa)
        pp_i = [0]

        def pp():
            pp_i[0] ^= 1
            return psP[pp_i[0]]

        def mm(out, lhsT, rhs, start, stop, reads, writes, skip=False):
            kw = {"skip_group_check": True} if skip else {}
            S.op("pe", lambda e: e.matmul(out, lhsT, rhs, start=start, stop=stop, **kw), reads, writes)

        def bias_rows(hl, src, n, f, h32, tb):
            for c0 in range(0, n, 1024):
                w = min(1024, n - c0)
                S.dma("sp", f[0:1, :w], src[:, c0:c0 + w], writes=[f], sem_tile=f)
                S.op("dve", lambda e: e.tensor_copy(out=hl[0:1, c0:c0 + w], in_=f[0:1, :w]), [f], [hl])
                S.op("dve", lambda e: e.tensor_copy(out=h32[0:1, :w], in_=hl[0:1, c0:c0 + w]), [hl], [h32])
                S.op("dve", lambda e: e.tensor_tensor(out=tb[0:1, :w], in0=f[0:1, :w], in1=h32[0:1, :w],
                                                      op=ALU.subtract), [f, h32], [tb])
                S.dma("sp", hl[1:2, c0:c0 + w], tb[0:1, :w], reads=[tb], writes=[hl], sem_tile=hl)

        def layer_norm_rows(x, T, xo, g_bc, b_bc, wk, xo_bf=None):
            stt, mv, rs = wk
            for cch in range(2):
                S.op("dve", lambda e, cch=cch: e.bn_stats(out=stt[:T, cch * 6:cch * 6 + 6], in_=x[:T, cch * 512:(cch + 1) * 512]),
                     [x], [stt])
            S.op("dve", lambda e: e.bn_aggr(out=mv[:T, :], in_=stt[:T, :]), [stt], [mv])
            S.op("act", lambda e: e.activation(out=rs[:T, :], in_=mv[:T, 1:2], func=AF.Ln, bias=eps_ln[:T, :], scale=1.0),
                 [mv, eps_ln], [rs])
            S.op("act", lambda e: e.activation(out=rs[:T, :], in_=rs[:T, :], func=AF.Exp, scale=-0.5), [rs], [rs])
            if g_bc is None:
                if xo_bf is not None:
                    S.op("dve", lambda e: e.tensor_scalar(out=xo_bf[:T, :], in0=x[:T, :], scalar1=mv[:T, 0:1], scalar2=rs[:T, 0:1],
                                                          op0=ALU.subtract, op1=ALU.mult), [x, mv, rs], [xo_bf])
                S.op("dve", lambda e: e.tensor_scalar(out=xo[:T, :], in0=x[:T, :], scalar1=mv[:T, 0:1], scalar2=rs[:T, 0:1],
                                                      op0=ALU.subtract, op1=ALU.mult), [x, mv, rs], [xo])
                return
            S.op("dve", lambda e: e.tensor_scalar(out=xo[:T, :], in0=x[:T, :], scalar1=mv[:T, 0:1], scalar2=rs[:T, 0:1],
                                                  op0=ALU.subtract, op1=ALU.mult), [x, mv, rs], [xo])
            S.op("dve", lambda e: e.tensor_tensor(out=xo[:T, :], in0=xo[:T, :], in1=g_bc[:T, :], op=ALU.mult),
                 [xo, g_bc], [xo])
            if xo_bf is not None:
                S.op("dve", lambda e: e.tensor_tensor(out=xo_bf[:T, :], in0=xo[:T, :], in1=b_bc[:T, :], op=ALU.add),
                     [xo, b_bc], [xo_bf])
                S.op("pool", lambda e: e.tensor_tensor(out=xo[:T, :], in0=xo[:T, :], in1=b_bc[:T, :], op=ALU.add),
                     [xo, b_bc], [xo])
            else:
                S.op("dve", lambda e: e.tensor_tensor(out=xo[:T, :], in0=xo[:T, :], in1=b_bc[:T, :], op=ALU.add),
                     [xo, b_bc], [xo])

        eps_ln = S.sb("eps_ln", [128, 1]); eps_rms = S.sb("eps_rms", [128, 1])
        S.op("dve", lambda e: e.memset(eps_ln[:, :], LN_EPS), [], [eps_ln])
        S.op("dve", lambda e: e.memset(eps_rms[:, :], RMS_EPS), [], [eps_rms])
        lnw = (S.sb("ln_st", [128, 12]), S.sb("ln_mv", [128, 2]), S.sb("ln_rs", [128, 1]))

        with ExitStack() as sa:
            def A(name, shape, dt=F32):
                return S.sb(name, shape, dt, st=sa)

            def sigmoid_act(dst, src_ap, T, reads):
                S.op("act", lambda e: e.activation(out=dst[:T, :], in_=src_ap, func=AF.Exp, scale=-1.0), reads, [dst])
                S.op("act", lambda e: e.activation(out=dst[:T, :], in_=dst[:T, :], func=AF.Ln, bias=ones[:T, 0:1], scale=1.0),
                     [dst, ones], [dst])
                S.op("act", lambda e: e.activation(out=dst[:T, :], in_=dst[:T, :], func=AF.Exp, scale=-1.0), [dst], [dst])


            triP = A("triP", [128, 128]); triS = A("triS", [128, 128]); negBlk = A("negBlk", [128, 128])
            blkones = A("blkones", [128, 128]); blkind = A("blkind", [128, 16]); blkindT = A("blkindT", [16, 128])
            sel0 = A("sel0", [128, 16])
            GRP = {"qa": (0, 512), "ka": (512, 1024), "va": (1024, 1536), "oa": (1536, 2048), "g": (2048, 2056),
                   "qb": (2056, 2568), "fb": (2568, 3080), "ib": (3080, 3592), "gb": (3592, 4104)}
            w_in_v = w_in.rearrange("(kc p) n -> p kc n", p=128)
            w_in_g = {}
            for nm, lo_, hi_, keys in (("B", 2048, 3080, ("g", "qb", "fb")), ("A", 0, 2048, ("qa", "ka", "va", "oa")),
                                       ("C", 3080, 4104, ("ib", "gb"))):
                slab = A("w_in_" + nm, [128, 8, hi_ - lo_], BF16)
                S.dma("pool", slab[:, :, :], w_in_v[:, :, lo_:hi_], writes=[slab], sem_tile=slab)
                for key in keys:
                    c0_, c1_ = GRP[key]
                    w_in_g[key] = slab.view(lambda h, a=c0_ - lo_, b=c1_ - lo_: h[:, :, a:b])
            w_out_t = A("w_out_t", [128, 8, D], BF16)
            S.dma("pool", w_out_t[:, :, :], w_out.rearrange("(kc p) n -> p kc n", p=128), writes=[w_out_t],
                  sem_tile=w_out_t)
            bin_hl = A("bin_hl", [2, DIN], BF16)
            bout_hl = A("bout_hl", [2, D], BF16)

            embg = A("embg", [128, D]); embb = A("embb", [128, D])
            ga_bc = A("ga_bc", [128, 512]); gb_bc = A("gb_bc", [128, 512])
            lb_bc = A("lb_bc", [128, 512]); oml_bc = A("oml_bc", [128, 512]); bfg_bc = A("bfg_bc", [128, 4])
            ln1g_col = A("ln1g_col", [128, 8]); ln1b_col = A("ln1b_col", [128, 8])

            xin = A("xin", [128, D]); xnb = A("xnb", [128, D], BF16); xnT = A("xnT", [128, 8, 128], BF16)
            gates = A("gates", [128, 8]); g1s = A("g1s", [128, 12]); diag = A("diag", [128, 4, 128])
            qa_bf = A("qa_bf", [128, 512], BF16); qt_bf = A("qt_bf", [128, 512], BF16)
            t1 = A("t1", [128, 512]); Fb = A("Fb", [128, 512]); Eb = A("Eb", [128, 512]); Qb = A("Qb", [128, 512])
            nBc = A("nBc", [128, 4]); zero4 = A("zero4", [128, 4])

            def mkset(i):
                n = lambda x: "%s_%d" % (x, i)
                return dict(
                    xn=A(n("xn"), [128, D]), gq=A(n("gq"), [128, 8]), rmax=A(n("rmax"), [128, 4]), tot=A(n("tot"), [128, 4]),
                    va=A(n("va"), [128, 512]), ka_bf=A(n("ka_bf"), [128, 512], BF16), qkT=A(n("qkT"), [128, 8, 128], BF16),
                    SmT=A(n("SmT"), [128, 4, 128], BF16), og=A(n("og"), [128, 512]), kt_bf=A(n("kt_bf"), [128, 512], BF16),
                    iv_bf=A(n("iv_bf"), [128, 512], BF16), qkT2=A(n("qkT2"), [128, 8, 128], BF16),
                    ScT=A(n("ScT"), [128, 4, 128], BF16), gg=A(n("gg"), [128, 512]), eaT=A(n("eaT"), [128, 4, 16]))
            sets = [mkset(0), mkset(1)]
            Mc = A("Mc", [128, 4]); rr = A("rr", [128, 4]); alpha = A("alpha", [128, 4]); g2s = A("g2s", [128, 28])
            Vaug = A("Vaug", [128, 4, 129], BF16)
            hn = A("hn", [128, 4, 128]); mix = A("mix", [128, D], BF16); mixT = A("mixT", [128, 8, 128], BF16)
            yb = A("yb", [128, D]); x1hb = A("x1hb", [128, D], BF16); x1Tt = A("x1Tt", [128, 8, 128], BF16)
            Cst = A("Cst", [128, 4, 129]); Cp32 = A("Cp32", [128, 4, 129]); Cb = A("Cb", [128, 4, 129], BF16)
            Sst = A("Sst", [128, 4, 128]); Sb = A("Sb", [128, 4, 128], BF16); Stm = A("Stm", [128, 4, 128])
            nT_sb = A("nT_sb", [128, 64]); nld = A("nld", [64, 128]); alphaD = A("alphaD", [128, 64])
            selA = A("selA", [128, 16, 4])
            qTm = A("qTm", [128, 4, 128], BF16); Vm = A("Vm", [128, 4, 129], BF16); ivm = A("ivm", [128, 512], BF16)
            Cp32b = A("Cp32b", [128, 4, 129]); Cb2 = A("Cb2", [128, 4, 129], BF16); Sb2 = A("Sb2", [128, 4, 128], BF16)
            Cp32s = [Cp32, Cp32b]; Cbs = [Cb, Cb2]; Sbs = [Sb, Sb2]
            vq = lambda h: h[:, 0:512].rearrange("p (a b) -> p a b", a=4)
            qTmA = [qTm, qa_bf.view(vq)]
            qTmB = [qt_bf.view(vq), xnb.view(vq)]
            v129 = lambda h: h[:, 0:516].rearrange("p (a b) -> p a b", a=4)
            v128 = lambda h: h[:, 0:512].rearrange("p (a b) -> p a b", a=4)
            Cout1 = A("Cout1", [128, 4, 129])
            Cld = [xin.view(v128), sets[0]["xn"].view(v128), sets[0]["va"].view(v128), sets[0]["og"].view(v128)]
            nld4 = [A("nld4_%d" % i, [128, 4]) for i in range(4)]
            NCS, NSS = 4, 3
            Cout = [yb.view(v128), Cout1.view(lambda h: h[:, :, 0:128])]
            nout = [A("nout0", [128, 4]), A("nout1", [128, 4])]
            Sld = [t1.view(v128), Fb.view(v128), sets[0]["gg"].view(v128)]
            Sout = [Eb.view(v128), Qb.view(v128)]
            tmpS = xin.view(v128)
            tmp3 = Stm
            msl = A("msl", [16, 4]); mnew = A("mnew", [128, 4])
            lnw2 = (S.sb("ln_st2", [128, 12], st=sa), S.sb("ln_mv2", [128, 2], st=sa), S.sb("ln_rs2", [128, 1], st=sa))

            bias_rows(bin_hl, b_in, DIN, sets[1]["xn"], yb, x1hb)
            for t_, s_ in ((embg, ln_emb_g), (embb, ln_emb_b), (bfg_bc, b_fg)):
                S.dma("sp", t_[:, :], s_.partition_broadcast(128), writes=[t_], sem_tile=t_)
            ln_pre = [True]
            for t_, s_ in ((triP, c_triP), (blkones, c_blkones), (blkind, c_blkind)):
                S.dma("sp", t_[:, :], s_, writes=[t_], sem_tile=t_)
            lbl = yb.view(lambda h: h[:, :].rearrange("p (r c) -> p r c", r=2))
            for r in range(2):
                S.dma("sp", lbl[:, r, :], lb_log[r, :].partition_broadcast(128), writes=[lbl], sem_tile=lbl)
            S.op("dve", lambda e: e.tensor_tensor(out=lb_bc[:, :], in0=lbl[:, 0, :], in1=lbl[:, 1, :], op=ALU.subtract),
                 [lbl], [lb_bc])
            sigmoid_act(lb_bc, lb_bc[:, :], 128, [lb_bc])
            S.op("dve", lambda e: e.tensor_scalar(out=oml_bc[:, :], in0=lb_bc[:, :], scalar1=-1.0, scalar2=1.0,
                                                  op0=ALU.mult, op1=ALU.add), [lb_bc], [oml_bc])
            for t_, s_ in ((ga_bc, g_a), (gb_bc, g_b)):
                S.dma("sp", t_[:, :], s_.partition_broadcast(128), writes=[t_], sem_tile=t_)
            bias_rows(bout_hl, b_out, D, sets[1]["xn"], yb, x1hb)
            for t_, s_ in ((triS, c_triS), (negBlk, c_negBlk), (blkindT, c_blkindT), (sel0, c_sel0)):
                S.dma("sp", t_[:, :], s_, writes=[t_], sem_tile=t_)
            S.dma("sp", ln1g_col[:, :], ln1_g.rearrange("(kc p) -> p kc", p=128), writes=[ln1g_col], sem_tile=ln1g_col,
                  allow_slow_non_contiguous=True)
            S.dma("sp", ln1b_col[:, :], ln1_b.rearrange("(kc p) -> p kc", p=128), writes=[ln1b_col], sem_tile=ln1b_col,
                  allow_slow_non_contiguous=True)
            S.op("dve", lambda e: e.memset(nBc[:, :], 0.0), [], [nBc])
            S.op("dve", lambda e: e.memset(zero4[:, :], 0.0), [], [zero4])
            S.op("dve", lambda e: e.memset(Mc[:, :], 0.0), [], [Mc])
            S.op("pool", lambda e: e.memset(qTm[:, :, :], 0.0), [], [qTm])

            def proj(T, key):
                c0, c1 = GRP[key]
                p = pp()
                n = c1 - c0
                wg = w_in_g[key]
                for kc in range(8):
                    mm(p[:T, :n], xnT[:, kc, :T], wg[:, kc, :], kc == 0, False, [xnT, wg], [p])
                mm(p[:T, :n], onesB[0:2, :T], bin_hl[0:2, c0:c1], False, True, [onesB, bin_hl], [p])
                return p

            def transposes(src, T, n, dst_off, pst):
                for i in range(n):
                    S.op("pe", lambda e, i=i: e.transpose(pst[:, dst_off + i, :T], src[:T, i * 128:(i + 1) * 128],
                                                          identB[:T, :T]), [src, identB], [pst])

            def tile_info(c):
                sm = (c == 17)
                T = 16 if c == 0 else 128
                if c == 0:
                    src, col0 = meta, 0
                elif sm:
                    src, col0 = x_s, 2064
                else:
                    src, col0 = x_p[(c - 1) * 128:c * 128, :], 16 + (c - 1) * 128
                return sm, T, src, col0

            def ln_stage(c):
                sm_, T_, src_, _ = tile_info(c)
                xn_ = sets[c & 1]["xn"]
                S.dma("sp", xin[:T_, :], src_, writes=[xin], sem_tile=xin)
                layer_norm_rows(xin, T_, xn_, embg, embb, lnw, xo_bf=xnb)

            def stage1(c):
                sm, T, src, col0 = tile_info(c)
                J = 16 if sm else 1
                B = sets[c & 1]
                xn, gq, rmax, tot = B["xn"], B["gq"], B["rmax"], B["tot"]
                kt_bf, eaT = B["kt_bf"], B["eaT"]
                ka_bf, va, iv_bf, og, gg = B["ka_bf"], B["va"], B["iv_bf"], B["og"], B["gg"]
                qkT, SmT, qkT2, ScT = B["qkT"], B["SmT"], B["qkT2"], B["ScT"]
                tri = triS if sm else triP
                fab, e1, l1 = (g1s[:, 4 * i:4 * i + 4] for i in range(3))
                nBt, G = gq[:, 0:4], gq[:, 4:8]
                S.atomic += 1
                transposes(xnb, T, 8, 0, psT)
                S.atomic -= 1
                S.op("act", lambda e: e.copy(out=xnT[:, :, :T], in_=psT[:, :, :T]), [psT], [xnT])
                yield
                p = proj(T, "g")
                S.op("dve", lambda e: e.tensor_copy(out=gates[:T, :], in_=p[:T, 0:8]), [p], [gates])
                S.op("dve", lambda e: e.tensor_tensor(out=fab[:T], in0=gates[:T, 4:8], in1=bfg_bc[:T, :], op=ALU.add),
                     [gates, bfg_bc], [g1s])
                S.op("act", lambda e: e.activation(out=e1[:T], in_=fab[:T], func=AF.Exp, scale=-1.0), [g1s], [g1s])
                S.op("act", lambda e: e.activation(out=l1[:T], in_=e1[:T], func=AF.Ln, bias=ones[:T, 0:1], scale=1.0),
                     [g1s, ones], [g1s])
                yield
                p = proj(T, "fb")
                sigmoid_act(t1, p[:T, :], T, [p])
                S.op("dve", lambda e: e.tensor_tensor(out=t1[:T, :], in0=t1[:T, :], in1=oml_bc[:T, :], op=ALU.mult),
                     [t1, oml_bc], [t1])
                S.op("dve", lambda e: e.tensor_tensor(out=Fb[:T, :], in0=t1[:T, :], in1=lb_bc[:T, :], op=ALU.add),
                     [t1, lb_bc], [Fb])
                S.op("act", lambda e: e.activation(out=Fb[:T, :], in_=Fb[:T, :], func=AF.Ln), [Fb], [Fb])
                S.op("dve", lambda e: e.tensor_tensor(out=t1[:T, :], in0=oml_bc[:T, :], in1=t1[:T, :], op=ALU.subtract),
                     [t1, oml_bc], [t1])
                yield
                p3 = pp()
                mm(p3[:T, 0:4], tri[:T, :T], l1[:T], True, True, [tri, g1s], [p3])
                tot_l = blkones if sm else ones
                mm(p3[:128, 4:8], tot_l[:T, :128], l1[:T], True, True, [tot_l, g1s], [p3], skip=True)
                nBsrc = zero4 if sm else nBc
                S.op("dve", lambda e: e.tensor_tensor(out=nBt[:T], in0=p3[:T, 0:4], in1=nBsrc[:T, :], op=ALU.add),
                     [p3, nBsrc], [gq])
                S.op("dve", lambda e: e.tensor_copy(out=tot[:, :], in_=p3[:, 4:8]), [p3], [tot])
                if not sm:
                    S.op("dve", lambda e: e.tensor_tensor(out=nBc[:, :], in0=nBc[:, :], in1=tot[:, :], op=ALU.add),
                         [nBc, tot], [nBc])
                S.op("dve", lambda e: e.tensor_tensor(out=G[:T], in0=gates[:T, 0:4], in1=nBt[:T], op=ALU.add),
                     [gates, gq], [gq])
                S.op("dve", lambda e: e.tensor_tensor(out=diag[:T, :, :T], in0=bc(ident[:T, :T].unsqueeze(1), [T, 4, T]),
                                                      in1=bc(G[:T].unsqueeze(2), [T, 4, T]), op=ALU.mult),
                     [ident, gq], [diag])
                yield
                pq = proj(T, "qb")
                sigmoid_act(Qb, pq[:T, :], T, [pq])
                S.op("dve", lambda e: e.tensor_tensor(out=Qb[:T, :], in0=pq[:T, :], in1=Qb[:T, :], op=ALU.mult),
                     [pq, Qb], [Qb])
                yield
                pa = pp()
                mm(pa[:T, :], tri[:T, :T], Fb[:T, :], True, True, [tri, Fb], [pa])
                for h in range(4):
                    mm(psS[:, h, :T], ones[:T, :128], diag[:T, h, :T], True, True, [ones, diag], [psS])
                if sm:
                    S.op("dve", lambda e: e.tensor_tensor(out=tmpS[:, :, :], in0=psS[:, :, :],
                                                          in1=bc(negBlk[:, :].unsqueeze(1), [128, 4, 128]), op=ALU.add),
                         [psS, negBlk], [tmpS])
                    S.op("dve", lambda e: e.tensor_reduce(out=rmax[:, :], in_=tmpS[:, :, :], axis=AX.X, op=ALU.max),
                         [tmpS], [rmax])
                else:
                    S.op("dve", lambda e: e.tensor_reduce(out=rmax[:, :], in_=psS[:, :, :T], axis=AX.X, op=ALU.max),
                         [psS], [rmax])
                S.op("act", lambda e: e.activation(out=Eb[:T, :], in_=pa[:T, :], func=AF.Exp), [pa], [Eb])
                S.op("dve", lambda e: e.tensor_tensor(out=qt_bf[:T, :], in0=Qb[:T, :], in1=Eb[:T, :], op=ALU.mult),
                     [Qb, Eb], [qt_bf])
                S.op("act", lambda e: e.activation(out=Eb[:T, :], in_=pa[:T, :], func=AF.Exp, scale=-1.0), [pa], [Eb])
                S.op("dve", lambda e: e.tensor_tensor(out=kt_bf[:T, :], in0=t1[:T, :], in1=Eb[:T, :], op=ALU.mult),
                     [t1, Eb], [kt_bf])
                yield
                p = proj(T, "ka")
                S.op("act", lambda e: e.activation(out=ka_bf[:T, :], in_=p[:T, :], func=AF.Identity, scale=float(128 ** -0.5)),
                     [p], [ka_bf])
                yield
                p = proj(T, "qa")
                S.op("dve", lambda e: e.tensor_copy(out=qa_bf[:T, :], in_=p[:T, :]), [p], [qa_bf])
                yield
                pe_ = pp()
                rsel = blkind if sm else ones
                for h in range(4):
                    mm(pe_[:, h * J:(h + 1) * J], Fb[:T, h * 128:(h + 1) * 128], rsel[:T, 0:J], True, True,
                       [Fb, rsel], [pe_], skip=True)
                S.op("act", lambda e: e.activation(out=eaT[:, :, 0:J], in_=pe_[:, 0:4 * J].rearrange("p (h j) -> p h j", h=4),
                                                   func=AF.Exp), [pe_], [eaT])
                yield
                p = proj(T, "va")
                S.op("act", lambda e: e.copy(out=va[:T, :], in_=p[:T, :]), [p], [va])
                yield
                S.atomic += 1
                transposes(qa_bf, T, 4, 0, psT)
                transposes(ka_bf, T, 4, 4, psT)
                S.atomic -= 1
                S.op("act", lambda e: e.copy(out=qkT[:, :, :T], in_=psT[:, :, :T]), [psT], [qkT])
                yield
                p = proj(T, "ib")
                S.op("act", lambda e: e.copy(out=iv_bf[:T, :], in_=p[:T, :]), [p], [iv_bf])
                for h in range(4):
                    mm(psS[:T, h, :T], qkT[:, 4 + h, :T], qkT[:, h, :T], True, True, [qkT], [psS])
                S.op("dve", lambda e: e.tensor_tensor(out=SmT[:T, :, :T], in0=psS[:T, :, :T],
                                                      in1=bc(tri[:T, :T].unsqueeze(1), [T, 4, T]), op=ALU.mult),
                     [psS, tri], [SmT])
                yield
                S.atomic += 1
                transposes(qt_bf, T, 4, 0, psT)
                transposes(kt_bf, T, 4, 4, psT)
                S.atomic -= 1
                S.op("act", lambda e: e.copy(out=qkT2[:, :, :T], in_=psT[:, :, :T]), [psT], [qkT2])
                yield
                p = proj(T, "oa")
                sigmoid_act(og, p[:T, :], T, [p])
                S.op("pool", lambda e: e.tensor_tensor(out=og[:T, :], in0=og[:T, :], in1=ga_bc[:T, :], op=ALU.mult),
                     [og, ga_bc], [og])
                for h in range(4):
                    mm(psS[:T, h, :T], qkT2[:, 4 + h, :T], qkT2[:, h, :T], True, True, [qkT2], [psS])
                S.op("dve", lambda e: e.tensor_tensor(out=ScT[:T, :, :T], in0=psS[:T, :, :T],
                                                      in1=bc(tri[:T, :T].unsqueeze(1), [T, 4, T]), op=ALU.mult),
                     [psS, tri], [ScT])
                yield
                p = proj(T, "gb")
                sigmoid_act(gg, p[:T, :], T, [p])
                S.op("pool", lambda e: e.tensor_tensor(out=gg[:T, :], in0=gg[:T, :], in1=gb_bc[:T, :], op=ALU.mult),
                     [gg, gb_bc], [gg])
                yield
                if c + 1 < ntiles:
                    ln_stage(c + 1)

            def stage2(c):
                sm, T, src, col0 = tile_info(c)
                J = 16 if sm else 1
                has_state = (c > 0)
                B = sets[c & 1]
                xn, gq, rmax, tot = B["xn"], B["gq"], B["rmax"], B["tot"]
                ka_bf, va, iv_bf, og, gg = B["ka_bf"], B["va"], B["iv_bf"], B["og"], B["gg"]
                qkT, SmT, qkT2, ScT, kt_bf, eaT = B["qkT"], B["SmT"], B["qkT2"], B["ScT"], B["kt_bf"], B["eaT"]
                nBt, G = gq[:, 0:4], gq[:, 4:8]
                d1, u, thr, den, rec, ssq, rstd = (g2s[:, 4 * i:4 * i + 4] for i in range(7))
                psC0 = psC.view(lambda h: h[:, 0, :])
                psO0 = psO.view(lambda h: h[:, 0, :])
                def prescale():
                    S.op("dve", lambda e: e.tensor_scalar(out=yb[:T, :], in0=xn[:T, :], scalar1=float(ALPHA), scalar2=None,
                                                          op0=ALU.mult), [xn], [yb])

                def x1_tail(cp):
                    _, Tp, _, colp = tile_info(cp)
                    S.atomic += 1
                    transposes(x1hb, Tp, 8, 0, psT)
                    for kc in range(8):
                        if kc == 7:
                            S.atomic -= 1
                        S.op("act", lambda e, kc=kc: e.activation(out=x1Tt[:, kc, :Tp], in_=psT[:, kc, :Tp], func=AF.Identity,
                                                                  bias=ln1b_col[:, kc:kc + 1], scale=ln1g_col[:, kc:kc + 1]),
                             [psT, ln1b_col, ln1g_col], [x1Tt])
                    S.dma("pool", x1T_d[:, :, colp:colp + Tp], x1Tt[:, :, :Tp], reads=[x1Tt], writes=[x1T_d], sem_tile=x1Tt)

                if not sm:
                    prescale()
                if c > 0:
                    x1_tail(c - 1)
                if sm:
                    S.dma("sp", msl[:, :], m_s, writes=[msl], sem_tile=msl)
                    p2 = psC0
                    mm(p2[:128, 0:4], blkindT[:16, :128], msl[:16, :4], True, True, [blkindT, msl], [p2])
                    S.op("dve", lambda e: e.tensor_copy(out=Mc[:, :], in_=p2[:, 0:4]), [p2], [Mc])
                S.op("dve", lambda e: e.tensor_tensor(out=rr[:, :], in0=rmax[:, :], in1=Mc[:, :], op=ALU.max), [rmax, Mc], [rr])
                S.op("dve", lambda e: e.tensor_tensor(out=d1[:T], in0=G[:T], in1=rr[:T, :], op=ALU.subtract), [gq, rr], [g2s])
                S.op("act", lambda e: e.activation(out=u[:T], in_=d1[:T], func=AF.Exp), [g2s], [g2s])
                S.op("dve", lambda e: e.tensor_tensor(out=d1[:T], in0=nBt[:T], in1=rr[:T, :], op=ALU.subtract), [gq, rr], [g2s])
                S.op("act", lambda e: e.activation(out=thr[:T], in_=d1[:T], func=AF.Exp), [g2s], [g2s])
                S.op("dve", lambda e: e.tensor_tensor(out=alpha[:, :], in0=Mc[:, :], in1=rr[:, :], op=ALU.subtract),
                     [Mc, rr], [alpha])
                S.op("act", lambda e: e.activation(out=alpha[:, :], in_=alpha[:, :], func=AF.Exp), [alpha], [alpha])
                if sm:
                    S.op("dve", lambda e: e.tensor_tensor(out=mnew[:, :], in0=rr[:, :], in1=tot[:, :], op=ALU.subtract),
                         [rr, tot], [mnew])
                    S.dma("pool", m_so, mnew[0:128:8, :], reads=[mnew], sem_tile=mnew, final=True)
                    S.op("dve", lambda e: e.tensor_tensor(out=selA[:, :, :], in0=bc(sel0[:, :].unsqueeze(2), [128, 16, 4]),
                                                          in1=bc(alpha[:, :].unsqueeze(1), [128, 16, 4]), op=ALU.mult),
                         [sel0, alpha], [selA])
                    p4 = psC0
                    mm(p4[:, 0:64], ones[:, :], selA[:, :, :], True, True, [ones, selA], [p4])
                    S.op("dve", lambda e: e.tensor_copy(out=alphaD[:, :], in_=p4[:, 0:64]), [p4], [alphaD])
                else:
                    S.op("dve", lambda e: e.tensor_copy(out=Mc[:, :], in_=rr[:, :]), [rr], [Mc])
                yield
                S.op("dve", lambda e: e.tensor_tensor(out=Vaug[:T, :, 0:128], in0=va[:T, :].rearrange("p (h e) -> p h e", h=4),
                                                      in1=bc(u[:T].unsqueeze(2), [T, 4, 128]), op=ALU.mult),
                     [va, g2s], [Vaug])
                S.op("dve", lambda e: e.tensor_copy(out=Vaug[:T, :, 128], in_=u[:T]), [g2s, Vaug], [Vaug])

                yield
                def oreg(h, n=129):
                    return psO[:T, h // 2, (h % 2) * 129:(h % 2) * 129 + n]

                def creg(h):
                    return psC[:, h // 2, (h % 2) * 129:(h % 2) * 129 + 129]

                started = [False, False]

                def omm(h, lhsT, rhs, reads, last):
                    b = h // 2
                    mm(oreg(h), lhsT, rhs, not started[b], last, reads, [psO], skip=True)
                    started[b] = True

                def rms_gate(src, gate, dst_lo):
                    S.op("dve", lambda e: e.tensor_tensor(out=tmp3[:T, :, :], in0=src[:T, :, :], in1=src[:T, :, :],
                                                           op=ALU.mult), [src], [tmp3])
                    S.op("dve", lambda e: e.tensor_reduce(out=ssq[:T], in_=tmp3[:T, :, :], axis=AX.X, op=ALU.add),
                         [tmp3], [g2s])
                    S.op("act", lambda e: e.activation(out=rstd[:T], in_=ssq[:T], func=AF.Ln, bias=eps_rms[:T, :],
                                                       scale=1.0 / 128.0), [g2s, eps_rms], [g2s])
                    S.op("act", lambda e: e.activation(out=rstd[:T], in_=rstd[:T], func=AF.Exp, scale=-0.5), [g2s], [g2s])
                    S.op("dve", lambda e: e.tensor_tensor(out=src[:T, :, :], in0=src[:T, :, :],
                                                          in1=bc(rstd[:T].unsqueeze(2), [T, 4, 128]), op=ALU.mult),
                         [src, g2s], [src])
                    S.op("dve", lambda e: e.tensor_tensor(out=mix[:T, dst_lo:dst_lo + 512].rearrange("p (h e) -> p h e", h=4),
                                                          in0=src[:T, :, :], in1=gate[:T, :].rearrange("p (h e) -> p h e", h=4),
                                                          op=ALU.mult), [src, gate], [mix])

                def load_C(j):
                    k = j % NCS
                    S.dma("sp", Cld[k][:, :, :], C_s[j], writes=[Cld[k]], sem_tile=Cld[k])
                    S.dma("sp", nld4[k][:, :], n_s[j], writes=[nld4[k]], sem_tile=nld4[k])

                def mloop():
                    def X(j):
                        Cp, Cbb = Cp32s[j & 1], Cbs[j & 1]
                        if sm:
                            Csrc, al, al_r = Cld[j % NCS], alphaD[:, 4 * j:4 * j + 4], [alphaD]
                            nsrc = nld4[j % NCS]
                            S.op("dve", lambda e: e.tensor_tensor(out=Cp[:, :, 0:128], in0=Csrc[:, :, :],
                                                                  in1=bc(al.unsqueeze(2), [128, 4, 128]), op=ALU.mult),
                                 [Csrc] + al_r, [Cp])
                            S.op("dve", lambda e: e.tensor_tensor(out=Cp[:, :, 128], in0=nsrc[:, :], in1=al, op=ALU.mult),
                                 [nsrc, Cp] + al_r, [Cp])
                        else:
                            Csrc, al, al_r = Cst, alpha[:, :], [alpha]
                            if not has_state:
                                return
                            S.op("dve", lambda e: e.tensor_tensor(out=Cp[:, :, :], in0=Csrc[:, :, :],
                                                                  in1=bc(al.unsqueeze(2), [128, 4, 129]), op=ALU.mult),
                                 [Csrc] + al_r, [Cp])
                        S.op("act", lambda e: e.copy(out=Cbb[:, :, :], in_=Cp[:, :, :]), [Cp], [Cbb])
                        if sm:
                            qm = qTmA[j & 1]
                            if j > 1:
                                S.op("act", lambda e: e.mul(out=qm[:, :, 8 * (j - 2):8 * (j - 1)],
                                                            in_=qkT[:, 0:4, 8 * (j - 2):8 * (j - 1)], mul=0.0), [qkT], [qm])
                            S.op("act", lambda e: e.copy(out=qm[:, :, 8 * j:8 * j + 8],
                                                         in_=qkT[:, 0:4, 8 * j:8 * j + 8]), [qkT], [qm])
                            qop, qr = qm, [qm]
                        else:
                            qop, qr = qkT, [qkT]
                        for h in range(4):
                            omm(h, qop[:, h, :T], Cbb[:, h, :], qr + [Cbb], False)

                    def Y(j):
                        Cp = Cp32s[j & 1]
                        Cdst = Cout[j & 1] if sm else Cst
                        if sm:
                            S.op("dve", lambda e: e.tensor_scalar(out=Vm[:, :, :], in0=Vaug[:, :, :], scalar1=blkind[:, j:j + 1],
                                                                  scalar2=None, op0=ALU.mult), [Vaug, blkind], [Vm])
                            Vop = Vm
                        else:
                            Vop = Vaug
                        for h in range(4):
                            mm(creg(h), ka_bf[:T, h * 128:(h + 1) * 128], Vop[:T, h, :], True, True, [ka_bf, Vop], [psC])
                        cview = psC[:, :, 0:258].rearrange("p g (i e) -> p g i e", i=2)
                        if sm:
                            nd_ = nout[j & 1]
                            S.op("dve", lambda e: e.tensor_tensor(out=Cdst[:, :, :].rearrange("p (g i) e -> p g i e", g=2),
                                                                  in0=Cp[:, :, 0:128].rearrange("p (g i) e -> p g i e", g=2),
                                                                  in1=cview[:, :, :, 0:128], op=ALU.add), [Cp, psC], [Cdst])
                            S.op("dve", lambda e: e.tensor_tensor(out=nd_[:, :].rearrange("p (g i) -> p g i", g=2),
                                                                  in0=Cp[:, :, 128].rearrange("p (g i) -> p g i", g=2),
                                                                  in1=cview[:, :, :, 128], op=ALU.add), [Cp, psC], [nd_])
                            S.dma("pool", C_so[j], Cdst[:, :, :], reads=[Cdst], sem_tile=Cdst, final=True)
                            S.dma("pool", n_so[j], nd_[:, :], reads=[nd_], sem_tile=nd_, final=True)
                        elif has_state:
                            S.op("dve", lambda e: e.tensor_tensor(out=Cdst[:, :, :].rearrange("p (g i) e -> p g i e", g=2),
                                                                  in0=Cp[:, :, :].rearrange("p (g i) e -> p g i e", g=2),
                                                                  in1=cview, op=ALU.add), [Cp, psC], [Cdst])
                        else:
                            S.op("dve", lambda e: e.tensor_copy(out=Cdst[:, :, :].rearrange("p (g i) e -> p g i e", g=2),
                                                                in_=cview), [psC], [Cdst])

                    if not sm:
                        yield
                        X(0)
                        Y(0)
                        return
                    for j0 in range(NCS - 1):
                        load_C(j0)
                    for qb_ in qTmA:
                        S.op("pool", lambda e: e.memset(qb_[:, :, :], 0.0), [], [qb_])
                    X(0)
                    for j in range(J):
                        yield
                        if j + NCS - 1 < J:
                            load_C(j + NCS - 1)
                        if j + 1 < J:
                            X(j + 1)
                        Y(j)

                if sm:
                    pOb_t, pCb_t, Stm_ = psP[0], psP[1], diag
                    psOb = psP[0][:, :].rearrange("p (h e) -> p h e", h=4)
                    psCb = psP[1][:, :].rearrange("p (h e) -> p h e", h=4)
                else:
                    pOb_t, pCb_t, Stm_ = psO, psC, Stm
                    psOb = psO[:, 0, :].rearrange("p (h e) -> p h e", h=4)
                    psCb = psC[:, 0, :].rearrange("p (h e) -> p h e", h=4)
                startedB = [False]

                def obmm(h, lhsT, rhs, reads, last):
                    mm(psOb[:T, h, :], lhsT, rhs, not startedB[0], last, reads, [pOb_t], skip=True)
                    startedB[0] = True

                def load_S(j):
                    k = j % NSS
                    S.dma("sp", Sld[k][:, :, :], S_s[j], writes=[Sld[k]], sem_tile=Sld[k])

                def hloop():
                    def X(j):
                        if sm:
                            Ssrc, Sbb = Sld[j % NSS], Sbs[j & 1]
                            S.op("act", lambda e: e.copy(out=Sbb[:, :, :], in_=Ssrc[:, :, :]), [Ssrc], [Sbb])
                            qm = qTmB[j & 1]
                            if j > 1:
                                S.op("act", lambda e: e.mul(out=qm[:, :, 8 * (j - 2):8 * (j - 1)],
                                                            in_=qkT2[:, 0:4, 8 * (j - 2):8 * (j - 1)], mul=0.0), [qkT2], [qm])
                            S.op("act", lambda e: e.copy(out=qm[:, :, 8 * j:8 * j + 8],
                                                         in_=qkT2[:, 0:4, 8 * j:8 * j + 8]), [qkT2], [qm])
                            qop, qr = qm, [qm]
                        else:
                            Sbb = Sb
                            qop, qr = qkT2, [qkT2]
                        if has_state:
                            for h in range(4):
                                obmm(h, qop[:, h, :T], Sbb[:, h, :], qr + [Sbb], False)

                    def Y(j):
                        if sm:
                            Ssrc, Sdst = Sld[j % NSS], Sout[j & 1]
                            S.op("dve", lambda e: e.tensor_scalar(out=ivm[:, :], in0=iv_bf[:, :], scalar1=blkind[:, j:j + 1],
                                                                  scalar2=None, op0=ALU.mult), [iv_bf, blkind], [ivm])
                            ivop = ivm
                        else:
                            Ssrc, Sdst = Sst, Sst
                            ivop = iv_bf
                        for h in range(4):
                            mm(psCb[:, h, :], kt_bf[:T, h * 128:(h + 1) * 128], ivop[:T, h * 128:(h + 1) * 128], True, True,
                               [kt_bf, ivop], [pCb_t])
                        ea_j = bc(eaT[:, :, j:j + 1], [128, 4, 128])
                        if has_state:
                            S.op("dve", lambda e: e.tensor_tensor(out=Stm_[:, :, :], in0=Ssrc[:, :, :], in1=psCb, op=ALU.add),
                                 [Ssrc, pCb_t], [Stm_])
                            S.op("dve", lambda e: e.tensor_tensor(out=Sdst[:, :, :], in0=Stm_[:, :, :], in1=ea_j, op=ALU.mult),
                                 [Stm_, eaT], [Sdst])
                        else:
                            S.op("dve", lambda e: e.tensor_tensor(out=Sdst[:, :, :], in0=psCb, in1=ea_j, op=ALU.mult),
                                 [pCb_t, eaT], [Sdst])
                        if sm:
                            S.dma("pool", S_so[j], Sdst[:, :, :], reads=[Sdst], sem_tile=Sdst, final=True)
                        else:
                            S.op("act", lambda e: e.copy(out=Sb[:, :, :], in_=Sst[:, :, :]), [Sst], [Sb])

                    if not sm:
                        yield
                        X(0)
                        Y(0)
                        return
                    for j0 in range(NSS):
                        load_S(j0)
                    for qb_ in qTmB:
                        S.op("pool", lambda e: e.memset(qb_[:, :, :], 0.0), [], [qb_])
                    X(0)
                    for j in range(J):
                        yield
                        if j + 1 < J:
                            X(j + 1)
                        Y(j)
                        if j + NSS < J:
                            load_S(j + NSS)

                def mpost():
                    for h in range(4):
                        omm(h, SmT[:T, h, :T], Vaug[:T, h, :], [SmT, Vaug], True)
                    yield
                    oden = psO[:T, :, 0:258].rearrange("p g (i e) -> p g i e", i=2)[:, :, :, 128]
                    onum = psO[:T, :, 0:258].rearrange("p g (i e) -> p g i e", i=2)[:, :, :, 0:128]
                    S.op("dve", lambda e: e.tensor_tensor(out=rec[:T].rearrange("p (g i) -> p g i", g=2), in0=oden,
                                                          in1=thr[:T].rearrange("p (g i) -> p g i", g=2), op=ALU.max),
                         [psO, g2s], [g2s])
                    S.op("dve", lambda e: e.scalar_tensor_tensor(out=den[:T].rearrange("p (g i) -> p g i", g=2), in0=oden,
                                                                 scalar=-1.0, in1=rec[:T].rearrange("p (g i) -> p g i", g=2),
                                                                 op0=ALU.mult, op1=ALU.max), [psO, g2s], [g2s])
                    S.op("dve", lambda e: e.reciprocal(out=rec[:T], in_=den[:T]), [g2s], [g2s])
                    S.op("dve", lambda e: e.tensor_tensor(out=hn[:T, :, :].rearrange("p (g i) e -> p g i e", g=2), in0=onum,
                                                          in1=bc(rec[:T].rearrange("p (g i) -> p g i", g=2).unsqueeze(3),
                                                                 [T, 2, 2, 128]), op=ALU.mult), [psO, g2s], [hn])
                    yield
                    rms_gate(hn, og, 0)

                def hpost():
                    for h in range(4):
                        obmm(h, ScT[:T, h, :T], iv_bf[:T, h * 128:(h + 1) * 128], [ScT, iv_bf], True)
                    S.op("act", lambda e: e.copy(out=hn[:T, :, :], in_=psOb[:T, :, :]), [pOb_t], [hn])
                    yield
                    rms_gate(hn, gg, 512)

                def il(*gens):
                    gens = list(gens)
                    while gens:
                        for g in list(gens):
                            try:
                                next(g)
                                yield
                            except StopIteration:
                                gens.remove(g)

                if sm:
                    S.run_streams([exhaust(mloop()), exhaust(hloop())], quanta=SQ)
                    yield from mpost()
                    yield
                    yield from hpost()
                else:
                    yield from mloop()
                    yield
                    yield from mpost()
                    yield
                    yield from hloop()
                    yield
                    yield from hpost()
                yield
                if sm:
                    prescale()
                S.atomic += 1
                transposes(mix, T, 8, 0, psT)
                S.atomic -= 1
                S.op("act", lambda e: e.copy(out=mixT[:, :, :T], in_=psT[:, :, :T]), [psT], [mixT])
                for half in range(2):
                    c0 = half * 512
                    p = (psC0, psO0)[half]
                    for kc in range(8):
                        mm(p[:T, :], mixT[:, kc, :T], w_out_t[:, kc, c0:c0 + 512], kc == 0, False, [mixT, w_out_t], [p])
                    mm(p[:T, :], onesB[0:2, :T], bout_hl[0:2, c0:c0 + 512], False, True, [onesB, bout_hl], [p])
                    S.op("dve", lambda e, p=p, c0=c0: e.tensor_tensor(out=yb[:T, c0:c0 + 512], in0=yb[:T, c0:c0 + 512],
                                                                     in1=p[:T, :], op=ALU.add), [yb, p], [yb])
                yield
                layer_norm_rows(yb, T, yb, None, None, lnw2, xo_bf=x1hb)
                S.dma("pool", x1h_d[c * 128:c * 128 + T, :], yb[:T, :], reads=[yb], writes=[x1h_d], sem_tile=yb)
                if c == ntiles - 1:
                    x1_tail(c)

                yield
                if c == 16:
                    S.dma("pool", C_po.rearrange("h d e -> d h e"), Cst[:, :, 0:128], reads=[Cst], sem_tile=Cst, final=True)
                    S.dma("pool", n_po.rearrange("h d -> d h"), Cst[:, :, 128], reads=[Cst], sem_tile=Cst, final=True,
                          allow_slow_non_contiguous=True)
                    S.dma("pool", S_po.rearrange("h d e -> d h e"), Sst[:, :, :], reads=[Sst], sem_tile=Sst, final=True)
                    S.op("dve", lambda e: e.tensor_tensor(out=mnew[:, :], in0=Mc[:, :], in1=nBc[:, :], op=ALU.subtract),
                         [Mc, nBc], [mnew])
                    S.dma("pool", m_po, mnew[0:1, :], reads=[mnew], sem_tile=mnew, final=True)

            ntiles = NT if STAGE >= 2 else 2
            def run_interleaved(*gens):
                gens = [g for g in gens if g is not None]
                while gens:
                    for g in list(gens):
                        try:
                            next(g)
                        except StopIteration:
                            gens.remove(g)

            def exhaust(gen):
                def f():
                    for _ in gen:
                        pass
                return f

            ln_stage(0)
            run_interleaved(stage1(0))
            for c in range(1, ntiles):
                S.run_streams([exhaust(stage1(c)), exhaust(stage2(c - 1))], quanta=QUANTA)
            run_interleaved(stage2(ntiles - 1))
            print("phase A sbuf bytes remaining:", nc.sbuf_bytes_remaining)

        spa.close()
        S.barrier()

        with ExitStack() as sb_:
            def Bf(name, shape, dt=F32):
                return S.sb(name, shape, dt, st=sb_)

            pu = [S.ps("pu0", [128, 512], st=sb_), S.ps("pu1", [128, 512], st=sb_)]
            pg = [S.ps("pg0", [128, 512], st=sb_), S.ps("pg1", [128, 512], st=sb_)]
            pc = [S.ps("pc0", [128, 512], st=sb_), S.ps("pc1", [128, 512], st=sb_)]
            pm = S.ps("pm", [128, 512], st=sb_)

            w_dn_t = Bf("w_dn_t", [128, NFC, D], BF16)
            ln1g_bc = Bf("ln1g_bc", [128, D]); ln1b_bc = Bf("ln1b_bc", [128, D])
            ln2g_bc = Bf("ln2g_bc", [128, D]); ln2b_bc = Bf("ln2b_bc", [128, D])
            x1Th = Bf("x1Th", [128, 8, 1152], BF16)
            S.dma("sp", x1Th[:, :, 0:1040], x1T_d[:, :, 0:1040], reads=[x1T_d], writes=[x1Th], sem_tile=x1Th)
            x1l = [Bf("x1l0", [128, D]), Bf("x1l1", [128, D])]
            zb = [Bf("zb0", [128, D]), Bf("zb1", [128, D])]
            bdn_hl = Bf("bdn_hl", [2, D], BF16)

            prm_ld = Bf("prm_ld", [NFC, 6, 128]); prm = Bf("prm", [128, 6, NFC])
            srcs = [b_up[0:DFF], b_up[DFF:2 * DFF], w_conv[0, :], w_conv[1, :], w_conv[2, :], b_conv]
            for k, s_ in enumerate(srcs):
                S.dma("sp", prm_ld[:, k, :], s_.rearrange("(fc p) -> fc p", p=128), writes=[prm_ld], sem_tile=prm_ld)
            for k in range(6):
                S.op("pe", lambda e, k=k: e.transpose(pm[:, k * NFC:(k + 1) * NFC], prm_ld[:NFC, k, :], ident[:NFC, :NFC]),
                     [prm_ld, ident], [pm])
            S.op("dve", lambda e: e.tensor_copy(out=prm[:, :, :], in_=pm[:, 0:6 * NFC].rearrange("p (k f) -> p k f", k=6)),
                 [pm], [prm])
            cvbuf = Bf("cvbuf", [34, DFF]); cst = Bf("cst", [128, NFC, 32])
            S.dma("sp", cvbuf[0:32, :], cv_s, writes=[cvbuf], sem_tile=cvbuf)
            for f0 in range(0, NFC, 11):
                for fc in range(f0, f0 + 11):
                    S.op("pe", lambda e, fc=fc, f0=f0: e.transpose(pm[:, (fc - f0) * 32:(fc - f0) * 32 + 32],
                                                                  cvbuf[0:32, fc * 128:(fc + 1) * 128], ident[:32, :32]),
                         [cvbuf, ident], [pm])
                S.op("dve", lambda e, f0=f0: e.tensor_copy(out=cst[:, f0:f0 + 11, :],
                                                           in_=pm[:, 0:352].rearrange("p (f r) -> p f r", f=11)),
                     [pm], [cst])
            ulast = Bf("ulast", [128, NFC, 34]); ucar = Bf("ucar", [128, NFC, 2])
            hbuf = Bf("hbuf", [128, NFC, 1152], BF16)
            bias_rows(bdn_hl, b_down, D, zb[0], zb[1], hbuf.view(lambda h: h[:, 0, :]))
            for t_, s_ in ((ln1g_bc, ln1_g), (ln1b_bc, ln1_b), (ln2g_bc, ln2_g), (ln2b_bc, ln2_b)):
                S.dma("sp", t_[:, :], s_.partition_broadcast(128), writes=[t_], sem_tile=t_)
            wub = [Bf("wub0", [128, 8, 256], BF16), Bf("wub1", [128, 8, 256], BF16)]
            ub = [Bf("ub0", [128, 514]), Bf("ub1", [128, 514])]
            ubs = Bf("ubs", [128, 16, 10])
            cvb = [Bf("cvb0", [128, 512]), Bf("cvb1", [128, 512])]
            slb = [Bf("slb0", [128, 512]), Bf("slb1", [128, 512])]
            w_up_v = w_up.rearrange("(kc p) n -> p kc n", p=128)

            def load_wub(fc, slot):
                S.dma("pool", wub[slot][:, :, 0:128], w_up_v[:, :, fc * 128:(fc + 1) * 128], writes=[wub[slot]],
                      sem_tile=wub[slot])
                S.dma("pool", wub[slot][:, :, 128:256], w_up_v[:, :, DFF + fc * 128:DFF + (fc + 1) * 128],
                      writes=[wub[slot]], sem_tile=wub[slot])

            HALVES = [
                dict(lo=0, hi=1040, groups=[(0, 347, "p"), (347, 694, "p"), (694, 1040, "p")], tiles=list(range(1, 9))),
                dict(lo=1040, hi=2192, groups=[(1040, 1552, "p"), (1552, 2064, "p"), (2064, 2192, "s")],
                     tiles=list(range(9, 18))),
            ]
            gi = [0]
            wslot = [0]
            nhalves = 2 if STAGE >= 3 else 0
            for hf_i in range(nhalves):
                hf = HALVES[hf_i]
                lo, hi = hf["lo"], hf["hi"]
                if hf_i > 0:
                    S.dma("sp", x1Th[:, :, 0:hi - lo], x1T_d[:, :, lo:hi], reads=[x1T_d], writes=[x1Th], sem_tile=x1Th)
                load_wub(0, wslot[0])
                if hf_i == 0:
                    w_dn_v = w_down.rearrange("(fc p) n -> p fc n", p=128)
                    for f0 in range(0, NFC, 11):
                        S.dma("pool", w_dn_t[:, f0:f0 + 11, :], w_dn_v[:, f0:f0 + 11, :], writes=[w_dn_t], sem_tile=w_dn_t)
                pend = [None]
                gsel = [0]
                def grp(fc, c0, c1, kind, W, s2, pn):
                    n = c1 - c0
                    l0 = c0 - lo
                    P_ = lambda k: prm[:, k, fc:fc + 1]
                    s3 = gsel[0] % 3
                    gsel[0] += 1
                    U, Gp = (pu[0], pu[1], pc[0])[s3], (pg[0], pg[1], pc[1])[s3]
                    CV, SL, UB = cvb[s2], slb[s2], ub[s2]
                    for kc in range(8):
                        mm(U[:, :n], W[:, kc, 0:128], x1Th[:, kc, l0:l0 + n], kc == 0, kc == 7, [W, x1Th], [U])
                    for kc in range(8):
                        mm(Gp[:, :n], W[:, kc, 128:256], x1Th[:, kc, l0:l0 + n], kc == 0, kc == 7, [W, x1Th], [Gp])
                    if kind == "p":
                        UBp = ub[s2 ^ 1]
                        if c0 == 0:
                            S.op("dve", lambda e: e.memset(UB[:, 0:2], 0.0), [], [UB])
                        elif c0 == 1040:
                            S.op("dve", lambda e: e.tensor_copy(out=UB[:, 0:2], in_=ucar[:, fc, :]), [ucar], [UB])
                        else:
                            npv = pn
                            S.op("dve", lambda e: e.tensor_copy(out=UB[:, 0:2], in_=UBp[:, npv:npv + 2]), [UBp], [UB])
                        S.op("act", lambda e: e.activation(out=UB[:, 2:2 + n], in_=U[:, :n], func=AF.Identity,
                                                           bias=P_(0), scale=1.0), [U, prm], [UB])
                        S.op("act", lambda e: e.activation(out=CV[:, :n], in_=UB[:, 2:2 + n], func=AF.Identity,
                                                           bias=P_(5), scale=P_(4)), [UB, prm], [CV])
                        S.op("dve", lambda e: e.scalar_tensor_tensor(out=CV[:, :n], in0=UB[:, 1:1 + n], scalar=P_(3),
                                                                     in1=CV[:, :n], op0=ALU.mult, op1=ALU.add),
                             [UB, prm, CV], [CV])
                        S.op("dve", lambda e: e.scalar_tensor_tensor(out=CV[:, :n], in0=UB[:, 0:n], scalar=P_(2),
                                                                     in1=CV[:, :n], op0=ALU.mult, op1=ALU.add),
                             [UB, prm, CV], [CV])
                        if c1 == 1040:
                            S.op("dve", lambda e: e.tensor_copy(out=ucar[:, fc, :], in_=UB[:, n:n + 2]), [UB], [ucar])
                        if c1 == 2064:
                            S.op("dve", lambda e: e.tensor_copy(out=ulast[:, fc, 32:34], in_=UB[:, n:n + 2]), [UB], [ulast])
                        yield
                        S.op("act", lambda e: e.activation(out=SL[:, :n], in_=CV[:, :n], func=AF.Silu), [CV], [SL])
                        S.op("dve", lambda e: e.scalar_tensor_tensor(out=hbuf[:, fc, l0:l0 + n], in0=Gp[:, :n], scalar=P_(1),
                                                                     in1=SL[:, :n], op0=ALU.add, op1=ALU.mult),
                             [Gp, prm, SL], [hbuf])
                    else:
                        v3 = lambda ap: ap.rearrange("p (j t) -> p j t", j=16)
                        S.op("dve", lambda e: e.tensor_copy(out=ubs[:, :, 0:2],
                                                            in_=cst[:, fc, :].rearrange("p (j r) -> p j r", j=16)),
                             [cst], [ubs])
                        S.op("act", lambda e: e.activation(out=ubs[:, :, 2:10], in_=v3(U[:, :128]), func=AF.Identity,
                                                           bias=P_(0), scale=1.0), [U, prm], [ubs])
                        S.op("act", lambda e: e.activation(out=v3(CV[:, :128]), in_=ubs[:, :, 2:10], func=AF.Identity,
                                                           bias=P_(5), scale=P_(4)), [ubs, prm], [CV])
                        S.op("dve", lambda e: e.scalar_tensor_tensor(out=v3(CV[:, :128]), in0=ubs[:, :, 1:9], scalar=P_(3),
                                                                     in1=v3(CV[:, :128]), op0=ALU.mult, op1=ALU.add),
                             [ubs, prm, CV], [CV])
                        S.op("dve", lambda e: e.scalar_tensor_tensor(out=v3(CV[:, :128]), in0=ubs[:, :, 0:8], scalar=P_(2),
                                                                     in1=v3(CV[:, :128]), op0=ALU.mult, op1=ALU.add),
                             [ubs, prm, CV], [CV])
                        S.op("dve", lambda e: e.tensor_copy(out=ulast[:, fc, 0:32].rearrange("p (j r) -> p j r", j=16),
                                                            in_=ubs[:, :, 8:10]), [ubs], [ulast])
                        yield
                        S.op("act", lambda e: e.activation(out=SL[:, :128], in_=CV[:, :128], func=AF.Silu), [CV], [SL])
                        S.op("dve", lambda e: e.scalar_tensor_tensor(out=hbuf[:, fc, l0:l0 + 128], in0=Gp[:, :128],
                                                                     scalar=P_(1), in1=SL[:, :128], op0=ALU.add,
                                                                     op1=ALU.mult), [Gp, prm, SL], [hbuf])

                def fin(g):
                    for _ in g:
                        pass

                for fc in range(NFC):
                    cur = wslot[0]
                    if fc + 1 < NFC:
                        load_wub(fc + 1, cur ^ 1)
                    W = wub[cur]
                    pn = 0
                    for (c0, c1, kind) in hf["groups"]:
                        s2 = gi[0] & 1
                        gi[0] += 1
                        g = grp(fc, c0, c1, kind, W, s2, pn)
                        next(g)
                        if pend[0] is not None:
                            fin(pend[0])
                        pend[0] = g
                        pn = c1 - c0
                    wslot[0] ^= 1

                if pend[0] is not None:
                    fin(pend[0])
                    pend[0] = None
                tiles = hf["tiles"]

                def load_x1(idx):
                    c = tiles[idx]
                    S.dma("sp", x1l[idx & 1][:, :], x1h_d[c * 128:(c + 1) * 128, :], reads=[x1h_d], writes=[x1l[idx & 1]],
                          sem_tile=x1l[idx & 1])

                load_x1(0)
                for idx, c in enumerate(tiles):
                    if idx + 1 < len(tiles):
                        load_x1(idx + 1)
                    col0 = 2064 if c == 17 else 16 + (c - 1) * 128
                    l0 = col0 - lo
                    X, Z = x1l[idx & 1], zb[idx & 1]
                    S.op("pool", lambda e: e.tensor_tensor(out=X[:, :], in0=X[:, :], in1=ln1g_bc[:, :], op=ALU.mult),
                         [X, ln1g_bc], [X])
                    S.op("pool", lambda e: e.tensor_tensor(out=X[:, :], in0=X[:, :], in1=ln1b_bc[:, :], op=ALU.add),
                         [X, ln1b_bc], [X])
                    for hh in range(2):
                        p = (pc, pu, pg)[idx % 3][hh]
                        for fc in range(NFC):
                            mm(p[:, :], hbuf[:, fc, l0:l0 + 128], w_dn_t[:, fc, hh * 512:(hh + 1) * 512], fc == 0, False,
                               [hbuf, w_dn_t], [p])
                        mm(p[:, :], onesB[0:2, :128], bdn_hl[0:2, hh * 512:(hh + 1) * 512], False, True, [onesB, bdn_hl], [p])
                        S.op("dve", lambda e, p=p, hh=hh: e.scalar_tensor_tensor(out=Z[:, hh * 512:(hh + 1) * 512],
                                                                                in0=X[:, hh * 512:(hh + 1) * 512],
                                                                                scalar=float(ALPHA), in1=p[:, :], op0=ALU.mult,
                                                                                op1=ALU.add), [X, p], [Z])
                    layer_norm_rows(Z, 128, Z, ln2g_bc, ln2b_bc, lnw)
                    dst = y_s if c == 17 else y_p[(c - 1) * 128:c * 128, :]
                    S.dma("act", dst, Z[:, :], reads=[Z], sem_tile=Z, final=True)

            if nhalves == 2:
                for f0 in range(0, NFC, 4):
                    nf = min(4, NFC - f0)
                    for fc in range(f0, f0 + nf):
                        S.op("pe", lambda e, fc=fc, f0=f0: e.transpose(pm[0:34, (fc - f0) * 128:(fc - f0 + 1) * 128],
                                                                      ulast[:, fc, :], ident[:, :]), [ulast, ident], [pm])
                    S.op("dve", lambda e, f0=f0, nf=nf: e.tensor_copy(out=cvbuf[0:34, f0 * 128:(f0 + nf) * 128],
                                                                     in_=pm[0:34, 0:nf * 128]), [pm], [cvbuf])
                S.dma("act", cv_so, cvbuf[0:32, :], reads=[cvbuf], sem_tile=cvbuf, final=True)
                S.dma("act", cv_po, cvbuf[32:34, :], reads=[cvbuf], sem_tile=cvbuf, final=True)
            S.finish()

        S.finish()
    return nc


_CACHE = {}


def _consts():
    t = np.arange(128)
    same = (t[:, None] // 8) == (t[None, :] // 8)
    triP = (t[:, None] <= t[None, :]).astype(np.float32)
    triS = (triP.astype(bool) & same).astype(np.float32)
    negBlk = np.where(same, 0.0, NEG).astype(np.float32)
    blkones = same.astype(np.float32)
    blkind = (t[:, None] // 8 == np.arange(16)[None, :]).astype(np.float32)
    sel0 = (t[:, None] == 8 * np.arange(16)[None, :]).astype(np.float32)
    return {"c_ident": np.eye(128, dtype=np.float32), "c_triP": triP, "c_triS": triS, "c_negBlk": negBlk,
            "c_blkones": blkones, "c_blkind": blkind, "c_blkindT": np.ascontiguousarray(blkind.T), "c_sel0": sel0}


def kernel(x_prompt, x_sample, state_mlstm_C, state_mlstm_n, state_mlstm_m, state_hgrn_S, state_ffn_conv,
           meta_tokens, ln_emb_g, ln_emb_b, w_in, b_in, b_fgate_a, g_norm_a, g_norm_b, hgrn_lb_logits,
           w_out, b_out, ln1_g, ln1_b, w_up, b_up, w_conv, b_conv, w_down, b_down, ln2_g, ln2_b):
    f = lambda a: np.ascontiguousarray(np.asarray(a, dtype=np.float32))
    if "nc" not in _CACHE:
        _CACHE["nc"] = build_program()
    nc = _CACHE["nc"]
    shared = {
        "meta": f(meta_tokens), "ln_emb_g": f(ln_emb_g), "ln_emb_b": f(ln_emb_b),
        "w_in": f(w_in[0]), "b_in": f(b_in[0]).reshape(1, DIN), "b_fg": f(b_fgate_a[0]),
        "g_a": f(g_norm_a[0]).reshape(512), "g_b": f(g_norm_b[0]).reshape(512), "lb_log": f(hgrn_lb_logits),
        "w_out": f(w_out[0]), "b_out": f(b_out[0]).reshape(1, D), "ln1_g": f(ln1_g[0]), "ln1_b": f(ln1_b[0]),
        "w_up": f(w_up[0]), "b_up": f(b_up[0]), "w_conv": f(w_conv[0]), "b_conv": f(b_conv[0]),
        "w_down": f(w_down[0]), "b_down": f(b_down[0]).reshape(1, D), "ln2_g": f(ln2_g[0]), "ln2_b": f(ln2_b[0]),
    }
    shared.update(_consts())
    in_maps = []
    for i in range(8):
        sl = slice(16 * i, 16 * i + 16)
        m = dict(shared)
        m["x_p"] = f(x_prompt[i])
        m["x_s"] = f(x_sample[sl]).reshape(128, D)
        m["C_s"] = f(np.asarray(state_mlstm_C[0, sl]).transpose(0, 2, 1, 3))
        m["n_s"] = f(np.asarray(state_mlstm_n[0, sl]).transpose(0, 2, 1))
        m["m_s"] = f(state_mlstm_m[0, sl]); m["S_s"] = f(np.asarray(state_hgrn_S[0, sl]).transpose(0, 2, 1, 3))
        m["cv_s"] = f(state_ffn_conv[0, sl]).reshape(32, DFF)
        in_maps.append(m)
    res = run_bass_kernel_spmd(nc, in_maps, core_ids=list(range(8)))
    R = res.results
    cat = lambda k: np.stack([np.asarray(r[k], dtype=np.float32) for r in R])
    y_prompt = cat("y_p")
    y_sample = cat("y_s").reshape(128, 8, D)
    C_p = cat("C_po")[None]; n_p = cat("n_po")[None]; m_p = cat("m_po").reshape(1, 8, 4)
    S_p = cat("S_po")[None]; cv_p = cat("cv_po")[None]
    C_so = np.ascontiguousarray(cat("C_so").transpose(0, 1, 3, 2, 4)).reshape(1, 128, 4, 128, 128)
    n_so = np.ascontiguousarray(cat("n_so").transpose(0, 1, 3, 2)).reshape(1, 128, 4, 128)
    m_so = cat("m_so").reshape(1, 128, 4)
    S_so = np.ascontiguousarray(cat("S_so").transpose(0, 1, 3, 2, 4)).reshape(1, 128, 4, 128, 128)
    cv_so = cat("cv_so").reshape(1, 128, 2, DFF)
    return (y_prompt, y_sample, C_p, n_p, m_p, S_p, cv_p, C_so, n_so, m_so, S_so, cv_so)
```
